# Optimizing a Trainium2 kernel written in Bass

```python
import math
import jax
import jax.numpy as jnp
from jax import lax
import numpy as np

D_MODEL = 2048
BATCH = 4
SEQ = 8192
DEPTH = 4

GRID_W = 64
CTX_LEN = 256
EPS = 1e-6

NA_HEAD_DIM = 128
NA_HEADS = D_MODEL // NA_HEAD_DIM
NA_WIDTH = NA_HEADS * NA_HEAD_DIM
WIN_H = 8
WIN_W = 16
ROPE_THETA = 10000.0

SSM_WIDTH = D_MODEL
SSM_HEAD_DIM = 64
SSM_HEADS = SSM_WIDTH // SSM_HEAD_DIM
SSM_STATE = 128
SSM_GROUPS = 8
D_CONV = 5
CHUNK = 128
CONV_CH = SSM_WIDTH + 2 * SSM_GROUPS * SSM_STATE

MIX_WIDTH = NA_WIDTH + SSM_WIDTH
IN_COLS = 4 * NA_WIDTH + SSM_WIDTH + CONV_CH + 2 * SSM_HEADS

kernel_name = "hybrid_na_ssd_prefix_dit"


def rms_norm(u, w):
    uf = u.astype(jnp.float32)
    y = uf * lax.rsqrt(jnp.mean(uf * uf, axis=-1, keepdims=True) + EPS)
    return (y * w.astype(jnp.float32)).astype(u.dtype)


def modulation(cond, ada_w, ada_b):
    m = jax.nn.silu(cond) @ ada_w + ada_b
    return jnp.split(m, 3, axis=-1)


def axial_rope_tables(n_tokens, head_dim):
    t = jnp.arange(n_tokens)
    rows = (t // GRID_W).astype(jnp.float32)
    cols = (t % GRID_W).astype(jnp.float32)
    n_pairs = head_dim // 4
    freqs = ROPE_THETA ** (-jnp.arange(n_pairs, dtype=jnp.float32) / n_pairs)
    ang = jnp.concatenate([rows[:, None] * freqs, cols[:, None] * freqs], axis=-1)
    return jnp.cos(ang), jnp.sin(ang)


def apply_rope(u, cos, sin):
    uf = u.astype(jnp.float32).reshape(*u.shape[:-1], -1, 2)
    u1, u2 = uf[..., 0], uf[..., 1]
    cs, sn = cos[None, :, None, :], sin[None, :, None, :]
    out = jnp.stack([u1 * cs - u2 * sn, u1 * sn + u2 * cs], axis=-1)
    return out.reshape(u.shape).astype(u.dtype)


def context_attention(q, k, v):
    scale = q.shape[-1] ** -0.5
    s = jnp.einsum("blhd,bmhd->bhlm", q, k).astype(jnp.float32) * scale
    p = jax.nn.softmax(s, axis=-1).astype(v.dtype)
    o = jnp.einsum("bhlm,bmhd->blhd", p, v)
    return o.reshape(*o.shape[:2], -1)


def neighbourhood_attention(q, k, v, k_ctx, v_ctx, rpb):
    bsz, n_tok, n_heads, hd = q.shape
    rows = n_tok // GRID_W
    kh, kw = min(WIN_H, rows), WIN_W
    scale = hd ** -0.5
    qg = q.reshape(bsz, rows, GRID_W, n_heads, hd).transpose(1, 0, 3, 2, 4)
    kg = k.reshape(bsz, rows, GRID_W, n_heads, hd).transpose(0, 3, 1, 2, 4)
    vg = v.reshape(bsz, rows, GRID_W, n_heads, hd).transpose(0, 3, 1, 2, 4)
    col = jnp.arange(GRID_W)
    c0 = jnp.clip(col - kw // 2, 0, GRID_W - kw)
    col_mask = (col[None, :] >= c0[:, None]) & (col[None, :] < c0[:, None] + kw)
    col_idx = jnp.clip(col[None, :] - col[:, None] + WIN_W - 1, 0, 2 * WIN_W - 2)
    rpb_cols = rpb[:, :, col_idx]
    n_loc = kh * GRID_W

    def row_block(args):
        r, q_r = args
        r0 = jnp.clip(r - kh // 2, 0, rows - kh)
        k_r = lax.dynamic_slice_in_dim(kg, r0, kh, axis=2).reshape(bsz, n_heads, n_loc, hd)
        v_r = lax.dynamic_slice_in_dim(vg, r0, kh, axis=2).reshape(bsz, n_heads, n_loc, hd)
        row_idx = r0 + jnp.arange(kh) - r + WIN_H - 1
        bias = rpb_cols[:, row_idx].transpose(0, 2, 1, 3)
        s_loc = jnp.einsum("bhqd,bhkd->bhqk", q_r, k_r).astype(jnp.float32) * scale
        s_loc = s_loc.reshape(bsz, n_heads, GRID_W, kh, GRID_W) + bias.astype(jnp.float32)
        s_loc = jnp.where(col_mask[:, None, :], s_loc, -jnp.inf).reshape(bsz, n_heads, GRID_W, n_loc)
        s_ctx = jnp.einsum("bhqd,bhkd->bhqk", q_r, k_ctx).astype(jnp.float32) * scale
        p = jax.nn.softmax(jnp.concatenate([s_loc, s_ctx], axis=-1), axis=-1).astype(v.dtype)
        return (jnp.einsum("bhqk,bhkd->bhqd", p[..., :n_loc], v_r)
                + jnp.einsum("bhqk,bhkd->bhqd", p[..., n_loc:], v_ctx))

    out = lax.map(row_block, (jnp.arange(rows), qg))
    return out.transpose(1, 0, 3, 2, 4).reshape(bsz, n_tok, n_heads * hd)


def depthwise_conv_centred(u, w, b):
    out = lax.conv_general_dilated(
        u, w[:, None, :].astype(u.dtype), window_strides=(1,),
        padding=[(D_CONV // 2, D_CONV // 2)],
        dimension_numbers=("NWC", "WIO", "NWC"), feature_group_count=u.shape[-1])
    return out + b


def ssd_chunked(xs, dt, a, b_in, c_in, init_state, with_output):
    bsz, n_tok, n_heads, hp = xs.shape
    n_groups, n_state = b_in.shape[-2:]
    hpg = n_heads // n_groups
    nc = n_tok // CHUNK
    f32 = jnp.float32
    xd = (xs.astype(f32) * dt[..., None]).reshape(bsz, nc, CHUNK, n_groups, hpg, hp)
    da = (dt * a).reshape(bsz, nc, CHUNK, n_groups, hpg)
    bc = b_in.astype(f32).reshape(bsz, nc, CHUNK, n_groups, n_state)
    cc = c_in.astype(f32).reshape(bsz, nc, CHUNK, n_groups, n_state)
    da_cs = jnp.cumsum(da, axis=2)
    da_tot = da_cs[:, :, -1]
    decay_to_end = jnp.exp(da_tot[:, :, None] - da_cs)
    chunk_states = jnp.einsum("bcqgn,bcqgh,bcqghp->bcghpn", bc, decay_to_end, xd)

    def carry_state(state, inp):
        st, tot = inp
        return state * jnp.exp(tot)[..., None, None] + st, state

    final, prev_states = lax.scan(
        carry_state, init_state.astype(f32).reshape(bsz, n_groups, hpg, hp, n_state),
        (jnp.moveaxis(chunk_states, 1, 0), jnp.moveaxis(da_tot, 1, 0)))
    final = final.reshape(bsz, n_heads, hp, n_state)
    if not with_output:
        return None, final
    prev_states = jnp.moveaxis(prev_states, 0, 1)
    tril = jnp.tril(jnp.ones((CHUNK, CHUNK), dtype=bool))
    seg = da_cs[:, :, :, None] - da_cs[:, :, None, :]
    decay = jnp.exp(jnp.where(tril[:, :, None, None], seg, -jnp.inf))
    cb = jnp.einsum("bcqgn,bckgn->bcqkg", cc, bc)
    y_diag = jnp.einsum("bcqkgh,bckghp->bcqghp", cb[..., None] * decay, xd)
    y_off = jnp.einsum("bcqgn,bcghpn,bcqgh->bcqghp", cc, prev_states, jnp.exp(da_cs))
    return (y_diag + y_off).reshape(bsz, n_tok, n_heads, hp).astype(xs.dtype), final


def ssd_bidirectional(xs, dt2, a2, b_in, c_in, state_fwd, state_bwd, with_output):
    flip = lambda u: jnp.flip(u, axis=1)
    y_f, s_f = ssd_chunked(xs, dt2[:, :, 0], a2[0], b_in, c_in, state_fwd, with_output)
    y_b, s_b = ssd_chunked(flip(xs), flip(dt2[:, :, 1]), a2[1], flip(b_in), flip(c_in),
                           state_bwd, with_output)
    y = y_f + flip(y_b) if with_output else None
    return y, s_f, s_b


def hybrid_layer(x, xc, c, c_ctx, cos, sin, ada_w, ada_b, norm_w, w_in, q_norm, k_norm, rpb,
                 conv_w, conv_b, dt_bias, a_log, d_skip, ssm_norm, w_out, update_ctx):
    bsz, n_tok, _ = x.shape
    n_ctx = xc.shape[1]
    f32 = jnp.float32
    shift, scale, gate = modulation(c, ada_w, ada_b)
    shift_c, scale_c, gate_c = modulation(c_ctx, ada_w, ada_b)
    h = rms_norm(x, norm_w) * (1 + scale[:, None]) + shift[:, None]
    hc = rms_norm(xc, norm_w) * (1 + scale_c) + shift_c

    widths = [NA_WIDTH, NA_WIDTH, NA_WIDTH, NA_WIDTH, SSM_WIDTH, CONV_CH]
    cuts = [int(s) for s in np.cumsum(widths)]
    q, k, v, g, z, xbc, dt_raw = jnp.split(h @ w_in, cuts, axis=-1)
    qc, kc, vc, gc, zc, xbcc, dt_raw_c = jnp.split(hc @ w_in, cuts, axis=-1)

    heads = lambda u: u.reshape(*u.shape[:-1], NA_HEADS, NA_HEAD_DIM)
    q = apply_rope(rms_norm(heads(q), q_norm), cos, sin)
    k = apply_rope(rms_norm(heads(k), k_norm), cos, sin)
    qc = rms_norm(heads(qc), q_norm)
    kc = rms_norm(heads(kc), k_norm)
    vc = heads(vc)
    attn = neighbourhood_attention(q, k, heads(v), kc.transpose(0, 2, 1, 3),
                                   vc.transpose(0, 2, 1, 3), rpb) * jax.nn.silu(g)

    a2 = -jnp.exp(a_log.astype(f32))
    dt2 = jax.nn.softplus(dt_raw.astype(f32).reshape(bsz, n_tok, 2, SSM_HEADS) + dt_bias.astype(f32))
    dt2c = jax.nn.softplus(dt_raw_c.astype(f32).reshape(bsz, n_ctx, 2, SSM_HEADS) + dt_bias.astype(f32))

    def ssm_inputs(u):
        u = jax.nn.silu(depthwise_conv_centred(u, conv_w, conv_b))
        xs, bs, cs = jnp.split(u, [SSM_WIDTH, SSM_WIDTH + SSM_GROUPS * SSM_STATE], axis=-1)
        lead = u.shape[:-1]
        return (xs.reshape(*lead, SSM_HEADS, SSM_HEAD_DIM),
                bs.reshape(*lead, SSM_GROUPS, SSM_STATE),
                cs.reshape(*lead, SSM_GROUPS, SSM_STATE))

    def ssm_out(y, xs, zg):
        y = y + xs * d_skip[:, None]
        return rms_norm(y.reshape(zg.shape) * jax.nn.silu(zg), ssm_norm)

    xs, bs, cs = ssm_inputs(xbc)
    xsc, bsc, csc = ssm_inputs(xbcc)
    zero = jnp.zeros((bsz, SSM_HEADS, SSM_HEAD_DIM, SSM_STATE), f32)
    yc, s_f, s_b = ssd_bidirectional(xsc, dt2c, a2, bsc, csc, zero, zero, update_ctx)
    y, _, _ = ssd_bidirectional(xs, dt2, a2, bs, cs, s_f, s_b, True)

    out = jnp.concatenate([attn, ssm_out(y, xs, z)], axis=-1) @ w_out
    x = x + gate[:, None] * out
    if update_ctx:
        attn_c = context_attention(qc, kc, vc) * jax.nn.silu(gc)
        out_c = jnp.concatenate([attn_c, ssm_out(yc, xsc, zc)], axis=-1) @ w_out
        xc = xc + gate_c * out_c
    return x, xc


def setup_inputs(seed: int = 0) -> dict:
    key = jax.random.key(seed)
    ks = jax.random.split(key, 20)
    f32 = jnp.float32
    nrm = lambda k, shape, s: jax.random.normal(k, shape, f32) * s
    dt0 = jnp.exp(jax.random.uniform(ks[13], (DEPTH, 2, SSM_HEADS), f32,
                                     math.log(1e-3), math.log(1e-1)))
    return {
        "x": nrm(ks[0], (BATCH, SEQ, D_MODEL), 1.0),
        "c": nrm(ks[1], (BATCH, D_MODEL), 1.0),
        "ctx": nrm(ks[2], (BATCH, CTX_LEN, D_MODEL), 1.0),
        "c_ctx": nrm(ks[3], (D_MODEL,), 1.0),
        "ada_w": nrm(ks[4], (DEPTH, D_MODEL, 3 * D_MODEL), 0.5 * D_MODEL ** -0.5),
        "ada_b": nrm(ks[5], (DEPTH, 3 * D_MODEL), 0.01),
        "norm_w": 1.0 + nrm(ks[6], (DEPTH, D_MODEL), 0.02),
        "w_in": nrm(ks[7], (DEPTH, D_MODEL, IN_COLS), D_MODEL ** -0.5),
        "q_norm": 1.0 + nrm(ks[8], (DEPTH, NA_HEAD_DIM), 0.02),
        "k_norm": 1.0 + nrm(ks[9], (DEPTH, NA_HEAD_DIM), 0.02),
        "rpb": nrm(ks[10], (DEPTH, NA_HEADS, 2 * WIN_H - 1, 2 * WIN_W - 1), 0.02),
        "conv_w": nrm(ks[11], (DEPTH, D_CONV, CONV_CH), D_CONV ** -0.5),
        "conv_b": nrm(ks[12], (DEPTH, CONV_CH), 0.01),
        "dt_bias": dt0 + jnp.log(-jnp.expm1(-dt0)),
        "a_log": jnp.log(jax.random.uniform(ks[14], (DEPTH, 2, SSM_HEADS), f32, 1.0, 16.0)),
        "d_skip": 1.0 + nrm(ks[15], (DEPTH, SSM_HEADS), 0.1),
        "ssm_norm": 1.0 + nrm(ks[16], (DEPTH, SSM_WIDTH), 0.02),
        "w_out": nrm(ks[17], (DEPTH, MIX_WIDTH, D_MODEL), MIX_WIDTH ** -0.5),
    }


def reference(x, c, ctx, c_ctx, ada_w, ada_b, norm_w, w_in, q_norm, k_norm, rpb, conv_w, conv_b,
              dt_bias, a_log, d_skip, ssm_norm, w_out):
    cos, sin = axial_rope_tables(x.shape[1], NA_HEAD_DIM)
    xc = ctx
    for layer in range(DEPTH):
        x, xc = hybrid_layer(
            x, xc, c, c_ctx, cos, sin, ada_w[layer], ada_b[layer], norm_w[layer], w_in[layer],
            q_norm[layer], k_norm[layer], rpb[layer], conv_w[layer], conv_b[layer],
            dt_bias[layer], a_log[layer], d_skip[layer], ssm_norm[layer], w_out[layer],
            update_ctx=layer < DEPTH - 1)
    return x
```

```python
import math
from contextlib import ExitStack
import numpy as np
import ml_dtypes
import concourse.bass as bass
import concourse.mybir as mybir
from concourse.bass_utils import run_bass_kernel_spmd

F32 = mybir.dt.float32
BF16 = mybir.dt.bfloat16
AF = mybir.ActivationFunctionType
ALU = mybir.AluOpType
AX = mybir.AxisListType

D = 2048
GW = 64
NH = 16
HD = 128
SH = 32
SP = 64
SN = 128
SG = 8
CONV_CH = 4096
IN_COLS = 14400
EPS = 1e-6
NEG = -30000.0


class Cfg:
    def __init__(self, rows=128, L=256, depth=4, debug=False, phases=None):
        self.rows = rows
        self.S = rows * GW
        self.L = L
        self.T = self.S + L
        self.NT = self.T // 128
        self.NC = L // 128
        self.NL = self.S // 128
        self.depth = depth
        self.debug = debug
        self.phases = phases


class Buf:
    __slots__ = ("name", "w", "r", "dw", "dr", "sem")

    def __init__(self, name):
        self.name = name
        self.w = {}
        self.r = {}
        self.dw = 0
        self.dr = 0
        self.sem = None


class Op:
    __slots__ = ("eng", "fn", "deps", "marked", "count", "dma", "barrier", "totals")

    def __init__(self, eng, fn):
        self.eng = eng
        self.fn = fn
        self.deps = []
        self.marked = False
        self.count = 0
        self.dma = None
        self.barrier = False
        self.totals = None


ENGS = ("pe", "act", "dve", "pool", "sp")
NDMASEM = 40


class Prog:
    def __init__(self):
        self.ops = []
        self.dma_tot = [0] * NDMASEM
        self.phase_bufs = []
        self.next_dma_sem = 0
        self.last_op = {e: None for e in ENGS}

    def buf(self, name):
        b = Buf(name)
        self.phase_bufs.append(b)
        return b

    def bufs(self, name, n):
        return [self.buf(f"{name}{i}") for i in range(n)]

    def _sem_for(self, b):
        if b.sem is None:
            assert self.next_dma_sem < NDMASEM, "out of DMA semaphores in this phase"
            b.sem = self.next_dma_sem
            self.next_dma_sem += 1
        return b.sem

    def add(self, eng, fn, reads=(), writes=(), dma_buf=None):
        op = Op(eng, fn)
        is_dma = dma_buf is not None
        deps = op.deps
        for b in reads:
            for e, o in b.w.items():
                if e == eng and eng == "pe" and not is_dma:
                    continue
                deps.append(o)
            if b.dw:
                deps.append((b.sem, b.dw))
        for b in writes:
            for e, o in b.w.items():
                if e != eng or is_dma:
                    deps.append(o)
            for e, o in b.r.items():
                if e != eng or is_dma:
                    deps.append(o)
            if b.dw:
                deps.append((b.sem, b.dw))
            if b.dr:
                deps.append((b.sem, b.dr))
        for d in deps:
            if isinstance(d, Op):
                d.marked = True
        if is_dma:
            s = self._sem_for(dma_buf)
            self.dma_tot[s] += 16
            op.dma = (s, self.dma_tot[s])
            for b in reads:
                b.dr = self.dma_tot[s] if b is dma_buf else b.dr
                if b is not dma_buf:
                    raise AssertionError("dma touching tracked buf other than dma_buf")
            for b in writes:
                if b is not dma_buf:
                    raise AssertionError("dma touching tracked buf other than dma_buf")
                b.dw = self.dma_tot[s]
                b.w = {}
                b.r = {}
        else:
            for b in reads:
                b.r[eng] = op
            for b in writes:
                b.w = {eng: op}
                b.r = {}
                b.dw = 0
                b.dr = 0
        self.ops.append(op)
        if not is_dma:
            self.last_op[eng] = op
        return op

    def barrier(self):
        for e in ENGS:
            lo = self.last_op[e]
            if lo is not None:
                lo.marked = True
        op = Op(None, None)
        op.barrier = True
        self.ops.append(op)
        for b in self.phase_bufs:
            b.w = {}
            b.r = {}
            b.dw = 0
            b.dr = 0
            b.sem = None
        self.phase_bufs = []
        self.next_dma_sem = 0
        for e in ENGS:
            self.last_op[e] = None

    def emit(self, nc, csem, dsem):
        cnt = {e: 0 for e in ENGS}
        dtot = [0] * NDMASEM
        for op in self.ops:
            if op.barrier:
                op.totals = (dict(cnt), list(dtot))
                continue
            if op.dma is not None:
                dtot[op.dma[0]] = op.dma[1]
            elif op.marked:
                cnt[op.eng] += 1
                op.count = cnt[op.eng]
        ops = self.ops
        engobj = {}

        def run(eng, e):
            waited = {}

            def wait(key, sem, val):
                if waited.get(key, 0) < val:
                    e.wait_ge(sem, val)
                    waited[key] = val

            for op in ops:
                if op.barrier:
                    c, dt = op.totals
                    for en in ENGS:
                        if c[en]:
                            wait(en, csem[en], c[en])
                    for i, v in enumerate(dt):
                        if v:
                            wait(i, dsem[i], v)
                    continue
                if op.eng != eng:
                    continue
                for d in op.deps:
                    if isinstance(d, Op):
                        wait(d.eng, csem[d.eng], d.count)
                    else:
                        wait(d[0], dsem[d[0]], d[1])
                ins = op.fn(e)
                if op.dma is not None:
                    ins.then_inc(dsem[op.dma[0]], 16)
                elif op.marked:
                    ins.then_inc(csem[eng], 1)

        with nc.Block() as block:
            @block.tensor
            def _(e):
                run("pe", e)

            @block.scalar
            def _(e):
                run("act", e)

            @block.vector
            def _(e):
                run("dve", e)

            @block.gpsimd
            def _(e):
                run("pool", e)

            @block.sync
            def _(e):
                run("sp", e)


class Builder:
    def __init__(self, cfg):
        self.cfg = cfg
        self.nc = bass.Bass("TRN2", target_bir_lowering=False)
        self.p = Prog()
        self.dbg_names = []
        self._uid = 0

    def sbt(self, name, shape, dt):
        self._uid += 1
        return self.nc.sbuf_tensor(f"{name}_{self._uid}", shape, dt)

    def pst(self, name, shape, dt):
        self._uid += 1
        return self.nc.psum_tensor(f"{name}_{self._uid}", shape, dt)

    def dram_in(self, name, shape, dt=F32):
        return self.nc.dram_tensor(name, list(shape), dt, kind="ExternalInput").ap()

    def dram_scratch(self, name, shape, dt, dbg=True):
        if self.cfg.debug and dbg:
            self.dbg_names.append(name)
            return self.nc.dram_tensor(name, list(shape), dt, kind="ExternalOutput").ap()
        return self.nc.dram_tensor(name, list(shape), dt).ap()

    def load(self, out_ap, in_ap, buf, eng="sp", **kw):
        self.p.add(eng, lambda e: e.dma_start(out=out_ap, in_=in_ap, **kw), writes=[buf], dma_buf=buf)

    def store(self, out_ap, in_ap, buf, eng="pool", **kw):
        self.p.add(eng, lambda e: e.dma_start(out=out_ap, in_=in_ap, **kw), reads=[buf], dma_buf=buf)

    def d2d(self, out_ap, in_ap, b, eng="pool"):
        self.p.add(eng, lambda e: e.dma_start(out=out_ap, in_=in_ap), writes=[b], dma_buf=b)

    def mm(self, out_ap, lhsT, rhs, start, stop, reads, wbuf):
        self.p.add("pe", lambda e: e.matmul(out_ap, lhsT=lhsT, rhs=rhs, start=start, stop=stop),
                   reads=reads, writes=[wbuf])

    def tr(self, out_ap, in_ap, ident, reads, wbuf):
        self.p.add("pe", lambda e: e.transpose(out_ap, in_ap, ident), reads=reads, writes=[wbuf])

    def act(self, out, in_, func, reads, writes, eng="act", **kw):
        self.p.add(eng, lambda e: e.activation(out=out, in_=in_, func=func, **kw), reads=reads, writes=writes)

    def tt(self, eng, out, in0, in1, op, reads, writes):
        self.p.add(eng, lambda e: e.tensor_tensor(out=out, in0=in0, in1=in1, op=op), reads=reads, writes=writes)

    def ts(self, eng, out, in0, s1, s2, op0, op1, reads, writes):
        if s2 is None:
            self.p.add(eng, lambda e: e.tensor_scalar(out=out, in0=in0, scalar1=s1, scalar2=None, op0=op0),
                       reads=reads, writes=writes)
        else:
            self.p.add(eng, lambda e: e.tensor_scalar(out=out, in0=in0, scalar1=s1, scalar2=s2, op0=op0, op1=op1),
                       reads=reads, writes=writes)

    def stt(self, eng, out, in0, scalar, in1, op0, op1, reads, writes):
        self.p.add(eng, lambda e: e.scalar_tensor_tensor(out=out, in0=in0, scalar=scalar, in1=in1, op0=op0, op1=op1),
                   reads=reads, writes=writes)

    def copy(self, eng, out, in_, reads, writes):
        if eng == "act":
            self.p.add(eng, lambda e: e.copy(out=out, in_=in_), reads=reads, writes=writes)
        else:
            self.p.add(eng, lambda e: e.tensor_copy(out=out, in_=in_), reads=reads, writes=writes)

    def memset(self, eng, ap, val, writes):
        self.p.add(eng, lambda e: e.memset(ap, val), writes=writes)

    def build(self):
        cfg = self.cfg
        nc = self.nc
        T, NT, depth = cfg.T, cfg.NT, cfg.depth
        I = {}
        I["xin"] = self.dram_in("xin", [T, D])
        I["cvec"] = self.dram_in("cvec", [128, 16, 2])
        I["ada_w"] = self.dram_in("ada_w", [depth, D, 3 * D])
        I["ada_bf"] = self.dram_in("ada_bf", [depth, 128, 48])
        I["ada_bg"] = self.dram_in("ada_bg", [depth, 1, D])
        I["norm_wf"] = self.dram_in("norm_wf", [depth, 128, 16])
        I["w_in"] = self.dram_in("w_in", [depth, D, IN_COLS])
        I["q_norm"] = self.dram_in("q_norm", [depth, 1, HD])
        I["k_norm"] = self.dram_in("k_norm", [depth, 1, HD])
        I["rope"] = self.dram_in("rope", [128, cfg.NL, 2, 64])
        I["bias"] = self.dram_in("bias", [depth, NH, 128, 21, 128])
        I["conv_w"] = self.dram_in("conv_w", [depth, 128, 32, 5])
        I["conv_b"] = self.dram_in("conv_b", [depth, 128, 32])
        I["dt_bias"] = self.dram_in("dt_bias", [depth, 1, 64])
        I["a_log"] = self.dram_in("a_log", [depth, 1, 64])
        I["d_skip"] = self.dram_in("d_skip", [depth, 1, 32])
        I["ssm_norm"] = self.dram_in("ssm_norm", [depth, 1, D])
        I["w_out"] = self.dram_in("w_out", [depth, 2 * D, D])
        I["consts"] = self.dram_in("consts", [128, 6, 128])
        I["negm"] = self.dram_in("negm", [128, 2, 512])
        self.I = I
        xout = nc.dram_tensor("xout", [T, D], F32, kind="ExternalOutput").ap()
        S = {}
        S["X"] = [I["xin"]] + [self.dram_scratch(f"X{l}", [T, D], F32, dbg=False) for l in range(1, depth)] + [xout]
        S["wbf_in"] = self.dram_scratch("wbf_in", [D, IN_COLS], BF16, dbg=False)
        S["wbf_out"] = self.dram_scratch("wbf_out", [2 * D, D], BF16, dbg=False)
        S["gate"] = self.dram_scratch("gate_s", [2, D], F32)
        for n in ("q_s", "k_s", "v_s", "sg_s", "sz_s"):
            S[n] = self.dram_scratch(n, [T, D], BF16)
        S["xbc_pre"] = self.dram_scratch("xbc_pre", [CONV_CH, T], F32)
        S["xbc_post"] = self.dram_scratch("xbc_post", [CONV_CH, T], BF16)
        S["dt_s"] = self.dram_scratch("dt_s", [T, 64], F32)
        S["yf_s"] = self.dram_scratch("yf_s", [T, D], F32)
        S["mix_s"] = self.dram_scratch("mix_s", [T, 2 * D], BF16)
        self.S = S

        with ExitStack() as es:
            csem = {e: es.enter_context(nc.semaphore(f"c_{e}")) for e in ENGS}
            dsem = [es.enter_context(nc.semaphore(f"d_{i}")) for i in range(NDMASEM)]
            P = {}
            P["consts"] = es.enter_context(self.sbt("p_consts", [128, 6, 128], F32))
            P["ident"] = es.enter_context(self.sbt("p_ident", [128, 128], BF16))
            P["modA"] = es.enter_context(self.sbt("p_modA", [128, 2, 16], F32))
            P["modB"] = es.enter_context(self.sbt("p_modB", [128, 2, 16], F32))
            P["gate"] = es.enter_context(self.sbt("p_gate", [128, 2, D], F32))
            self.P = P
            self.es_top = es
            self.phase_init()
            for l in range(depth):
                self.layer(l)
            self.p.barrier()
            self.p.emit(nc, csem, dsem)
        return nc

    def want(self, name):
        return self.cfg.phases is None or name in self.cfg.phases

    def phase_init(self):
        p, P, I = self.p, self.P, self.I
        b = p.buf("consts")
        self.load(P["consts"][:], I["consts"][:, :, :], b)
        self.copy("dve", P["ident"][:], P["consts"][:, 0, :], [b], [b])
        p.barrier()

    def layer(self, l):
        if self.want("W"):
            self.phase_W(l)
        if self.want("M"):
            self.phase_M(l)
        if self.want("A"):
            self.phase_A(l)
        if self.want("B"):
            self.phase_B(l)
        if self.want("C0"):
            self.phase_C0(l)
        if self.want("C1"):
            self.phase_C(l, 0)
        if self.want("C2"):
            self.phase_C(l, 1)
        if self.want("D"):
            self.phase_D(l)

    def phase_W(self, l):
        I, S = self.I, self.S
        nchunk = 16
        rows = D // nchunk
        bd = [self.p.buf("d2da"), self.p.buf("d2db")]
        for i in range(nchunk):
            self.d2d(S["wbf_in"][i * rows:(i + 1) * rows, :], I["w_in"][l, i * rows:(i + 1) * rows, :], bd[i % 2])
        rows = 2 * D // nchunk
        for i in range(nchunk):
            self.d2d(S["wbf_out"][i * rows:(i + 1) * rows, :], I["w_out"][l, i * rows:(i + 1) * rows, :], bd[i % 2])
        self.p.barrier()

    def phase_M(self, l):
        nc, p, P, I, S = self.nc, self.p, self.P, self.I, self.S
        with ExitStack() as es:
            sb = lambda name, shape, dt=F32: es.enter_context(self.sbt(name, shape, dt))
            cv = sb("m_cv", [128, 16, 2]); sc = sb("m_sc", [128, 16, 2])
            wt = [sb(f"m_w{i}", [128, 16, 512]) for i in range(2)]
            bfm = sb("m_bf", [128, 48]); nw = sb("m_nw", [128, 16]); bg = sb("m_bg", [2, D])
            mf = sb("m_mf", [128, 32, 2]); grow = sb("m_grow", [2, D])
            psf = es.enter_context(self.pst("m_psf", [128, 32, 2], F32))
            psg = [es.enter_context(self.pst(f"m_psg{i}", [2, 512], F32)) for i in range(2)]
            b_cv, b_sc, b_bf, b_nw, b_bg, b_mf, b_grow, b_psf = (p.buf(n) for n in
                                                                ("cv", "sc", "bf", "nw", "bg", "mf", "grow", "psf"))
            b_wt = p.bufs("wt", 2); b_psg = p.bufs("psg", 2)
            b_modA, b_modB, b_gate = p.buf("modA"), p.buf("modB"), p.buf("gate")
            self.load(cv[:], I["cvec"][:, :, :], b_cv)
            self.load(bfm[:], I["ada_bf"][l, :, :], b_bf)
            self.load(nw[:], I["norm_wf"][l, :, :], b_nw)
            self.load(bg[:], I["ada_bg"][l, :, :].to_broadcast([2, D]), b_bg)
            self.act(sc[:], cv[:], AF.Silu, [b_cv], [b_sc])
            wv = I["ada_w"][l].rearrange("(kc p) n -> p kc n", p=128)
            for blk in range(12):
                w = wt[blk % 2]; bw = b_wt[blk % 2]
                self.load(w[:], wv[:, :, blk * 512:(blk + 1) * 512], bw)
                if blk < 8:
                    for ft in range(4):
                        j = blk * 4 + ft
                        for kc in range(16):
                            self.mm(psf[:, j, :], w[:, kc, ft * 128:(ft + 1) * 128], sc[:, kc, :],
                                    kc == 0, kc == 15, [bw, b_sc], b_psf)
                else:
                    g = blk - 8
                    ps = psg[g % 2]; bps = b_psg[g % 2]
                    for kc in range(16):
                        self.mm(ps[:, :], sc[:, kc, :], w[:, kc, :], kc == 0, kc == 15, [bw, b_sc], bps)
                    self.tt("dve", grow[:, g * 512:(g + 1) * 512], ps[:, :], bg[:, g * 512:(g + 1) * 512],
                            ALU.add, [bps, b_bg], [b_grow])
            self.tt("dve", mf[:], psf[:], bfm[:, 0:32].unsqueeze(2).to_broadcast([128, 32, 2]), ALU.add,
                    [b_psf, b_bf], [b_mf])
            for s, ci in ((0, 1), (1, 0)):
                self.stt("dve", P["modA"][:, s, :], mf[:, 16:32, ci], 1.0, nw[:], ALU.add, ALU.mult,
                         [b_mf, b_nw], [b_modA])
                self.copy("dve", P["modB"][:, s, :], mf[:, 0:16, ci], [b_mf], [b_modB])
            self.store(S["gate"][:, :], grow[:], b_grow, eng="sp")
            p.barrier()
            self.load(P["gate"][:, 0, :], S["gate"][1:2, :].to_broadcast([128, D]), b_gate)
            self.load(P["gate"][:, 1, :], S["gate"][0:1, :].to_broadcast([128, D]), b_gate)
            p.barrier()

    def phase_A(self, l):
        nc, p, P, I, S, cfg = self.nc, self.p, self.P, self.I, self.S, self.cfg
        X = S["X"][l]
        TB = 1024
        sbs = [(0, cfg.L)]
        t0 = cfg.L
        while t0 < cfg.T:
            sbs.append((t0, min(TB, cfg.T - t0)))
            t0 += TB
        with ExitStack() as es:
            sb = lambda name, shape, dt=F32: es.enter_context(self.sbt(name, shape, dt))
            hT = sb("a_hT", [128, 16, TB], BF16)
            xt = [sb(f"a_xt{i}", [128, D]) for i in range(2)]
            xn = [sb(f"a_xn{i}", [128, D], BF16) for i in range(2)]
            junk = sb("a_junk", [128, D], BF16)
            ss = sb("a_ss", [128, 2]); rstd = sb("a_rstd", [128, 2])
            wt = [sb(f"a_wt{i}", [128, 16, 512], BF16) for i in range(3)]
            qkn = sb("a_qkn", [128, 2, 4, 128])
            rope = sb("a_rope", [128, cfg.NL, 2, 64])
            dtb = sb("a_dtb", [128, 64])
            sq = [sb(f"a_sq{i}", [128, 512]) for i in range(2)]
            hs = [sb(f"a_hs{i}", [128, 8]) for i in range(2)]
            tq = [sb(f"a_tq{i}", [128, 512]) for i in range(2)]
            r1 = [sb(f"a_r1{i}", [128, 4, 64]) for i in range(2)]
            r2 = [sb(f"a_r2{i}", [128, 4, 64]) for i in range(2)]
            ob = [sb(f"a_ob{i}", [128, 512], BF16) for i in range(3)]
            of = [sb(f"a_of{i}", [128, 512], F32) for i in range(2)]
            dtt = [sb(f"a_dtt{i}", [128, 4, 64]) for i in range(2)]
            tp = [es.enter_context(self.pst(f"a_tp{i}", [128, 1024], BF16)) for i in range(2)]
            mmps = [es.enter_context(self.pst(f"a_mm{i}", [128, 512], F32)) for i in range(4)]
            b_hT = p.bufs("hT", TB // 128)
            b_xt = p.bufs("xt", 2); b_xn = p.bufs("xn", 2); b_junk = p.buf("junk")
            b_ss = p.bufs("ss", 2); b_rstd = p.bufs("rstd", 2)
            b_wt = p.bufs("wt", 3); b_qkn = p.buf("qkn"); b_rope = p.buf("rope"); b_dtb = p.buf("dtb")
            b_sq = p.bufs("sq", 2); b_hs = p.bufs("hs", 2); b_tq = p.bufs("tq", 2)
            b_r1 = p.bufs("r1", 2); b_r2 = p.bufs("r2", 2); b_ob = p.bufs("ob", 3); b_of = p.bufs("of", 2)
            b_dtt = p.bufs("dtt", 2)
            b_tp = p.bufs("tp", 2); b_mm = p.bufs("mm", 4)
            ident = P["ident"]

            self.load(qkn[:, 0, :, :], I["q_norm"][l, :, :].unsqueeze(1).to_broadcast([128, 4, 128]), b_qkn)
            self.load(qkn[:, 1, :, :], I["k_norm"][l, :, :].unsqueeze(1).to_broadcast([128, 4, 128]), b_qkn)
            self.load(rope[:], I["rope"][:, :, :, :], b_rope)
            self.load(dtb[:], I["dt_bias"][l, :, :].to_broadcast([128, 64]), b_dtb)
            self.ts("dve", qkn[:, 0, :, :], qkn[:, 0, :, :], HD ** -0.5, None, ALU.mult, None, [b_qkn], [b_qkn])

            wsrc = S["wbf_in"].rearrange("(kc p) n -> p kc n", p=128)
            xbcT = S["xbc_pre"]
            wcnt = 0
            mmcnt = 0
            epi = 0
            for (tb0, tbn) in sbs:
                ntile = tbn // 128
                is_ctx = tb0 < cfg.L
                ms = 0 if is_ctx else 1
                for i in range(ntile):
                    s = i % 2
                    r0 = tb0 + i * 128
                    self.load(xt[s][:], X[r0:r0 + 128, :], b_xt[s])
                    self.act(junk[:], xt[s][:], AF.Square, [b_xt[s]], [b_junk, b_ss[s]], accum_out=ss[:, s:s + 1])
                    self.act(rstd[:, s:s + 1], ss[:, s:s + 1], AF.Sqrt, [b_ss[s]], [b_rstd[s]], scale=1.0 / D, bias=EPS)
                    self.p.add("dve", (lambda e, s=s: e.reciprocal(out=rstd[:, s:s + 1], in_=rstd[:, s:s + 1])),
                               reads=[b_rstd[s]], writes=[b_rstd[s]])
                    self.act(xn[s][:], xt[s][:], AF.Copy, [b_xt[s], b_rstd[s]], [b_xn[s]], scale=rstd[:, s:s + 1])
                    for half in range(2):
                        for k8 in range(8):
                            kc = half * 8 + k8
                            self.tr(tp[half][:, k8 * 128:(k8 + 1) * 128], xn[s][:, kc * 128:(kc + 1) * 128], ident[:],
                                    [b_xn[s]], b_tp[half])
                        for k8 in range(8):
                            kc = half * 8 + k8
                            src = tp[half][:, k8 * 128:(k8 + 1) * 128]
                            dst = hT[:, kc, i * 128:(i + 1) * 128]
                            if k8 % 2 == 0:
                                self.act(dst, src, AF.Identity, [b_tp[half]], [b_hT[i]],
                                         scale=P["modA"][:, ms, kc:kc + 1], bias=P["modB"][:, ms, kc:kc + 1])
                            else:
                                self.ts("dve", dst, src, P["modA"][:, ms, kc:kc + 1], P["modB"][:, ms, kc:kc + 1],
                                        ALU.mult, ALU.add, [b_tp[half]], [b_hT[i]])
                blocks = []
                for f, fam in enumerate(("q", "k", "v", "g", "z")):
                    for j in range(4):
                        blocks.append((fam, f * D + j * 512, 512, j))
                for j in range(8):
                    blocks.append(("xbc", 5 * D + j * 512, 512, j))
                blocks.append(("dt", 5 * D + CONV_CH, 64, 0))
                for (fam, c0, ncol, j) in blocks:
                    ws = wcnt % 3; wcnt += 1
                    w = wt[ws]; bw = b_wt[ws]
                    self.load(w[:, :, 0:ncol], wsrc[:, :, c0:c0 + ncol], bw)
                    if fam == "xbc":
                        for cs_ in range(4):
                            ch0 = j * 512 + cs_ * 128
                            for tg in range(0, tbn, 512):
                                tn = min(512, tbn - tg)
                                m = mmcnt % 4; mmcnt += 1
                                tiles = [b_hT[(tg + q) // 128] for q in range(0, tn, 128)]
                                for kc in range(16):
                                    self.mm(mmps[m][:, 0:tn], w[:, kc, cs_ * 128:(cs_ + 1) * 128], hT[:, kc, tg:tg + tn],
                                            kc == 0, kc == 15, [bw] + tiles, b_mm[m])
                                o = of[epi % 2]; bo = b_of[epi % 2]; epi += 1
                                self.copy("act", o[:, 0:tn], mmps[m][:, 0:tn], [b_mm[m]], [bo])
                                self.store(xbcT[ch0:ch0 + 128, tb0 + tg:tb0 + tg + tn], o[:, 0:tn], bo)
                        continue
                    for i in range(ntile):
                        r0 = tb0 + i * 128
                        m = mmcnt % 4; mmcnt += 1
                        ps = mmps[m]; bps = b_mm[m]
                        for kc in range(16):
                            self.mm(ps[:, 0:ncol], hT[:, kc, i * 128:(i + 1) * 128], w[:, kc, 0:ncol],
                                    kc == 0, kc == 15, [bw, b_hT[i]], bps)
                        e2 = epi % 2; epi += 1
                        if fam in ("q", "k"):
                            qi = 0 if fam == "q" else 1
                            self.act(sq[e2][:], ps[:, :], AF.Square, [bps], [b_sq[e2]])
                            self.p.add("dve", (lambda e, e2=e2: e.reduce_sum(out=hs[e2][:, 0:4], in_=sq[e2][:].rearrange("p (h d) -> p h d", h=4), axis=AX.X)),
                                       reads=[b_sq[e2]], writes=[b_hs[e2]])
                            self.act(hs[e2][:, 4:8], hs[e2][:, 0:4], AF.Sqrt, [b_hs[e2]], [b_hs[e2]], scale=1.0 / HD, bias=EPS)
                            self.p.add("dve", (lambda e, e2=e2: e.reciprocal(out=hs[e2][:, 4:8], in_=hs[e2][:, 4:8])),
                                       reads=[b_hs[e2]], writes=[b_hs[e2]])
                            t = tq[e2]
                            self.tt("dve", t[:].rearrange("p (h d) -> p h d", h=4), ps[:, :].rearrange("p (h d) -> p h d", h=4),
                                    hs[e2][:, 4:8].unsqueeze(2).to_broadcast([128, 4, 128]), ALU.mult, [bps, b_hs[e2]], [b_tq[e2]])
                            o = ob[epi % 3]; bo = b_ob[epi % 3]
                            if is_ctx:
                                self.tt("dve", o[:].rearrange("p (h d) -> p h d", h=4), t[:].rearrange("p (h d) -> p h d", h=4),
                                        qkn[:, qi, :, :], ALU.mult, [b_tq[e2], b_qkn], [bo])
                            else:
                                self.tt("pool", t[:].rearrange("p (h d) -> p h d", h=4), t[:].rearrange("p (h d) -> p h d", h=4),
                                        qkn[:, qi, :, :], ALU.mult, [b_tq[e2], b_qkn], [b_tq[e2]])
                                lt = (r0 - cfg.L) // 128
                                cosb = rope[:, lt, 0, :].unsqueeze(1).to_broadcast([128, 4, 64])
                                sinb = rope[:, lt, 1, :].unsqueeze(1).to_broadcast([128, 4, 64])
                                t4 = t[:].rearrange("p (h i two) -> p h i two", h=4, two=2)
                                o4 = o[:].rearrange("p (h i two) -> p h i two", h=4, two=2)
                                te, to = t4[:, :, :, 0], t4[:, :, :, 1]
                                a, b2 = r1[e2], r2[e2]
                                rd = [b_tq[e2], b_rope]
                                self.tt("dve", a[:], te, cosb, ALU.mult, rd, [b_r1[e2]])
                                self.tt("pool", b2[:], to, sinb, ALU.mult, rd, [b_r2[e2]])
                                self.tt("dve", o4[:, :, :, 0], a[:], b2[:], ALU.subtract, [b_r1[e2], b_r2[e2]], [bo])
                                self.tt("dve", a[:], te, sinb, ALU.mult, rd, [b_r1[e2]])
                                self.tt("pool", b2[:], to, cosb, ALU.mult, rd, [b_r2[e2]])
                                self.tt("dve", o4[:, :, :, 1], a[:], b2[:], ALU.add, [b_r1[e2], b_r2[e2]], [bo])
                            dst = S["q_s" if fam == "q" else "k_s"]
                            self.store(dst[r0:r0 + 128, j * 512:(j + 1) * 512], o[:], bo)
                        elif fam == "v":
                            o = ob[epi % 3]; bo = b_ob[epi % 3]
                            self.copy("act", o[:], ps[:, :], [bps], [bo])
                            self.store(S["v_s"][r0:r0 + 128, j * 512:(j + 1) * 512], o[:], bo)
                        elif fam in ("g", "z"):
                            o = ob[epi % 3]; bo = b_ob[epi % 3]
                            self.act(o[:], ps[:, :], AF.Silu, [bps], [bo])
                            dst = S["sg_s" if fam == "g" else "sz_s"]
                            self.store(dst[r0:r0 + 128, j * 512:(j + 1) * 512], o[:], bo)
                        else:
                            d4 = dtt[e2]; bd = b_dtt[e2]
                            self.tt("dve", d4[:, 0, :], ps[:, 0:64], dtb[:], ALU.add, [bps, b_dtb], [bd])
                            self.act(d4[:, 1, :], d4[:, 0, :], AF.Abs, [bd], [bd])
                            self.act(d4[:, 2, :], d4[:, 1, :], AF.Exp, [bd], [bd], scale=-1.0)
                            self.act(d4[:, 2, :], d4[:, 2, :], AF.Ln, [bd], [bd], bias=1.0)
                            self.ts("dve", d4[:, 1, :], d4[:, 0, :], 0.0, None, ALU.max, None, [bd], [bd])
                            self.tt("dve", d4[:, 3, :], d4[:, 1, :], d4[:, 2, :], ALU.add, [bd], [bd])
                            self.store(S["dt_s"][r0:r0 + 128, :], d4[:, 3, :], bd)
            p.barrier()

    def phase_B(self, l):
        nc, p, P, I, S, cfg = self.nc, self.p, self.P, self.I, self.S, self.cfg
        NT, NC = cfg.NT, cfg.NC
        slots, protos, keylists, cls_of = bias_slots(cfg)
        CH = 16
        nch = (NT + CH - 1) // CH
        ng8 = (NT + 7) // 8
        with ExitStack() as es:
            sb = lambda name, shape, dt=F32: es.enter_context(self.sbt(name, shape, dt))
            ktok = sb("b_ktok", [128, NT, 128], BF16); qtok = sb("b_qtok", [128, NT, 128], BF16)
            vtok = sb("b_vtok", [128, NT, 132], BF16); sg = sb("b_sg", [128, NT, 128], BF16)
            KT = sb("b_KT", [128, NT * 128], BF16); QT = sb("b_QT", [128, NT * 128], BF16)
            biasf = sb("b_biasf", [128, 21, 128], F32); biasb = sb("b_biasb", [128, 21, 128], BF16)
            PT = [sb(f"b_PT{i}", [128, 1024], BF16) for i in range(2)]
            rinv = sb("b_rinv", [128, 2])
            obuf = [sb(f"b_ob{i}", [128, 8, 128], BF16) for i in range(2)]
            tp = [es.enter_context(self.pst(f"b_tp{i}", [128, 1024], BF16)) for i in range(2)]
            st = [es.enter_context(self.pst(f"b_st{i}", [128, 1024], F32)) for i in range(2)]
            ops = [es.enter_context(self.pst(f"b_o{i}", [128, 512], F32)) for i in range(2)]
            b_ktok = p.bufs("ktok", nch); b_qtok = p.bufs("qtok", nch); b_vtok = p.bufs("vtok", nch)
            b_sg = p.bufs("sg", nch)
            b_KT = p.bufs("KT", ng8); b_QT = p.bufs("QT", ng8)
            b_biasf = p.buf("biasf"); b_biasb = p.buf("biasb")
            b_PT = p.bufs("PT", 2); b_rinv = p.bufs("rinv", 2); b_ob = p.bufs("ob", 2)
            b_tp = p.bufs("tp", 2); b_st = p.bufs("st", 2); b_o = p.bufs("o", 2)
            ident = P["ident"]
            for c in range(nch):
                t0, t1 = c * CH, min(NT, (c + 1) * CH)
                self.memset("pool", vtok[:, t0:t1, 128:129], 1.0, [b_vtok[c]])
            kv = S["k_s"].rearrange("(t p) c -> p t c", p=128)
            qv = S["q_s"].rearrange("(t p) c -> p t c", p=128)
            vv = S["v_s"].rearrange("(t p) c -> p t c", p=128)
            gv = S["sg_s"].rearrange("(t p) c -> p t c", p=128)
            mv = S["mix_s"].rearrange("(t p) c -> p t c", p=128)
            tpc = 0
            for h in range(NH):
                hs_ = slice(h * 128, (h + 1) * 128)
                self.load(biasf[:], I["bias"][l, h, :, :, :], b_biasf)
                for c in range(nch):
                    t0, t1 = c * CH, min(NT, (c + 1) * CH)
                    self.load(ktok[:, t0:t1, :], kv[:, t0:t1, hs_], b_ktok[c])
                    self.load(qtok[:, t0:t1, :], qv[:, t0:t1, hs_], b_qtok[c])
                for c in range(nch):
                    t0, t1 = c * CH, min(NT, (c + 1) * CH)
                    self.load(vtok[:, t0:t1, 0:128], vv[:, t0:t1, hs_], b_vtok[c])
                    self.load(sg[:, t0:t1, :], gv[:, t0:t1, hs_], b_sg[c])
                self.copy("pool", biasb[:], biasf[:], [b_biasf], [b_biasb])
                for g8 in range(ng8):
                    t0, t1 = g8 * 8, min(NT, g8 * 8 + 8)
                    for (src, bsrc, dstT, bdst) in ((ktok, b_ktok, KT, b_KT), (qtok, b_qtok, QT, b_QT)):
                        tps = tp[tpc % 2]; btp = b_tp[tpc % 2]; tpc += 1
                        for t in range(t0, t1):
                            self.tr(tps[:, (t - t0) * 128:(t - t0 + 1) * 128], src[:, t, :], ident[:], [bsrc[t // CH]], btp)
                        n = (t1 - t0) * 128
                        self.copy("act" if tpc % 2 else "dve", dstT[:, t0 * 128:t0 * 128 + n], tps[:, 0:n], [btp], [bdst[g8]])
                for qi in range(NT):
                    x2 = qi % 2
                    keys = []
                    if qi >= NC:
                        i = qi - NC
                        for j in keylists[i]:
                            keys.append((NC + j, slots[(cls_of[i], j - i)]))
                    for j in range(NC):
                        keys.append((j, None))
                    n = len(keys)
                    for s_, (kt, bs) in enumerate(keys):
                        osl = st[x2][:, s_ * 128:(s_ + 1) * 128]
                        self.mm(osl, KT[:, kt * 128:(kt + 1) * 128], QT[:, qi * 128:(qi + 1) * 128], True, bs is None,
                                [b_KT[kt // 8], b_QT[qi // 8]], b_st[x2])
                        if bs is not None:
                            self.mm(osl, biasb[:, bs, :], ident[:], False, True, [b_biasb], b_st[x2])
                    self.act(PT[x2][:, 0:n * 128], st[x2][:, 0:n * 128], AF.Exp, [b_st[x2]], [b_PT[x2]])
                    for s_, (kt, bs) in enumerate(keys):
                        self.mm(ops[x2][:, 0:129], PT[x2][:, s_ * 128:(s_ + 1) * 128], vtok[:, kt, 0:129], s_ == 0, s_ == n - 1,
                                [b_PT[x2], b_vtok[kt // CH]], b_o[x2])
                    self.p.add("dve", (lambda e, x2=x2: e.reciprocal(out=rinv[:, x2:x2 + 1], in_=ops[x2][:, 128:129])),
                               reads=[b_o[x2]], writes=[b_rinv[x2]])
                    og = (qi // 8) % 2
                    self.stt("dve", obuf[og][:, qi % 8, :], ops[x2][:, 0:128], rinv[:, x2:x2 + 1], sg[:, qi, :], ALU.mult, ALU.mult,
                             [b_o[x2], b_rinv[x2], b_sg[qi // CH]], [b_ob[og]])
                    if qi % 8 == 7 or qi == NT - 1:
                        t0 = (qi // 8) * 8
                        self.store(mv[:, t0:qi + 1, hs_], obuf[og][:, 0:qi + 1 - t0, :], b_ob[og])
            p.barrier()

    def phase_C0(self, l):
        nc, p, P, I, S, cfg = self.nc, self.p, self.P, self.I, self.S, self.cfg
        T, L = cfg.T, cfg.L
        with ExitStack() as es:
            sb = lambda name, shape, dt=F32: es.enter_context(self.sbt(name, shape, dt))
            xr = [sb(f"c_xr{i}", [128, T]) for i in range(2)]
            acc = [sb(f"c_acc{i}", [128, T]) for i in range(2)]
            ob = sb("c_ob", [128, T], BF16)
            cw = sb("c_cw", [128, 32, 5]); cb = sb("c_cb", [128, 32])
            b_xr = p.bufs("xr", 2); b_acc = p.bufs("acc", 2); b_ob = p.buf("ob"); b_cw = p.buf("cw")
            self.load(cw[:], I["conv_w"][l, :, :, :], b_cw)
            self.load(cb[:], I["conv_b"][l, :, :], b_cw)
            for ct in range(32):
                s_ = ct % 2
                eng = "dve"
                x, a = xr[s_], acc[s_]
                self.load(x[:], S["xbc_pre"][ct * 128:(ct + 1) * 128, :], b_xr[s_])
                for (sa, sb_) in ((0, L), (L, T)):
                    self.ts(eng, a[:, sa:sb_], x[:, sa:sb_], cw[:, ct, 2:3], None, ALU.mult, None,
                            [b_xr[s_], b_cw], [b_acc[s_]])
                    for j in (0, 1, 3, 4):
                        sh = j - 2
                        lo, hi = max(sa, sa - sh), min(sb_, sb_ - sh)
                        self.stt(eng, a[:, lo:hi], x[:, lo + sh:hi + sh], cw[:, ct, j:j + 1], a[:, lo:hi], ALU.mult, ALU.add,
                                 [b_xr[s_], b_cw, b_acc[s_]], [b_acc[s_]])
                self.act(ob[:], a[:], AF.Silu, [b_acc[s_], b_cw], [b_ob], bias=cb[:, ct:ct + 1])
                self.store(S["xbc_post"][ct * 128:(ct + 1) * 128, :], ob[:], b_ob, eng="sp")
            p.barrier()

    def phase_C(self, l, dr):
        nc, p, P, I, S, cfg = self.nc, self.p, self.P, self.I, self.S, self.cfg
        NT, NC = cfg.NT, cfg.NC
        order = list(range(NT)) if dr == 0 else (list(range(NC - 1, -1, -1)) + list(range(NT - 1, NC - 1, -1)))
        tcol = 127 if dr == 0 else 0
        with ExitStack() as es:
            sb = lambda name, shape, dt=F32: es.enter_context(self.sbt(name, shape, dt))
            alog = sb("s_alog", [128, 64]); abc = sb("s_abc", [128, 64])
            dsk = sb("s_dsk", [128, 32]); snw = sb("s_snw", [128, D])
            negf = sb("s_negf", [128, 512]); negb = sb("s_negb", [128, 512], BF16)
            BCt = [sb(f"s_BCt{i}", [128, 16, 128], BF16) for i in range(2)]
            xsT = [sb(f"s_xsT{i}", [128, 16, 128], BF16) for i in range(2)]
            dtt = [sb(f"s_dt{i}", [128, 32]) for i in range(2)]
            xst = [sb(f"s_xst{i}", [128, 32, 64], BF16) for i in range(2)]
            Btok = [sb(f"s_Btok{i}", [128, 8, 128], BF16) for i in range(2)]
            da = [sb(f"s_da{i}", [128, 32]) for i in range(2)]
            ecr = [sb(f"s_ecr{i}", [128, 64]) for i in range(2)]
            ncs = [sb(f"s_ncs{i}", [128, 32]) for i in range(2)]
            xd = [sb(f"s_xd{i}", [128, 32, 64], BF16) for i in range(2)]
            xdd = [sb(f"s_xdd{i}", [128, 32, 64], BF16) for i in range(2)]
            CBs = [sb(f"s_CBs{i}", [128, 128]) for i in range(2)]
            dec4 = [sb(f"s_dec{i}", [128, 4, 128]) for i in range(2)]
            etot = [sb(f"s_etot{i}", [128, 4]) for i in range(2)]
            MT4 = [sb(f"s_MT{i}", [128, 4, 128], BF16) for i in range(2)]
            yo = [sb(f"s_yo{i}", [128, 4, 64]) for i in range(2)]
            yacc = [sb(f"s_yacc{i}", [128, 32, 64]) for i in range(2)]
            stT = sb("s_stT", [128, 32, 64]); stb = sb("s_stb", [128, 32, 64], BF16)
            tmp = sb("s_tmp", [128, 32, 64])
            if dr == 1:
                yft = [sb(f"s_yf{i}", [128, D]) for i in range(2)]
                szt = [sb(f"s_sz{i}", [128, D], BF16) for i in range(2)]
                junk = sb("s_junk", [128, D], BF16)
                ssq = sb("s_ssq", [128, 2])
                outb = [sb(f"s_out{i}", [128, D], BF16) for i in range(2)]
                b_yf = p.bufs("yf", 2); b_sz = p.bufs("sz", 2); b_junk = p.buf("junk"); b_ssq = p.bufs("ssq", 2)
                b_out = p.bufs("out", 2)
            tp = [es.enter_context(self.pst(f"s_tp{i}", [128, 1024], BF16)) for i in range(2)]
            cr_ps = es.enter_context(self.pst("s_cr", [128, 64], F32))
            csb_ps = es.enter_context(self.pst("s_csb", [128, 512], F32))
            cb_ps = es.enter_context(self.pst("s_cb", [128, 128], F32))
            yoff_ps = es.enter_context(self.pst("s_yoff", [128, 256], F32))
            yd_ps = es.enter_context(self.pst("s_yd", [128, 256], F32))
            cst_ps = es.enter_context(self.pst("s_cst", [128, 256], F32))
            b_c = p.buf("consts_s")
            b_BCt = p.bufs("BCt", 2); b_xsT = p.bufs("xsT", 2); b_dt = p.bufs("dt", 2); b_xst = p.bufs("xst", 2)
            b_Btok = p.bufs("Btok", 2); b_da = p.bufs("da", 2); b_ecr = p.bufs("ecr", 2); b_ncs = p.bufs("ncs", 2)
            b_xd = p.bufs("xd", 2); b_xdd = p.bufs("xdd", 2); b_CBs = p.bufs("CBs", 2); b_dec = p.bufs("dec", 2)
            b_etot = p.bufs("etot", 2); b_MT = p.bufs("MT", 2); b_yo = p.bufs("yo", 2); b_yacc = p.bufs("yacc", 2)
            b_stT = p.bufs("stT", 8); b_stb = p.bufs("stb", 8); b_tmp = p.buf("tmp")
            b_tp = p.bufs("tp", 2); b_cr = p.buf("cr"); b_csb = p.buf("csb"); b_cb = p.buf("cb")
            b_yoff = p.buf("yoff"); b_yd = p.buf("yd"); b_cst = p.buf("cst")
            ident = P["ident"]
            C = P["consts"]
            Uc = C[:, 1, :] if dr == 0 else C[:, 3, :]
            SLc = C[:, 2, :] if dr == 0 else C[:, 4, :]
            self.load(alog[:], I["a_log"][l, :, :].to_broadcast([128, 64]), b_c)
            self.load(dsk[:], I["d_skip"][l, :, :].to_broadcast([128, 32]), b_c)
            self.load(snw[:], I["ssm_norm"][l, :, :].to_broadcast([128, D]), b_c)
            self.load(negf[:], I["negm"][:, dr, :], b_c)
            self.act(abc[:], alog[:], AF.Exp, [b_c], [b_c])
            self.ts("dve", abc[:], abc[:], -1.0, None, ALU.mult, None, [b_c], [b_c])
            self.copy("dve", negb[:], negf[:], [b_c], [b_c])
            self.memset("dve", stT[:], 0.0, b_stT)
            self.memset("pool", stb[:], 0.0, b_stb)
            tpc = 0
            for it, c in enumerate(order):
                x2 = it % 2
                cs_ = slice(c * 128, (c + 1) * 128)
                self.load(BCt[x2][:], S["xbc_post"][2048:4096, cs_].rearrange("(g n) t -> n g t", n=128), b_BCt[x2])
                self.load(xsT[x2][:], S["xbc_post"][0:2048, cs_].rearrange("(g n) t -> n g t", n=128), b_xsT[x2])
                self.load(dtt[x2][:], S["dt_s"][cs_, dr * 32:(dr + 1) * 32], b_dt[x2])
                if dr == 1:
                    self.load(yft[x2][:], S["yf_s"][cs_, :], b_yf[x2])
                    self.load(szt[x2][:], S["sz_s"][cs_, :], b_sz[x2])
                for half in range(2):
                    tps = tp[tpc % 2]; btp = b_tp[tpc % 2]; tpc += 1
                    for k8 in range(8):
                        self.tr(tps[:, k8 * 128:(k8 + 1) * 128], xsT[x2][:, half * 8 + k8, :], ident[:], [b_xsT[x2]], btp)
                    self.copy("act", xst[x2][:, half * 16:(half + 1) * 16, :].rearrange("p h d -> p (h d)"), tps[:, :],
                              [btp], [b_xst[x2]])
                tps = tp[tpc % 2]; btp = b_tp[tpc % 2]; tpc += 1
                for g in range(8):
                    self.tr(tps[:, g * 128:(g + 1) * 128], BCt[x2][:, g, :], ident[:], [b_BCt[x2]], btp)
                self.copy("act", Btok[x2][:].rearrange("p g n -> p (g n)"), tps[:, :], [btp], [b_Btok[x2]])
                self.tt("dve", da[x2][:], dtt[x2][:], abc[:, dr * 32:(dr + 1) * 32], ALU.mult, [b_dt[x2], b_c], [b_da[x2]])
                self.mm(cr_ps[:, 0:32], Uc, da[x2][:], True, True, [b_da[x2]], b_cr)
                self.mm(cr_ps[:, 32:64], SLc, da[x2][:], True, True, [b_da[x2]], b_cr)
                self.act(ecr[x2][:], cr_ps[:, :], AF.Exp, [b_cr], [b_ecr[x2]])
                self.ts("dve", ncs[x2][:], cr_ps[:, 0:32], -1.0, None, ALU.mult, None, [b_cr], [b_ncs[x2]])
                self.tt("dve", xd[x2][:], xst[x2][:], dtt[x2][:].unsqueeze(2).to_broadcast([128, 32, 64]), ALU.mult,
                        [b_xst[x2], b_dt[x2]], [b_xd[x2]])
                self.tt("pool", xdd[x2][:], xd[x2][:], ecr[x2][:, 32:64].unsqueeze(2).to_broadcast([128, 32, 64]), ALU.mult,
                        [b_xd[x2], b_ecr[x2]], [b_xdd[x2]])
                for g in range(8):
                    g2 = g % 2
                    self.mm(csb_ps[:, :], ident[:], negb[:], True, False, [b_c], b_csb)
                    for h4 in range(4):
                        h = g * 4 + h4
                        self.mm(csb_ps[:, h4 * 128:(h4 + 1) * 128], da[x2][:, h:h + 1].to_broadcast([128, 128]), Uc,
                                False, h4 == 3, [b_da[x2]], b_csb)
                    self.mm(cb_ps[:, :], BCt[x2][:, g, :], BCt[x2][:, 8 + g, :], True, True, [b_BCt[x2]], b_cb)
                    self.copy("act", CBs[g2][:], cb_ps[:, :], [b_cb], [b_CBs[g2]])
                    for h4 in range(4):
                        h = g * 4 + h4
                        self.act(dec4[g2][:, h4, :], csb_ps[:, h4 * 128:(h4 + 1) * 128], AF.Exp, [b_csb, b_ncs[x2]], [b_dec[g2]],
                                 bias=ncs[x2][:, h:h + 1])
                    self.act(etot[g2][:], csb_ps[:, :].rearrange("p (h t) -> p h t", h=4)[:, :, tcol], AF.Exp, [b_csb], [b_etot[g2]])
                    self.tt("dve", MT4[g2][:], dec4[g2][:], CBs[g2][:].unsqueeze(1).to_broadcast([128, 4, 128]), ALU.mult,
                            [b_dec[g2], b_CBs[g2]], [b_MT[g2]])
                    self.mm(yoff_ps[:, :], BCt[x2][:, 8 + g, :], stb[:, g * 4:(g + 1) * 4, :].rearrange("p h d -> p (h d)"),
                            True, True, [b_BCt[x2], b_stb[g]], b_yoff)
                    self.tt("dve", yo[g2][:], yoff_ps[:, :].rearrange("p (h d) -> p h d", h=4),
                            ecr[x2][:, g * 4:(g + 1) * 4].unsqueeze(2).to_broadcast([128, 4, 64]), ALU.mult,
                            [b_yoff, b_ecr[x2]], [b_yo[g2]])
                    for h4 in range(4):
                        h = g * 4 + h4
                        self.mm(yd_ps[:, h4 * 64:(h4 + 1) * 64], MT4[g2][:, h4, :], xd[x2][:, h, :], True, True,
                                [b_MT[g2], b_xd[x2]], b_yd)
                    self.tt("dve", yacc[x2][:, g * 4:(g + 1) * 4, :], yd_ps[:, :].rearrange("p (h d) -> p h d", h=4), yo[g2][:],
                            ALU.add, [b_yd, b_yo[g2]], [b_yacc[x2]])
                    self.mm(cst_ps[:, :], Btok[x2][:, g, :], xdd[x2][:, g * 4:(g + 1) * 4, :].rearrange("p h d -> p (h d)"),
                            True, True, [b_Btok[x2], b_xdd[x2]], b_cst)
                    for h4 in range(4):
                        h = g * 4 + h4
                        self.stt("dve", stT[:, h, :], stT[:, h, :], etot[g2][:, h4:h4 + 1], cst_ps[:, h4 * 64:(h4 + 1) * 64],
                                 ALU.mult, ALU.add, [b_stT[g], b_etot[g2], b_cst], [b_stT[g]])
                    self.copy("pool", stb[:, g * 4:(g + 1) * 4, :], stT[:, g * 4:(g + 1) * 4, :], [b_stT[g]], [b_stb[g]])
                if dr == 0:
                    self.tt("pool", tmp[:], xst[x2][:], dsk[:].unsqueeze(2).to_broadcast([128, 32, 64]), ALU.mult,
                            [b_xst[x2], b_c], [b_tmp])
                    self.tt("pool", yacc[x2][:], yacc[x2][:], tmp[:], ALU.add, [b_yacc[x2], b_tmp], [b_yacc[x2]])
                    self.store(S["yf_s"][cs_, :], yacc[x2][:].rearrange("p h d -> p (h d)"), b_yacc[x2], eng="sp")
                else:
                    ya = yacc[x2][:].rearrange("p h d -> p (h d)")
                    self.tt("pool", ya, ya, yft[x2][:], ALU.add, [b_yacc[x2], b_yf[x2]], [b_yacc[x2]])
                    self.tt("pool", ya, ya, szt[x2][:], ALU.mult, [b_yacc[x2], b_sz[x2]], [b_yacc[x2]])
                    self.act(junk[:], ya, AF.Square, [b_yacc[x2]], [b_junk, b_ssq[x2]], accum_out=ssq[:, x2:x2 + 1])
                    self.act(ssq[:, x2:x2 + 1], ssq[:, x2:x2 + 1], AF.Sqrt, [b_ssq[x2]], [b_ssq[x2]], scale=1.0 / D, bias=EPS)
                    self.p.add("dve", (lambda e, x2=x2: e.reciprocal(out=ssq[:, x2:x2 + 1], in_=ssq[:, x2:x2 + 1])),
                               reads=[b_ssq[x2]], writes=[b_ssq[x2]])
                    self.stt("dve", outb[x2][:], ya, ssq[:, x2:x2 + 1], snw[:], ALU.mult, ALU.mult,
                             [b_yacc[x2], b_ssq[x2], b_c], [b_out[x2]])
                    self.store(S["mix_s"][cs_, D:2 * D], outb[x2][:], b_out[x2], eng="sp")
            p.barrier()

    def phase_D(self, l):
        nc, p, P, I, S, cfg = self.nc, self.p, self.P, self.I, self.S, self.cfg
        X, Xn = S["X"][l], S["X"][l + 1]
        TB = 512
        sbs = [(0, cfg.L)]
        t0 = cfg.L
        while t0 < cfg.T:
            sbs.append((t0, min(TB, cfg.T - t0)))
            t0 += TB
        with ExitStack() as es:
            sb = lambda name, shape, dt=F32: es.enter_context(self.sbt(name, shape, dt))
            mixT = sb("d_mixT", [128, 32, TB], BF16)
            mt = [sb(f"d_mt{i}", [128, 2 * D], BF16) for i in range(2)]
            wo = [sb(f"d_wo{i}", [128, 32, 512], BF16) for i in range(2)]
            xs_ = [sb(f"d_xs{i}", [128, 512]) for i in range(3)]
            tm = [sb(f"d_tm{i}", [128, 512]) for i in range(2)]
            tp = [es.enter_context(self.pst(f"d_tp{i}", [128, 1024], BF16)) for i in range(2)]
            mmps = [es.enter_context(self.pst(f"d_mm{i}", [128, 512], F32)) for i in range(4)]
            b_mixT = p.bufs("mixT", TB // 128); b_mt = p.bufs("mt", 2); b_wo = p.bufs("wo", 2)
            b_xs = p.bufs("xs", 3); b_tm = p.bufs("tm", 2); b_tp = p.bufs("tp", 2); b_mm = p.bufs("mm", 4)
            ident = P["ident"]
            wsrc = S["wbf_out"].rearrange("(kc p) n -> p kc n", p=128)
            tpc = 0; wc = 0; mc = 0; xc = 0
            for (tb0, tbn) in sbs:
                ntile = tbn // 128
                ms = 0 if tb0 < cfg.L else 1
                for i in range(ntile):
                    r0 = tb0 + i * 128
                    m = mt[i % 2]; bm = b_mt[i % 2]
                    self.load(m[:], S["mix_s"][r0:r0 + 128, :], bm)
                    for b4 in range(4):
                        tps = tp[tpc % 2]; btp = b_tp[tpc % 2]; tpc += 1
                        for k8 in range(8):
                            kc = b4 * 8 + k8
                            self.tr(tps[:, k8 * 128:(k8 + 1) * 128], m[:, kc * 128:(kc + 1) * 128], ident[:], [bm], btp)
                        self.copy("act" if b4 % 2 else "dve", mixT[:, b4 * 8:(b4 + 1) * 8, i * 128:(i + 1) * 128],
                                  tps[:, :].rearrange("p (k t) -> p k t", k=8), [btp], [b_mixT[i]])
                for cb in range(4):
                    w = wo[wc % 2]; bw = b_wo[wc % 2]; wc += 1
                    self.load(w[:], wsrc[:, :, cb * 512:(cb + 1) * 512], bw)
                    for i in range(ntile):
                        r0 = tb0 + i * 128
                        ps = mmps[mc % 4]; bps = b_mm[mc % 4]; mc += 1
                        for kc in range(32):
                            self.mm(ps[:, :], mixT[:, kc, i * 128:(i + 1) * 128], w[:, kc, :], kc == 0, kc == 31,
                                    [b_mixT[i], bw], bps)
                        x = xs_[xc % 3]; bx = b_xs[xc % 3]
                        t = tm[xc % 2]; bt = b_tm[xc % 2]; xc += 1
                        self.load(x[:], X[r0:r0 + 128, cb * 512:(cb + 1) * 512], bx)
                        self.tt("dve", t[:], ps[:, :], P["gate"][:, ms, cb * 512:(cb + 1) * 512], ALU.mult, [bps], [bt])
                        self.tt("pool", x[:], x[:], t[:], ALU.add, [bx, bt], [bx])
                        self.store(Xn[r0:r0 + 128, cb * 512:(cb + 1) * 512], x[:], bx)
            p.barrier()


def rope_tables(cfg):
    t = np.arange(cfg.S)
    rows = (t // GW).astype(np.float32)
    cols = (t % GW).astype(np.float32)
    n_pairs = HD // 4
    freqs = (10000.0 ** (-np.arange(n_pairs, dtype=np.float32) / n_pairs)).astype(np.float32)
    ang = np.concatenate([rows[:, None] * freqs, cols[:, None] * freqs], axis=-1).astype(np.float32)
    cos, sin = np.cos(ang).astype(np.float32), np.sin(ang).astype(np.float32)
    tab = np.stack([cos, sin], axis=1)
    return np.ascontiguousarray(tab.reshape(cfg.NL, 128, 2, 64).transpose(1, 0, 2, 3))


def const_tables():
    k = np.arange(128)[:, None]
    t = np.arange(128)[None, :]
    c = np.zeros((128, 6, 128), np.float32)
    c[:, 0] = np.eye(128)
    c[:, 1] = (k <= t)
    c[:, 2] = (k > t)
    c[:, 3] = (k >= t)
    c[:, 4] = (k < t)
    negm = np.zeros((128, 2, 512), np.float32)
    negm[:, 0] = np.tile(NEG * (k > t), (1, 4))
    negm[:, 1] = np.tile(NEG * (k < t), (1, 4))
    return c, negm


def bias_classes(cfg):
    NL = cfg.NL
    out = []
    for i in range(NL):
        r0a = min(max(2 * i - 4, 0), cfg.rows - 8)
        r0b = min(max(2 * i + 1 - 4, 0), cfg.rows - 8)
        lo, hi = r0a, r0b + 7
        js = list(range(lo // 2, hi // 2 + 1))
        out.append(js)
    return out


def bias_slots(cfg):
    NL = cfg.NL
    keylists = bias_classes(cfg)
    slots = {}
    protos = []
    cls_of = []
    for i in range(NL):
        if 2 <= i <= NL - 3:
            cls = 0
        elif i < 2:
            cls = 1 + i
        else:
            cls = 3 + (i - (NL - 2))
        cls_of.append(cls)
        for j in keylists[i]:
            key = (cls, j - i)
            if key not in slots:
                slots[key] = len(protos)
                protos.append((i, j))
    assert len(protos) <= 21, len(protos)
    return slots, protos, keylists, cls_of


def build_bias_tables(cfg, rpb):
    depth = rpb.shape[0]
    slots, protos, keylists, cls_of = bias_slots(cfg)
    qa = np.arange(128)
    qr_off, qc = qa // 64, qa % 64
    tabs = np.full((depth, NH, 128, 21, 128), NEG, np.float32)
    rows = cfg.rows
    for sl, (i, j) in enumerate(protos):
        qr = 2 * i + qr_off
        kr = 2 * j + qr_off
        kc = qc
        r0 = np.clip(qr - 4, 0, rows - 8)
        row_ok = (kr[None, :] >= r0[:, None]) & (kr[None, :] < r0[:, None] + 8)
        c0 = np.clip(qc - 8, 0, GW - 16)
        col_ok = (kc[None, :] >= c0[:, None]) & (kc[None, :] < c0[:, None] + 16)
        ok = row_ok & col_ok
        ridx = np.clip(kr[None, :] - qr[:, None] + 7, 0, 14)
        cidx = np.clip(kc[None, :] - qc[:, None] + 15, 0, 30)
        g = rpb[:, :, ridx, cidx]
        tabs[:, :, :, sl, :] = np.where(ok[None, None], g, NEG)
    return tabs


def host_inputs(cfg, b, inp):
    depth = cfg.depth
    f = lambda a: np.ascontiguousarray(np.asarray(a, dtype=np.float32))
    m = {}
    m["xin"] = f(np.concatenate([inp["ctx"][b], inp["x"][b]], axis=0))
    cv = np.stack([inp["c"][b], inp["c_ctx"]], axis=-1)
    m["cvec"] = f(cv.reshape(16, 128, 2).transpose(1, 0, 2))
    m["ada_w"] = f(inp["ada_w"][:depth])
    ab = inp["ada_b"][:depth]
    m["ada_bf"] = f(ab.reshape(depth, 48, 128).transpose(0, 2, 1))
    m["ada_bg"] = f(ab[:, None, 2 * D:3 * D])
    m["norm_wf"] = f(inp["norm_w"][:depth].reshape(depth, 16, 128).transpose(0, 2, 1))
    m["w_in"] = f(inp["w_in"][:depth])
    m["q_norm"] = f(inp["q_norm"][:depth, None, :])
    m["k_norm"] = f(inp["k_norm"][:depth, None, :])
    m["rope"] = rope_tables(cfg)
    tabs = build_bias_tables(cfg, np.asarray(inp["rpb"][:depth], np.float32))
    m["bias"] = tabs
    cw = inp["conv_w"][:depth]
    m["conv_w"] = f(cw.reshape(depth, 5, 32, 128).transpose(0, 3, 2, 1))
    m["conv_b"] = f(inp["conv_b"][:depth].reshape(depth, 32, 128).transpose(0, 2, 1))
    m["dt_bias"] = f(inp["dt_bias"][:depth].reshape(depth, 1, 64))
    m["a_log"] = f(inp["a_log"][:depth].reshape(depth, 1, 64))
    m["d_skip"] = f(inp["d_skip"][:depth].reshape(depth, 1, 32))
    m["ssm_norm"] = f(inp["ssm_norm"][:depth].reshape(depth, 1, D))
    m["w_out"] = f(inp["w_out"][:depth])
    c, negm = const_tables()
    m["consts"] = c
    m["negm"] = negm
    return m


_NC_CACHE = {}


def kernel(**inputs):
    cfg = Cfg()
    inp = {k: np.asarray(v) for k, v in inputs.items()}
    if "nc" not in _NC_CACHE:
        _NC_CACHE["nc"] = Builder(cfg).build()
    nc = _NC_CACHE["nc"]
    B = inp["x"].shape[0]
    maps = [host_inputs(cfg, i % B, inp) for i in range(B)]
    in_maps = [maps[i % B] for i in range(8)]
    res = run_bass_kernel_spmd(nc, in_maps, core_ids=list(range(8)))
    out = np.stack([res.results[i]["xout"][cfg.L:] for i in range(B)], axis=0)
    return out.astype(np.float32)
```

```python
import math
from contextlib import ExitStack
import numpy as np
import ml_dtypes
import concourse.bass as bass
import concourse.mybir as mybir
from concourse.bass_utils import run_bass_kernel_spmd

F32 = mybir.dt.float32
BF16 = mybir.dt.bfloat16
AF = mybir.ActivationFunctionType
ALU = mybir.AluOpType
AX = mybir.AxisListType

D = 2048
GW = 64
NH = 16
HD = 128
SH = 32
SP = 64
SN = 128
SG = 8
CONV_CH = 4096
IN_COLS = 14400
EPS = 1e-6
NEG = -30000.0


class Cfg:
    def __init__(self, rows=128, L=256, depth=4, debug=False, phases=None):
        self.rows = rows
        self.S = rows * GW
        self.L = L
        self.T = self.S + L
        self.NT = self.T // 128
        self.NC = L // 128
        self.NL = self.S // 128
        self.depth = depth
        self.debug = debug
        self.phases = phases
        self.skew = True


class Buf:
    __slots__ = ("name", "w", "r", "dw", "dr", "sem")

    def __init__(self, name):
        self.name = name
        self.w = {}
        self.r = {}
        self.dw = 0
        self.dr = 0
        self.sem = None


class Op:
    __slots__ = ("eng", "fn", "deps", "marked", "count", "dma", "barrier", "totals")

    def __init__(self, eng, fn):
        self.eng = eng
        self.fn = fn
        self.deps = []
        self.marked = False
        self.count = 0
        self.dma = None
        self.barrier = False
        self.totals = None


ENGS = ("pe", "act", "dve", "pool", "sp")
NDMASEM = 40


class Prog:
    def __init__(self):
        self.ops = []
        self.dma_tot = [0] * NDMASEM
        self.phase_bufs = []
        self.next_dma_sem = 0
        self.last_op = {e: None for e in ENGS}

    def buf(self, name):
        b = Buf(name)
        self.phase_bufs.append(b)
        return b

    def bufs(self, name, n):
        return [self.buf(f"{name}{i}") for i in range(n)]

    def _sem_for(self, b):
        if b.sem is None:
            assert self.next_dma_sem < NDMASEM, "out of DMA semaphores in this phase"
            b.sem = self.next_dma_sem
            self.next_dma_sem += 1
        return b.sem

    def add(self, eng, fn, reads=(), writes=(), dma_buf=None):
        op = Op(eng, fn)
        is_dma = dma_buf is not None
        deps = op.deps
        for b in reads:
            for e, o in b.w.items():
                if e == eng and eng == "pe" and not is_dma:
                    continue
                deps.append(o)
            if b.dw:
                deps.append((b.sem, b.dw))
        for b in writes:
            for e, o in b.w.items():
                if e != eng or is_dma:
                    deps.append(o)
            for e, o in b.r.items():
                if e != eng or is_dma:
                    deps.append(o)
            if b.dw:
                deps.append((b.sem, b.dw))
            if b.dr:
                deps.append((b.sem, b.dr))
        for d in deps:
            if isinstance(d, Op):
                d.marked = True
        if is_dma:
            s = self._sem_for(dma_buf)
            self.dma_tot[s] += 16
            op.dma = (s, self.dma_tot[s])
            for b in reads:
                b.dr = self.dma_tot[s] if b is dma_buf else b.dr
                if b is not dma_buf:
                    raise AssertionError("dma touching tracked buf other than dma_buf")
            for b in writes:
                if b is not dma_buf:
                    raise AssertionError("dma touching tracked buf other than dma_buf")
                b.dw = self.dma_tot[s]
                b.w = {}
                b.r = {}
        else:
            for b in reads:
                b.r[eng] = op
            for b in writes:
                b.w = {eng: op}
                b.r = {}
                b.dw = 0
                b.dr = 0
        self.ops.append(op)
        if not is_dma:
            self.last_op[eng] = op
        return op

    def barrier(self):
        for e in ENGS:
            lo = self.last_op[e]
            if lo is not None:
                lo.marked = True
        op = Op(None, None)
        op.barrier = True
        self.ops.append(op)
        for b in self.phase_bufs:
            b.w = {}
            b.r = {}
            b.dw = 0
            b.dr = 0
            b.sem = None
        self.phase_bufs = []
        self.next_dma_sem = 0
        for e in ENGS:
            self.last_op[e] = None

    def emit(self, nc, csem, dsem):
        cnt = {e: 0 for e in ENGS}
        dtot = [0] * NDMASEM
        for op in self.ops:
            if op.barrier:
                op.totals = (dict(cnt), list(dtot))
                continue
            if op.dma is not None:
                dtot[op.dma[0]] = op.dma[1]
            elif op.marked:
                cnt[op.eng] += 1
                op.count = cnt[op.eng]
        ops = self.ops
        engobj = {}

        def run(eng, e):
            waited = {}

            def wait(key, sem, val):
                if waited.get(key, 0) < val:
                    e.wait_ge(sem, val)
                    waited[key] = val

            for op in ops:
                if op.barrier:
                    c, dt = op.totals
                    for en in ENGS:
                        if c[en]:
                            wait(en, csem[en], c[en])
                    for i, v in enumerate(dt):
                        if v:
                            wait(i, dsem[i], v)
                    continue
                if op.eng != eng:
                    continue
                for d in op.deps:
                    if isinstance(d, Op):
                        wait(d.eng, csem[d.eng], d.count)
                    else:
                        wait(d[0], dsem[d[0]], d[1])
                ins = op.fn(e)
                if op.dma is not None:
                    ins.then_inc(dsem[op.dma[0]], 16)
                elif op.marked:
                    ins.then_inc(csem[eng], 1)

        with nc.Block() as block:
            @block.tensor
            def _(e):
                run("pe", e)

            @block.scalar
            def _(e):
                run("act", e)

            @block.vector
            def _(e):
                run("dve", e)

            @block.gpsimd
            def _(e):
                run("pool", e)

            @block.sync
            def _(e):
                run("sp", e)


class Builder:
    def __init__(self, cfg):
        self.cfg = cfg
        self.nc = bass.Bass("TRN2", target_bir_lowering=False)
        self.p = Prog()
        self.dbg_names = []
        self._uid = 0

    def sbt(self, name, shape, dt):
        self._uid += 1
        return self.nc.sbuf_tensor(f"{name}_{self._uid}", shape, dt)

    def pst(self, name, shape, dt):
        self._uid += 1
        return self.nc.psum_tensor(f"{name}_{self._uid}", shape, dt)

    def dram_in(self, name, shape, dt=F32):
        return self.nc.dram_tensor(name, list(shape), dt, kind="ExternalInput").ap()

    def dram_scratch(self, name, shape, dt, dbg=True):
        if self.cfg.debug and dbg:
            self.dbg_names.append(name)
            return self.nc.dram_tensor(name, list(shape), dt, kind="ExternalOutput").ap()
        return self.nc.dram_tensor(name, list(shape), dt).ap()

    def load(self, out_ap, in_ap, buf, eng="sp", **kw):
        self.p.add(eng, lambda e: e.dma_start(out=out_ap, in_=in_ap, **kw), writes=[buf], dma_buf=buf)

    def store(self, out_ap, in_ap, buf, eng="pool", **kw):
        self.p.add(eng, lambda e: e.dma_start(out=out_ap, in_=in_ap, **kw), reads=[buf], dma_buf=buf)

    def d2d(self, out_ap, in_ap, b, eng="pool"):
        self.p.add(eng, lambda e: e.dma_start(out=out_ap, in_=in_ap), writes=[b], dma_buf=b)

    def mm(self, out_ap, lhsT, rhs, start, stop, reads, wbuf):
        self.p.add("pe", lambda e: e.matmul(out_ap, lhsT=lhsT, rhs=rhs, start=start, stop=stop),
                   reads=reads, writes=[wbuf])

    def tr(self, out_ap, in_ap, ident, reads, wbuf):
        self.p.add("pe", lambda e: e.transpose(out_ap, in_ap, ident), reads=reads, writes=[wbuf])

    def act(self, out, in_, func, reads, writes, eng="act", **kw):
        self.p.add(eng, lambda e: e.activation(out=out, in_=in_, func=func, **kw), reads=reads, writes=writes)

    def tt(self, eng, out, in0, in1, op, reads, writes):
        self.p.add(eng, lambda e: e.tensor_tensor(out=out, in0=in0, in1=in1, op=op), reads=reads, writes=writes)

    def ts(self, eng, out, in0, s1, s2, op0, op1, reads, writes):
        if s2 is None:
            self.p.add(eng, lambda e: e.tensor_scalar(out=out, in0=in0, scalar1=s1, scalar2=None, op0=op0),
                       reads=reads, writes=writes)
        else:
            self.p.add(eng, lambda e: e.tensor_scalar(out=out, in0=in0, scalar1=s1, scalar2=s2, op0=op0, op1=op1),
                       reads=reads, writes=writes)

    def stt(self, eng, out, in0, scalar, in1, op0, op1, reads, writes):
        self.p.add(eng, lambda e: e.scalar_tensor_tensor(out=out, in0=in0, scalar=scalar, in1=in1, op0=op0, op1=op1),
                   reads=reads, writes=writes)

    def copy(self, eng, out, in_, reads, writes):
        if eng == "act":
            self.p.add(eng, lambda e: e.copy(out=out, in_=in_), reads=reads, writes=writes)
        else:
            self.p.add(eng, lambda e: e.tensor_copy(out=out, in_=in_), reads=reads, writes=writes)

    def memset(self, eng, ap, val, writes):
        self.p.add(eng, lambda e: e.memset(ap, val), writes=writes)

    def build(self):
        cfg = self.cfg
        nc = self.nc
        T, NT, depth = cfg.T, cfg.NT, cfg.depth
        I = {}
        I["xin"] = self.dram_in("xin", [T, D])
        I["cvec"] = self.dram_in("cvec", [128, 16, 2])
        I["ada_w"] = self.dram_in("ada_w", [depth, D, 3 * D])
        I["ada_bf"] = self.dram_in("ada_bf", [depth, 128, 48])
        I["ada_bg"] = self.dram_in("ada_bg", [depth, 1, D])
        I["norm_wf"] = self.dram_in("norm_wf", [depth, 128, 16])
        I["w_in"] = self.dram_in("w_in", [depth, D, IN_COLS])
        I["q_norm"] = self.dram_in("q_norm", [depth, 1, HD])
        I["k_norm"] = self.dram_in("k_norm", [depth, 1, HD])
        I["rope"] = self.dram_in("rope", [128, cfg.NL, 2, 64])
        I["bias"] = self.dram_in("bias", [depth, NH, 128, 21, 128])
        I["conv_w"] = self.dram_in("conv_w", [depth, 128, 32, 5])
        I["conv_b"] = self.dram_in("conv_b", [depth, 128, 32])
        I["dt_bias"] = self.dram_in("dt_bias", [depth, 1, 64])
        I["a_log"] = self.dram_in("a_log", [depth, 1, 64])
        I["d_skip"] = self.dram_in("d_skip", [depth, 1, 32])
        I["ssm_norm"] = self.dram_in("ssm_norm", [depth, 1, D])
        I["w_out"] = self.dram_in("w_out", [depth, 2 * D, D])
        I["consts"] = self.dram_in("consts", [128, 6, 128])
        I["negm"] = self.dram_in("negm", [128, 2, 512])
        self.I = I
        xout = nc.dram_tensor("xout", [T, D], F32, kind="ExternalOutput").ap()
        S = {}
        S["X"] = [I["xin"]] + [self.dram_scratch(f"X{l}", [T, D], F32, dbg=False) for l in range(1, depth)] + [xout]
        S["wbf_in"] = [self.dram_scratch(f"wbf_in{i}", [D, IN_COLS], BF16, dbg=False) for i in range(2)]
        S["wbf_out"] = [self.dram_scratch(f"wbf_out{i}", [2 * D, D], BF16, dbg=False) for i in range(2)]
        S["gate"] = self.dram_scratch("gate_s", [2, D], F32)
        for n in ("q_s", "k_s", "v_s", "sg_s", "sz_s"):
            S[n] = self.dram_scratch(n, [T, D], BF16)
        S["xbc_pre"] = self.dram_scratch("xbc_pre", [CONV_CH, T], BF16)
        S["xbc_post"] = self.dram_scratch("xbc_post", [CONV_CH, T], BF16)
        S["dt_s"] = self.dram_scratch("dt_s", [T, 64], F32)
        S["yf_s"] = self.dram_scratch("yf_s", [T, D], F32)
        S["mix_s"] = self.dram_scratch("mix_s", [T, 2 * D], BF16)
        self.S = S

        with ExitStack() as es:
            csem = {e: es.enter_context(nc.semaphore(f"c_{e}")) for e in ENGS}
            dsem = [es.enter_context(nc.semaphore(f"d_{i}")) for i in range(NDMASEM)]
            P = {}
            P["consts"] = es.enter_context(self.sbt("p_consts", [128, 6, 128], F32))
            P["ident"] = es.enter_context(self.sbt("p_ident", [128, 128], BF16))
            P["modA"] = es.enter_context(self.sbt("p_modA", [128, 2, 16], F32))
            P["modB"] = es.enter_context(self.sbt("p_modB", [128, 2, 16], F32))
            P["gate"] = es.enter_context(self.sbt("p_gate", [128, 2, D], F32))
            self.P = P
            self.es_top = es
            self.phase_init()
            for l in range(depth):
                self.layer(l)
            self.p.barrier()
            self.p.emit(nc, csem, dsem)
        return nc

    def want(self, name):
        return self.cfg.phases is None or name in self.cfg.phases

    def phase_init(self):
        p, P, I = self.p, self.P, self.I
        b = p.buf("consts")
        self.load(P["consts"][:], I["consts"][:, :, :], b)
        self.copy("dve", P["ident"][:], P["consts"][:, 0, :], [b], [b])
        p.barrier()

    def layer(self, l):
        if self.want("W") and l == 0:
            self.issue_W(0)
            self.p.barrier()
        if self.want("M"):
            self.phase_M(l)
        if self.want("A"):
            self.phase_A(l)
        if self.want("B"):
            self.phase_B(l)
        if self.want("C0"):
            self.phase_C0(l)
        if self.want("C1"):
            self.phase_C(l, 0)
        if self.want("C2"):
            self.phase_C(l, 1)
        if self.want("D"):
            self.phase_D(l)

    def issue_W(self, l):
        I, S = self.I, self.S
        if l >= self.cfg.depth:
            return
        nchunk = 16
        rows = D // nchunk
        bd = [self.p.buf("d2da"), self.p.buf("d2db")]
        for i in range(nchunk):
            self.d2d(S["wbf_in"][l % 2][i * rows:(i + 1) * rows, :], I["w_in"][l, i * rows:(i + 1) * rows, :], bd[i % 2])
        rows = 2 * D // nchunk
        for i in range(nchunk):
            self.d2d(S["wbf_out"][l % 2][i * rows:(i + 1) * rows, :], I["w_out"][l, i * rows:(i + 1) * rows, :], bd[i % 2])

    def phase_M(self, l):
        nc, p, P, I, S = self.nc, self.p, self.P, self.I, self.S
        with ExitStack() as es:
            sb = lambda name, shape, dt=F32: es.enter_context(self.sbt(name, shape, dt))
            cv = sb("m_cv", [128, 16, 2]); sc = sb("m_sc", [128, 16, 2])
            wt = [sb(f"m_w{i}", [128, 16, 512]) for i in range(2)]
            bfm = sb("m_bf", [128, 48]); nw = sb("m_nw", [128, 16]); bg = sb("m_bg", [2, D])
            mf = sb("m_mf", [128, 32, 2]); grow = sb("m_grow", [2, D])
            psf = es.enter_context(self.pst("m_psf", [128, 32, 2], F32))
            psg = [es.enter_context(self.pst(f"m_psg{i}", [2, 512], F32)) for i in range(2)]
            b_cv, b_sc, b_bf, b_nw, b_bg, b_mf, b_grow, b_psf = (p.buf(n) for n in
                                                                ("cv", "sc", "bf", "nw", "bg", "mf", "grow", "psf"))
            b_wt = p.bufs("wt", 2); b_psg = p.bufs("psg", 2)
            b_modA, b_modB, b_gate = p.buf("modA"), p.buf("modB"), p.buf("gate")
            self.load(cv[:], I["cvec"][:, :, :], b_cv)
            self.load(bfm[:], I["ada_bf"][l, :, :], b_bf)
            self.load(nw[:], I["norm_wf"][l, :, :], b_nw)
            self.load(bg[:], I["ada_bg"][l, :, :].to_broadcast([2, D]), b_bg)
            self.act(sc[:], cv[:], AF.Silu, [b_cv], [b_sc])
            wv = I["ada_w"][l].rearrange("(kc p) n -> p kc n", p=128)
            for blk in range(12):
                w = wt[blk % 2]; bw = b_wt[blk % 2]
                self.load(w[:], wv[:, :, blk * 512:(blk + 1) * 512], bw)
                if blk < 8:
                    for ft in range(4):
                        j = blk * 4 + ft
                        for kc in range(16):
                            self.mm(psf[:, j, :], w[:, kc, ft * 128:(ft + 1) * 128], sc[:, kc, :],
                                    kc == 0, kc == 15, [bw, b_sc], b_psf)
                else:
                    g = blk - 8
                    ps = psg[g % 2]; bps = b_psg[g % 2]
                    for kc in range(16):
                        self.mm(ps[:, :], sc[:, kc, :], w[:, kc, :], kc == 0, kc == 15, [bw, b_sc], bps)
                    self.tt("dve", grow[:, g * 512:(g + 1) * 512], ps[:, :], bg[:, g * 512:(g + 1) * 512],
                            ALU.add, [bps, b_bg], [b_grow])
            self.tt("dve", mf[:], psf[:], bfm[:, 0:32].unsqueeze(2).to_broadcast([128, 32, 2]), ALU.add,
                    [b_psf, b_bf], [b_mf])
            for s, ci in ((0, 1), (1, 0)):
                self.stt("dve", P["modA"][:, s, :], mf[:, 16:32, ci], 1.0, nw[:], ALU.add, ALU.mult,
                         [b_mf, b_nw], [b_modA])
                self.copy("dve", P["modB"][:, s, :], mf[:, 0:16, ci], [b_mf], [b_modB])
            self.store(S["gate"][:, :], grow[:], b_grow, eng="sp")
            p.barrier()
            self.load(P["gate"][:, 0, :], S["gate"][1:2, :].to_broadcast([128, D]), b_gate)
            self.load(P["gate"][:, 1, :], S["gate"][0:1, :].to_broadcast([128, D]), b_gate)
            p.barrier()

    def phase_A(self, l):
        nc, p, P, I, S, cfg = self.nc, self.p, self.P, self.I, self.S, self.cfg
        X = S["X"][l]
        TB = 1024
        NTB = TB // 128
        sbs = [(0, cfg.L)]
        t0 = cfg.L
        while t0 < cfg.T:
            sbs.append((t0, min(TB, cfg.T - t0)))
            t0 += TB
        with ExitStack() as es:
            sb = lambda name, shape, dt=F32: es.enter_context(self.sbt(name, shape, dt))
            hT2 = [sb(f"a_hT{i}", [128, 16, TB], BF16) for i in range(2)]
            xt = [sb(f"a_xt{i}", [128, D]) for i in range(2)]
            xn = [sb("a_xn", [128, D], BF16)]
            junk = sb("a_junk", [128, D], BF16)
            ss = sb("a_ss", [128, 2]); rstd = sb("a_rstd", [128, 2])
            wt = [sb(f"a_wt{i}", [128, 16, 512], BF16) for i in range(3)]
            qkn = sb("a_qkn", [128, 2, 4, 128])
            rope = sb("a_rope", [128, NTB, 2, 64])
            ropew = sb("a_ropew", [128, NTB, 2, 4, 64])
            dtb = sb("a_dtb", [128, 64])
            sq = [sb(f"a_sq{i}", [128, 512]) for i in range(2)]
            hs = [sb(f"a_hs{i}", [128, 8]) for i in range(2)]
            tsb = [sb(f"a_tsb{i}", [128, 512]) for i in range(2)]
            m1 = [sb(f"a_m1{i}", [128, 4, 64]) for i in range(2)]
            m2 = [sb(f"a_m2{i}", [128, 4, 64]) for i in range(2)]
            m3 = [sb(f"a_m3{i}", [128, 4, 64]) for i in range(2)]
            m4 = [sb(f"a_m4{i}", [128, 4, 64]) for i in range(2)]
            ob = [sb(f"a_ob{i}", [128, 512], BF16) for i in range(3)]
            of = [sb(f"a_of{i}", [128, 512], BF16) for i in range(2)]
            dtt = [sb(f"a_dtt{i}", [128, 4, 64]) for i in range(2)]
            tp = [es.enter_context(self.pst(f"a_tp{i}", [128, 1024], BF16)) for i in range(2)]
            mmps = [es.enter_context(self.pst(f"a_mm{i}", [128, 512], F32)) for i in range(4)]
            b_hT2 = [p.bufs(f"hT{i}_", NTB) for i in range(2)]
            b_xt = p.bufs("xt", 2); b_xn = p.bufs("xn", 1); b_junk = p.buf("junk")
            b_ss = p.bufs("ss", 2); b_rstd = p.bufs("rstd", 2)
            b_wt = p.bufs("wt", 3); b_qkn = p.buf("qkn"); b_rope = p.buf("rope"); b_ropew = p.buf("ropew"); b_dtb = p.buf("dtb")
            b_sq = p.bufs("sq", 2); b_hs = p.bufs("hs", 2); b_tsb = p.bufs("tsb", 2)
            b_m1 = p.bufs("m1", 2); b_m2 = p.bufs("m2", 2); b_m3 = p.bufs("m3", 2); b_m4 = p.bufs("m4", 2)
            b_ob = p.bufs("ob", 3); b_of = p.bufs("of", 2)
            b_dtt = p.bufs("dtt", 2)
            b_tp = p.bufs("tp", 2); b_mm = p.bufs("mm", 4)
            ident = P["ident"]
            if l == 0:
                pass

            self.load(qkn[:, 0, :, :], I["q_norm"][l, :, :].unsqueeze(1).to_broadcast([128, 4, 128]), b_qkn)
            self.load(qkn[:, 1, :, :], I["k_norm"][l, :, :].unsqueeze(1).to_broadcast([128, 4, 128]), b_qkn)
            self.load(dtb[:], I["dt_bias"][l, :, :].to_broadcast([128, 64]), b_dtb)
            self.ts("dve", qkn[:, 0, :, :], qkn[:, 0, :, :], HD ** -0.5, None, ALU.mult, None, [b_qkn], [b_qkn])

            wsrc = S["wbf_in"][l % 2].rearrange("(kc p) n -> p kc n", p=128)
            xbcT = S["xbc_pre"]
            if self.want("W"):
                self.issue_W(l + 1)
            st_ = {"w": 0, "mm": 0, "epi": 0}

            def step1(sbi):
                tb0, tbn = sbs[sbi]
                ntile = tbn // 128
                is_ctx = tb0 < cfg.L
                ms = 0 if is_ctx else 1
                hT = hT2[sbi % 2]; b_hT = b_hT2[sbi % 2]
                if not is_ctx:
                    lt0 = (tb0 - cfg.L) // 128
                    self.load(rope[:, 0:ntile, :, :], I["rope"][:, lt0:lt0 + ntile, :, :], b_rope)
                    for i in range(ntile):
                        for qi in range(2):
                            w2 = qkn[:, qi, 0, :].rearrange("p (i two) -> p i two", two=2)
                            we, wo = w2[:, :, 0], w2[:, :, 1]
                            cos_, sin_ = rope[:, i, 0, :], rope[:, i, 1, :]
                            for a_, (tb_, w_) in enumerate(((cos_, we), (sin_, wo), (sin_, we), (cos_, wo))):
                                self.tt("pool", ropew[:, i, qi, a_, :], tb_, w_, ALU.mult, [b_rope, b_qkn], [b_ropew])
                for i in range(ntile):
                    s = i % 2
                    r0 = tb0 + i * 128
                    self.load(xt[s][:], X[r0:r0 + 128, :], b_xt[s])
                    self.act(junk[:], xt[s][:], AF.Square, [b_xt[s]], [b_junk, b_ss[s]], accum_out=ss[:, s:s + 1])
                    self.act(rstd[:, s:s + 1], ss[:, s:s + 1], AF.Sqrt, [b_ss[s]], [b_rstd[s]], scale=1.0 / D, bias=EPS)
                    self.p.add("dve", (lambda e, s=s: e.reciprocal(out=rstd[:, s:s + 1], in_=rstd[:, s:s + 1])),
                               reads=[b_rstd[s]], writes=[b_rstd[s]])
                    self.act(xn[0][:], xt[s][:], AF.Copy, [b_xt[s], b_rstd[s]], [b_xn[0]], scale=rstd[:, s:s + 1])
                    for half in range(2):
                        for k8 in range(8):
                            kc = half * 8 + k8
                            self.tr(tp[half][:, k8 * 128:(k8 + 1) * 128], xn[0][:, kc * 128:(kc + 1) * 128], ident[:],
                                    [b_xn[0]], b_tp[half])
                        for k8 in range(8):
                            kc = half * 8 + k8
                            src = tp[half][:, k8 * 128:(k8 + 1) * 128]
                            dst = hT[:, kc, i * 128:(i + 1) * 128]
                            if k8 % 2 == 0:
                                self.act(dst, src, AF.Identity, [b_tp[half]], [b_hT[i]],
                                         scale=P["modA"][:, ms, kc:kc + 1], bias=P["modB"][:, ms, kc:kc + 1])
                            else:
                                self.ts("dve", dst, src, P["modA"][:, ms, kc:kc + 1], P["modB"][:, ms, kc:kc + 1],
                                        ALU.mult, ALU.add, [b_tp[half]], [b_hT[i]])

            def do_block(sbi, fam, c0, ncol, j):
                tb0, tbn = sbs[sbi]
                ntile = tbn // 128
                is_ctx = tb0 < cfg.L
                hT = hT2[sbi % 2]; b_hT = b_hT2[sbi % 2]
                ws = st_["w"] % 3; st_["w"] += 1
                w = wt[ws]; bw = b_wt[ws]
                self.load(w[:, :, 0:ncol], wsrc[:, :, c0:c0 + ncol], bw)
                if fam == "xbc":
                    for cs_ in range(4):
                        ch0 = j * 512 + cs_ * 128
                        for tg in range(0, tbn, 512):
                            tn = min(512, tbn - tg)
                            m = st_["mm"] % 4; st_["mm"] += 1
                            tiles = [b_hT[(tg + q) // 128] for q in range(0, tn, 128)]
                            for kc in range(16):
                                self.mm(mmps[m][:, 0:tn], w[:, kc, cs_ * 128:(cs_ + 1) * 128], hT[:, kc, tg:tg + tn],
                                        kc == 0, kc == 15, [bw] + tiles, b_mm[m])
                            o = of[st_["epi"] % 2]; bo = b_of[st_["epi"] % 2]; st_["epi"] += 1
                            self.copy("act", o[:, 0:tn], mmps[m][:, 0:tn], [b_mm[m]], [bo])
                            self.store(xbcT[ch0:ch0 + 128, tb0 + tg:tb0 + tg + tn], o[:, 0:tn], bo)
                    return
                for i in range(ntile):
                    r0 = tb0 + i * 128
                    m = st_["mm"] % 4; st_["mm"] += 1
                    ps = mmps[m]; bps = b_mm[m]
                    for kc in range(16):
                        self.mm(ps[:, 0:ncol], hT[:, kc, i * 128:(i + 1) * 128], w[:, kc, 0:ncol],
                                kc == 0, kc == 15, [bw, b_hT[i]], bps)
                    e2 = st_["epi"] % 2; st_["epi"] += 1
                    epi = st_["epi"]
                    if fam in ("q", "k"):
                        qi = 0 if fam == "q" else 1
                        t = tsb[e2]
                        self.copy("act", t[:], ps[:, :], [bps], [b_tsb[e2]])
                        self.act(sq[e2][:], ps[:, :], AF.Square, [bps], [b_sq[e2]])
                        self.p.add("dve", (lambda e, e2=e2: e.reduce_sum(out=hs[e2][:, 0:4], in_=sq[e2][:].rearrange("p (h d) -> p h d", h=4), axis=AX.X)),
                                   reads=[b_sq[e2]], writes=[b_hs[e2]])
                        self.act(hs[e2][:, 4:8], hs[e2][:, 0:4], AF.Sqrt, [b_hs[e2]], [b_hs[e2]], scale=1.0 / HD, bias=EPS)
                        self.p.add("dve", (lambda e, e2=e2: e.reciprocal(out=hs[e2][:, 4:8], in_=hs[e2][:, 4:8])),
                                   reads=[b_hs[e2]], writes=[b_hs[e2]])
                        o = ob[epi % 3]; bo = b_ob[epi % 3]
                        if is_ctx:
                            t3 = t[:].rearrange("p (h d) -> p h d", h=4)
                            self.tt("dve", t3, t3, hs[e2][:, 4:8].unsqueeze(2).to_broadcast([128, 4, 128]), ALU.mult,
                                    [b_tsb[e2], b_hs[e2]], [b_tsb[e2]])
                            self.tt("dve", o[:].rearrange("p (h d) -> p h d", h=4), t3, qkn[:, qi, :, :], ALU.mult,
                                    [b_tsb[e2], b_qkn], [bo])
                        else:
                            t4 = t[:].rearrange("p (h i two) -> p h i two", h=4, two=2)
                            o4 = o[:].rearrange("p (h i two) -> p h i two", h=4, two=2)
                            te, to = t4[:, :, :, 0], t4[:, :, :, 1]
                            A = [ropew[:, i, qi, a_, :].unsqueeze(1).to_broadcast([128, 4, 64]) for a_ in range(4)]
                            rb = hs[e2][:, 4:8].unsqueeze(2).to_broadcast([128, 4, 64])
                            rd = [b_tsb[e2], b_ropew]
                            self.tt("dve", m1[e2][:], te, A[0], ALU.mult, rd, [b_m1[e2]])
                            self.tt("pool", m2[e2][:], to, A[1], ALU.mult, rd, [b_m2[e2]])
                            self.tt("dve", m3[e2][:], te, A[2], ALU.mult, rd, [b_m3[e2]])
                            self.tt("pool", m4[e2][:], to, A[3], ALU.mult, rd, [b_m4[e2]])
                            self.tt("dve", m1[e2][:], m1[e2][:], m2[e2][:], ALU.subtract, [b_m1[e2], b_m2[e2]], [b_m1[e2]])
                            self.tt("pool", m3[e2][:], m3[e2][:], m4[e2][:], ALU.add, [b_m3[e2], b_m4[e2]], [b_m3[e2]])
                            self.tt("dve", o4[:, :, :, 0], m1[e2][:], rb, ALU.mult, [b_m1[e2], b_hs[e2]], [bo])
                            self.tt("pool", o4[:, :, :, 1], m3[e2][:], rb, ALU.mult, [b_m3[e2], b_hs[e2]], [bo])
                        dst = S["q_s" if fam == "q" else "k_s"]
                        self.store(dst[r0:r0 + 128, j * 512:(j + 1) * 512], o[:], bo)
                    elif fam == "v":
                        o = ob[epi % 3]; bo = b_ob[epi % 3]
                        self.copy("act", o[:], ps[:, :], [bps], [bo])
                        self.store(S["v_s"][r0:r0 + 128, j * 512:(j + 1) * 512], o[:], bo)
                    elif fam in ("g", "z"):
                        o = ob[epi % 3]; bo = b_ob[epi % 3]
                        self.act(o[:], ps[:, :], AF.Silu, [bps], [bo])
                        dst = S["sg_s" if fam == "g" else "sz_s"]
                        self.store(dst[r0:r0 + 128, j * 512:(j + 1) * 512], o[:], bo)
                    else:
                        d4 = dtt[e2]; bd = b_dtt[e2]
                        self.tt("dve", d4[:, 0, :], ps[:, 0:64], dtb[:], ALU.add, [bps, b_dtb], [bd])
                        self.act(d4[:, 1, :], d4[:, 0, :], AF.Abs, [bd], [bd])
                        self.act(d4[:, 2, :], d4[:, 1, :], AF.Exp, [bd], [bd], scale=-1.0)
                        self.act(d4[:, 2, :], d4[:, 2, :], AF.Ln, [bd], [bd], bias=1.0)
                        self.ts("dve", d4[:, 1, :], d4[:, 0, :], 0.0, None, ALU.max, None, [bd], [bd])
                        self.tt("dve", d4[:, 3, :], d4[:, 1, :], d4[:, 2, :], ALU.add, [bd], [bd])
                        self.store(S["dt_s"][r0:r0 + 128, :], d4[:, 3, :], bd)

            blocks = []
            for f, fam in enumerate(("q", "k", "v", "g", "z")):
                for j in range(4):
                    blocks.append((fam, f * D + j * 512, 512, j))
            for j in range(8):
                blocks.append(("xbc", 5 * D + j * 512, 512, j))
            blocks.append(("dt", 5 * D + CONV_CH, 64, 0))
            if getattr(cfg, "a_fams", None):
                blocks = [b_ for b_ in blocks if b_[0] in cfg.a_fams]
            hoist = max(0, len(blocks) - 10)
            step1(0)
            for sbi in range(len(sbs)):
                for bi, (fam, c0, ncol, j) in enumerate(blocks):
                    if bi == hoist and sbi + 1 < len(sbs):
                        step1(sbi + 1)
                    do_block(sbi, fam, c0, ncol, j)
            p.barrier()

    def phase_B(self, l):
        nc, p, P, I, S, cfg = self.nc, self.p, self.P, self.I, self.S, self.cfg
        NT, NC = cfg.NT, cfg.NC
        slots, protos, keylists, cls_of = bias_slots(cfg)
        CH = 16
        nch = (NT + CH - 1) // CH
        ng8 = (NT + 7) // 8
        with ExitStack() as es:
            sb = lambda name, shape, dt=F32: es.enter_context(self.sbt(name, shape, dt))
            ktok = sb("b_ktok", [128, NT, 128], BF16); qtok = sb("b_qtok", [128, NT, 128], BF16)
            vtok = sb("b_vtok", [128, NT, 132], BF16); sg = sb("b_sg", [128, NT, 128], BF16)
            KT = sb("b_KT", [128, NT * 128], BF16); QT = sb("b_QT", [128, NT * 128], BF16)
            biasf = sb("b_biasf", [128, 21, 128], F32); biasb = sb("b_biasb", [128, 21, 128], BF16)
            PT = [sb(f"b_PT{i}", [128, 1024], BF16) for i in range(2)]
            rinv = sb("b_rinv", [128, 2])
            obuf = [sb(f"b_ob{i}", [128, 8, 128], BF16) for i in range(2)]
            tp = [es.enter_context(self.pst(f"b_tp{i}", [128, 1024], BF16)) for i in range(2)]
            st = [es.enter_context(self.pst(f"b_st{i}", [128, 1024], F32)) for i in range(2)]
            ops = [es.enter_context(self.pst(f"b_o{i}", [128, 512], F32)) for i in range(2)]
            b_ktok = p.bufs("ktok", nch); b_qtok = p.bufs("qtok", nch); b_vtok = p.bufs("vtok", nch)
            b_sg = p.bufs("sg", nch)
            b_KT = p.bufs("KT", ng8); b_QT = p.bufs("QT", ng8)
            b_biasf = p.buf("biasf"); b_biasb = p.buf("biasb")
            b_PT = p.bufs("PT", 2); b_rinv = p.bufs("rinv", 2); b_ob = p.bufs("ob", 2)
            b_tp = p.bufs("tp", 2); b_st = p.bufs("st", 2); b_o = p.bufs("o", 2)
            ident = P["ident"]
            for c in range(nch):
                t0, t1 = c * CH, min(NT, (c + 1) * CH)
                self.memset("pool", vtok[:, t0:t1, 128:129], 1.0, [b_vtok[c]])
            kv = S["k_s"].rearrange("(t p) c -> p t c", p=128)
            qv = S["q_s"].rearrange("(t p) c -> p t c", p=128)
            vv = S["v_s"].rearrange("(t p) c -> p t c", p=128)
            gv = S["sg_s"].rearrange("(t p) c -> p t c", p=128)
            mv = S["mix_s"].rearrange("(t p) c -> p t c", p=128)
            tpc = 0
            for h in range(NH):
                hs_ = slice(h * 128, (h + 1) * 128)
                self.load(biasf[:], I["bias"][l, h, :, :, :], b_biasf)
                for c in range(nch):
                    t0, t1 = c * CH, min(NT, (c + 1) * CH)
                    self.load(ktok[:, t0:t1, :], kv[:, t0:t1, hs_], b_ktok[c])
                    self.load(qtok[:, t0:t1, :], qv[:, t0:t1, hs_], b_qtok[c])
                for c in range(nch):
                    t0, t1 = c * CH, min(NT, (c + 1) * CH)
                    self.load(vtok[:, t0:t1, 0:128], vv[:, t0:t1, hs_], b_vtok[c])
                    self.load(sg[:, t0:t1, :], gv[:, t0:t1, hs_], b_sg[c])
                self.copy("pool", biasb[:], biasf[:], [b_biasf], [b_biasb])
                for g8 in range(ng8):
                    t0, t1 = g8 * 8, min(NT, g8 * 8 + 8)
                    for (src, bsrc, dstT, bdst) in ((ktok, b_ktok, KT, b_KT), (qtok, b_qtok, QT, b_QT)):
                        tps = tp[tpc % 2]; btp = b_tp[tpc % 2]; tpc += 1
                        for t in range(t0, t1):
                            self.tr(tps[:, (t - t0) * 128:(t - t0 + 1) * 128], src[:, t, :], ident[:], [bsrc[t // CH]], btp)
                        n = (t1 - t0) * 128
                        self.copy("act" if tpc % 2 else "dve", dstT[:, t0 * 128:t0 * 128 + n], tps[:, 0:n], [btp], [bdst[g8]])
                def keys_of(qi):
                    keys = []
                    if qi >= NC:
                        i = qi - NC
                        for j in keylists[i]:
                            keys.append((NC + j, slots[(cls_of[i], j - i)]))
                    for j in range(NC):
                        keys.append((j, None))
                    return keys

                def qk(qi):
                    x2 = qi % 2
                    keys = keys_of(qi)
                    n = len(keys)
                    for s_, (kt, bs) in enumerate(keys):
                        osl = st[x2][:, s_ * 128:(s_ + 1) * 128]
                        self.mm(osl, KT[:, kt * 128:(kt + 1) * 128], QT[:, qi * 128:(qi + 1) * 128], True, bs is None,
                                [b_KT[kt // 8], b_QT[qi // 8]], b_st[x2])
                        if bs is not None:
                            self.mm(osl, biasb[:, bs, :], ident[:], False, True, [b_biasb], b_st[x2])
                    self.act(PT[x2][:, 0:n * 128], st[x2][:, 0:n * 128], AF.Exp, [b_st[x2]], [b_PT[x2]])

                def pv(qi):
                    x2 = qi % 2
                    keys = keys_of(qi)
                    n = len(keys)
                    for s_, (kt, bs) in enumerate(keys):
                        self.mm(ops[x2][:, 0:129], PT[x2][:, s_ * 128:(s_ + 1) * 128], vtok[:, kt, 0:129], s_ == 0, s_ == n - 1,
                                [b_PT[x2], b_vtok[kt // CH]], b_o[x2])
                    self.p.add("dve", (lambda e, x2=x2: e.reciprocal(out=rinv[:, x2:x2 + 1], in_=ops[x2][:, 128:129])),
                               reads=[b_o[x2]], writes=[b_rinv[x2]])
                    og = (qi // 8) % 2
                    self.stt("dve", obuf[og][:, qi % 8, :], ops[x2][:, 0:128], rinv[:, x2:x2 + 1], sg[:, qi, :], ALU.mult, ALU.mult,
                             [b_o[x2], b_rinv[x2], b_sg[qi // CH]], [b_ob[og]])
                    if qi % 8 == 7 or qi == NT - 1:
                        t0 = (qi // 8) * 8
                        self.store(mv[:, t0:qi + 1, hs_], obuf[og][:, 0:qi + 1 - t0, :], b_ob[og])

                qk(0)
                for qi in range(NT):
                    if qi + 1 < NT:
                        qk(qi + 1)
                    pv(qi)
            p.barrier()

    def phase_C0(self, l):
        nc, p, P, I, S, cfg = self.nc, self.p, self.P, self.I, self.S, self.cfg
        T, L, S_ = cfg.T, cfg.L, cfg.S
        TP = T + 8
        with ExitStack() as es:
            sb = lambda name, shape, dt=F32: es.enter_context(self.sbt(name, shape, dt))
            xr = [sb(f"c_xr{i}", [128, TP], BF16) for i in range(2)]
            ob = [sb(f"c_ob{i}", [128, T], BF16) for i in range(2)]
            dg = [sb(f"c_dg{i}", [128, 5, 128], BF16) for i in range(2)]
            cw = sb("c_cw", [128, 32, 5]); cb = sb("c_cb", [128, 32])
            ps = [es.enter_context(self.pst(f"c_ps{i}", [128, 512], F32)) for i in range(4)]
            b_xr = p.bufs("xr", 2); b_ob = p.bufs("ob", 2); b_dg = p.bufs("dg", 2); b_cw = p.buf("cw"); b_ps = p.bufs("ps", 4)
            self.load(cw[:], I["conv_w"][l, :, :, :], b_cw)
            self.load(cb[:], I["conv_b"][l, :, :], b_cw)
            for i in range(2):
                self.memset("pool", xr[i][:], 0.0, [b_xr[i]])
            blocks = [(0, L, 2)]
            t0 = L
            while t0 < T:
                n = min(512, T - t0)
                blocks.append((t0, n, t0 + 6))
                t0 += n
            pc = 0
            for ct in range(32):
                s_ = ct % 2
                x = xr[s_]
                self.load(x[:, 2:2 + L], S["xbc_pre"][ct * 128:(ct + 1) * 128, 0:L], b_xr[s_])
                self.load(x[:, L + 6:L + 6 + S_], S["xbc_pre"][ct * 128:(ct + 1) * 128, L:T], b_xr[s_])
                for j in range(5):
                    self.ts("dve", dg[s_][:, j, :], P["consts"][:, 0, :], cw[:, ct, j:j + 1], None, ALU.mult, None,
                            [b_cw], [b_dg[s_]])
                for (tk0, n, pc0) in blocks:
                    q = pc % 4; pc += 1
                    for j in range(5):
                        self.mm(ps[q][:, 0:n], dg[s_][:, j, :], x[:, pc0 + j - 2:pc0 + j - 2 + n], j == 0, j == 4,
                                [b_dg[s_], b_xr[s_]], b_ps[q])
                    self.act(ob[s_][:, tk0:tk0 + n], ps[q][:, 0:n], AF.Silu, [b_ps[q], b_cw], [b_ob[s_]], bias=cb[:, ct:ct + 1])
                self.store(S["xbc_post"][ct * 128:(ct + 1) * 128, :], ob[s_][:], b_ob[s_], eng="sp")
            p.barrier()

    def phase_C(self, l, dr):
        nc, p, P, I, S, cfg = self.nc, self.p, self.P, self.I, self.S, self.cfg
        NT, NC = cfg.NT, cfg.NC
        order = list(range(NT)) if dr == 0 else (list(range(NC - 1, -1, -1)) + list(range(NT - 1, NC - 1, -1)))
        tcol = 127 if dr == 0 else 0
        with ExitStack() as es:
            sb = lambda name, shape, dt=F32: es.enter_context(self.sbt(name, shape, dt))
            alog = sb("s_alog", [128, 64]); abc = sb("s_abc", [128, 64])
            dsk = sb("s_dsk", [128, 32]); snw = sb("s_snw", [128, D])
            negf = sb("s_negf", [128, 512]); negb = sb("s_negb", [128, 512], BF16)
            BCt = [sb(f"s_BCt{i}", [128, 16, 128], BF16) for i in range(2)]
            xsT = [sb(f"s_xsT{i}", [128, 16, 128], BF16) for i in range(2)]
            dtt = [sb(f"s_dt{i}", [128, 32]) for i in range(2)]
            xst = [sb(f"s_xst{i}", [128, 32, 64], BF16) for i in range(2)]
            Btok = [sb(f"s_Btok{i}", [128, 8, 128], BF16) for i in range(2)]
            da = [sb(f"s_da{i}", [128, 32]) for i in range(2)]
            ecr = [sb(f"s_ecr{i}", [128, 64]) for i in range(2)]
            ncs = [sb(f"s_ncs{i}", [128, 32]) for i in range(2)]
            xd = [sb(f"s_xd{i}", [128, 32, 64], BF16) for i in range(2)]
            xdd = [sb(f"s_xdd{i}", [128, 32, 64], BF16) for i in range(2)]
            CBs = [sb(f"s_CBs{i}", [128, 128]) for i in range(2)]
            dec4 = [sb(f"s_dec{i}", [128, 4, 128]) for i in range(2)]
            etot = [sb(f"s_etot{i}", [128, 4]) for i in range(2)]
            MT4 = [sb(f"s_MT{i}", [128, 4, 128], BF16) for i in range(2)]
            yo = [sb(f"s_yo{i}", [128, 4, 64]) for i in range(2)]
            yacc = [sb(f"s_yacc{i}", [128, 32, 64]) for i in range(2)]
            stT = sb("s_stT", [128, 32, 64]); stb = sb("s_stb", [128, 32, 64], BF16)
            tmp = sb("s_tmp", [128, 32, 64])
            if dr == 1:
                yft = [sb(f"s_yf{i}", [128, D]) for i in range(2)]
                szt = [sb(f"s_sz{i}", [128, D], BF16) for i in range(2)]
                junk = sb("s_junk", [128, D], BF16)
                ssq = sb("s_ssq", [128, 2])
                outb = [sb(f"s_out{i}", [128, D], BF16) for i in range(2)]
                b_yf = p.bufs("yf", 2); b_sz = p.bufs("sz", 2); b_junk = p.buf("junk"); b_ssq = p.bufs("ssq", 2)
                b_out = p.bufs("out", 2)
            tp = [es.enter_context(self.pst("s_tp", [128, 1024], BF16))]
            csb_l = [es.enter_context(self.pst(f"s_csb{i}", [128, 512], F32)) for i in range(2)]
            cy_l = [es.enter_context(self.pst(f"s_cy{i}", [128, 512], F32)) for i in range(2)]
            ydc_l = [es.enter_context(self.pst(f"s_ydc{i}", [128, 512], F32)) for i in range(2)]
            cr_t = es.enter_context(self.pst("s_cr", [128, 512], F32))
            cr_ps = cr_t[:, 0:64]
            b_c = p.buf("consts_s")
            b_BCt = p.bufs("BCt", 2); b_xsT = p.bufs("xsT", 2); b_dt = p.bufs("dt", 2); b_xst = p.bufs("xst", 2)
            b_Btok = p.bufs("Btok", 2); b_da = p.bufs("da", 2); b_ecr = p.bufs("ecr", 2); b_ncs = p.bufs("ncs", 2)
            b_xd = p.bufs("xd", 2); b_xdd = p.bufs("xdd", 2); b_CBs = p.bufs("CBs", 2); b_dec = p.bufs("dec", 2)
            b_etot = p.bufs("etot", 2); b_MT = p.bufs("MT", 2); b_yo = p.bufs("yo", 2); b_yacc = p.bufs("yacc", 2)
            b_stT = p.bufs("stT", 8); b_stb = p.bufs("stb", 8); b_tmp = p.buf("tmp")
            b_tp = p.bufs("tp", 1); b_cr = p.buf("cr"); b_csb = p.bufs("csb", 2); b_cb = p.bufs("cy", 2)
            b_yoff = b_cb; b_yd = p.bufs("ydc", 2); b_cst = b_yd
            ident = P["ident"]
            C = P["consts"]
            Uc = C[:, 1, :] if dr == 0 else C[:, 3, :]
            SLc = C[:, 2, :] if dr == 0 else C[:, 4, :]
            self.load(alog[:], I["a_log"][l, :, :].to_broadcast([128, 64]), b_c)
            self.load(dsk[:], I["d_skip"][l, :, :].to_broadcast([128, 32]), b_c)
            self.load(snw[:], I["ssm_norm"][l, :, :].to_broadcast([128, D]), b_c)
            self.load(negf[:], I["negm"][:, dr, :], b_c)
            self.act(abc[:], alog[:], AF.Exp, [b_c], [b_c])
            self.ts("dve", abc[:], abc[:], -1.0, None, ALU.mult, None, [b_c], [b_c])
            self.copy("dve", negb[:], negf[:], [b_c], [b_c])
            self.memset("dve", stT[:], 0.0, b_stT)
            self.memset("pool", stb[:], 0.0, b_stb)
            def prologue(it):
                c = order[it]
                x2 = it % 2
                cs_ = slice(c * 128, (c + 1) * 128)
                self.load(BCt[x2][:], S["xbc_post"][2048:4096, cs_].rearrange("(g n) t -> n g t", n=128), b_BCt[x2])
                self.load(xsT[x2][:], S["xbc_post"][0:2048, cs_].rearrange("(g n) t -> n g t", n=128), b_xsT[x2])
                self.load(dtt[x2][:], S["dt_s"][cs_, dr * 32:(dr + 1) * 32], b_dt[x2])
                if dr == 1:
                    self.load(yft[x2][:], S["yf_s"][cs_, :], b_yf[x2])
                    self.load(szt[x2][:], S["sz_s"][cs_, :], b_sz[x2])
                tps = tp[0]; btp = b_tp[0]
                for half in range(2):
                    for k8 in range(8):
                        self.tr(tps[:, k8 * 128:(k8 + 1) * 128], xsT[x2][:, half * 8 + k8, :], ident[:], [b_xsT[x2]], btp)
                    self.copy("act", xst[x2][:, half * 16:(half + 1) * 16, :].rearrange("p h d -> p (h d)"), tps[:, :],
                              [btp], [b_xst[x2]])
                for g in range(8):
                    self.tr(tps[:, g * 128:(g + 1) * 128], BCt[x2][:, g, :], ident[:], [b_BCt[x2]], btp)
                self.copy("act", Btok[x2][:].rearrange("p g n -> p (g n)"), tps[:, :], [btp], [b_Btok[x2]])
                self.tt("dve", da[x2][:], dtt[x2][:], abc[:, dr * 32:(dr + 1) * 32], ALU.mult, [b_dt[x2], b_c], [b_da[x2]])
                self.mm(cr_ps[:, 0:32], Uc, da[x2][:], True, True, [b_da[x2]], b_cr)
                self.mm(cr_ps[:, 32:64], SLc, da[x2][:], True, True, [b_da[x2]], b_cr)
                self.act(ecr[x2][:], cr_ps, AF.Exp, [b_cr], [b_ecr[x2]])
                self.ts("dve", ncs[x2][:], cr_ps[:, 0:32], -1.0, None, ALU.mult, None, [b_cr], [b_ncs[x2]])
                self.tt("dve", xd[x2][:], xst[x2][:], dtt[x2][:].unsqueeze(2).to_broadcast([128, 32, 64]), ALU.mult,
                        [b_xst[x2], b_dt[x2]], [b_xd[x2]])
                self.tt("pool", xdd[x2][:], xd[x2][:], ecr[x2][:, 32:64].unsqueeze(2).to_broadcast([128, 32, 64]), ALU.mult,
                        [b_xd[x2], b_ecr[x2]], [b_xdd[x2]])

            def stage1(it, g):
                x2 = it % 2
                g2 = g % 2
                csb_ps = csb_l[g2]; bcsb = b_csb[g2]
                cb_ps = cy_l[g2][:, 0:128]; yoff_ps = cy_l[g2][:, 128:384]
                self.mm(csb_ps[:, :], ident[:], negb[:], True, False, [b_c], bcsb)
                for h4 in range(4):
                    h = g * 4 + h4
                    self.mm(csb_ps[:, h4 * 128:(h4 + 1) * 128], da[x2][:, h:h + 1].to_broadcast([128, 128]), Uc,
                            False, h4 == 3, [b_da[x2]], bcsb)
                self.mm(cb_ps, BCt[x2][:, g, :], BCt[x2][:, 8 + g, :], True, True, [b_BCt[x2]], b_cb[g2])
                self.mm(yoff_ps, BCt[x2][:, 8 + g, :], stb[:, g * 4:(g + 1) * 4, :].rearrange("p h d -> p (h d)"),
                        True, True, [b_BCt[x2], b_stb[g]], b_yoff[g2])
                self.copy("act", CBs[g2][:], cb_ps, [b_cb[g2]], [b_CBs[g2]])
                for h4 in range(4):
                    h = g * 4 + h4
                    self.act(dec4[g2][:, h4, :], csb_ps[:, h4 * 128:(h4 + 1) * 128], AF.Exp, [bcsb, b_ncs[x2]], [b_dec[g2]],
                             bias=ncs[x2][:, h:h + 1])
                self.act(etot[g2][:], csb_ps[:, :].rearrange("p (h t) -> p h t", h=4)[:, :, tcol], AF.Exp, [bcsb], [b_etot[g2]])
                self.tt("dve", MT4[g2][:], dec4[g2][:], CBs[g2][:].unsqueeze(1).to_broadcast([128, 4, 128]), ALU.mult,
                        [b_dec[g2], b_CBs[g2]], [b_MT[g2]])
                self.tt("dve", yo[g2][:], yoff_ps.rearrange("p (h d) -> p h d", h=4),
                        ecr[x2][:, g * 4:(g + 1) * 4].unsqueeze(2).to_broadcast([128, 4, 64]), ALU.mult,
                        [b_yoff[g2], b_ecr[x2]], [b_yo[g2]])

            def stage2(it, g):
                x2 = it % 2
                g2 = g % 2
                yd_ps = ydc_l[g2][:, 0:256]; cst_ps = ydc_l[g2][:, 256:512]
                for h4 in range(4):
                    h = g * 4 + h4
                    self.mm(yd_ps[:, h4 * 64:(h4 + 1) * 64], MT4[g2][:, h4, :], xd[x2][:, h, :], True, True,
                            [b_MT[g2], b_xd[x2]], b_yd[g2])
                self.mm(cst_ps, Btok[x2][:, g, :], xdd[x2][:, g * 4:(g + 1) * 4, :].rearrange("p h d -> p (h d)"),
                        True, True, [b_Btok[x2], b_xdd[x2]], b_cst[g2])
                self.tt("dve", yacc[x2][:, g * 4:(g + 1) * 4, :], yd_ps.rearrange("p (h d) -> p h d", h=4), yo[g2][:],
                        ALU.add, [b_yd[g2], b_yo[g2]], [b_yacc[x2]])
                for h4 in range(4):
                    h = g * 4 + h4
                    self.stt("dve", stT[:, h, :], stT[:, h, :], etot[g2][:, h4:h4 + 1], cst_ps[:, h4 * 64:(h4 + 1) * 64],
                             ALU.mult, ALU.add, [b_stT[g], b_etot[g2], b_cst[g2]], [b_stT[g]])
                self.copy("pool", stb[:, g * 4:(g + 1) * 4, :], stT[:, g * 4:(g + 1) * 4, :], [b_stT[g]], [b_stb[g]])

            def epilogue(it):
                c = order[it]
                x2 = it % 2
                cs_ = slice(c * 128, (c + 1) * 128)
                if dr == 0:
                    self.tt("pool", tmp[:], xst[x2][:], dsk[:].unsqueeze(2).to_broadcast([128, 32, 64]), ALU.mult,
                            [b_xst[x2], b_c], [b_tmp])
                    self.tt("pool", yacc[x2][:], yacc[x2][:], tmp[:], ALU.add, [b_yacc[x2], b_tmp], [b_yacc[x2]])
                    self.store(S["yf_s"][cs_, :], yacc[x2][:].rearrange("p h d -> p (h d)"), b_yacc[x2], eng="sp")
                else:
                    ya = yacc[x2][:].rearrange("p h d -> p (h d)")
                    self.tt("pool", ya, ya, yft[x2][:], ALU.add, [b_yacc[x2], b_yf[x2]], [b_yacc[x2]])
                    self.tt("pool", ya, ya, szt[x2][:], ALU.mult, [b_yacc[x2], b_sz[x2]], [b_yacc[x2]])
                    self.act(junk[:], ya, AF.Square, [b_yacc[x2]], [b_junk, b_ssq[x2]], accum_out=ssq[:, x2:x2 + 1])
                    self.act(ssq[:, x2:x2 + 1], ssq[:, x2:x2 + 1], AF.Sqrt, [b_ssq[x2]], [b_ssq[x2]], scale=1.0 / D, bias=EPS)
                    self.p.add("dve", (lambda e, x2=x2: e.reciprocal(out=ssq[:, x2:x2 + 1], in_=ssq[:, x2:x2 + 1])),
                               reads=[b_ssq[x2]], writes=[b_ssq[x2]])
                    self.stt("dve", outb[x2][:], ya, ssq[:, x2:x2 + 1], snw[:], ALU.mult, ALU.mult,
                             [b_yacc[x2], b_ssq[x2], b_c], [b_out[x2]])
                    self.store(S["mix_s"][cs_, D:2 * D], outb[x2][:], b_out[x2], eng="sp")

            nit = len(order)
            if not self.cfg.skew:
                for it in range(nit):
                    prologue(it)
                    for g in range(8):
                        stage1(it, g)
                        stage2(it, g)
                    epilogue(it)
            else:
                prologue(0)
                stage1(0, 0)
                for it in range(nit):
                    for g in range(8):
                        if g == 3 and it + 1 < nit:
                            prologue(it + 1)
                        if g + 1 < 8:
                            stage1(it, g + 1)
                        elif it + 1 < nit:
                            stage1(it + 1, 0)
                        stage2(it, g)
                    epilogue(it)
            p.barrier()

    def phase_D(self, l):
        nc, p, P, I, S, cfg = self.nc, self.p, self.P, self.I, self.S, self.cfg
        X, Xn = S["X"][l], S["X"][l + 1]
        TB = 512
        sbs = [(0, cfg.L)]
        t0 = cfg.L
        while t0 < cfg.T:
            sbs.append((t0, min(TB, cfg.T - t0)))
            t0 += TB
        with ExitStack() as es:
            sb = lambda name, shape, dt=F32: es.enter_context(self.sbt(name, shape, dt))
            mixT = sb("d_mixT", [128, 32, TB], BF16)
            mt = [sb(f"d_mt{i}", [128, 2 * D], BF16) for i in range(2)]
            wo = [sb(f"d_wo{i}", [128, 32, 512], BF16) for i in range(2)]
            xs_ = [sb(f"d_xs{i}", [128, 512]) for i in range(3)]
            tm = [sb(f"d_tm{i}", [128, 512]) for i in range(2)]
            tp = [es.enter_context(self.pst(f"d_tp{i}", [128, 1024], BF16)) for i in range(2)]
            mmps = [es.enter_context(self.pst(f"d_mm{i}", [128, 512], F32)) for i in range(4)]
            b_mixT = p.bufs("mixT", TB // 128); b_mt = p.bufs("mt", 2); b_wo = p.bufs("wo", 2)
            b_xs = p.bufs("xs", 3); b_tm = p.bufs("tm", 2); b_tp = p.bufs("tp", 2); b_mm = p.bufs("mm", 4)
            ident = P["ident"]
            wsrc = S["wbf_out"][l % 2].rearrange("(kc p) n -> p kc n", p=128)
            tpc = 0; wc = 0; mc = 0; xc = 0
            for (tb0, tbn) in sbs:
                ntile = tbn // 128
                ms = 0 if tb0 < cfg.L else 1
                for i in range(ntile):
                    r0 = tb0 + i * 128
                    m = mt[i % 2]; bm = b_mt[i % 2]
                    self.load(m[:], S["mix_s"][r0:r0 + 128, :], bm)
                    for b4 in range(4):
                        tps = tp[tpc % 2]; btp = b_tp[tpc % 2]; tpc += 1
                        for k8 in range(8):
                            kc = b4 * 8 + k8
                            self.tr(tps[:, k8 * 128:(k8 + 1) * 128], m[:, kc * 128:(kc + 1) * 128], ident[:], [bm], btp)
                        self.copy("act" if b4 % 2 else "dve", mixT[:, b4 * 8:(b4 + 1) * 8, i * 128:(i + 1) * 128],
                                  tps[:, :].rearrange("p (k t) -> p k t", k=8), [btp], [b_mixT[i]])
                for cb in range(4):
                    w = wo[wc % 2]; bw = b_wo[wc % 2]; wc += 1
                    self.load(w[:], wsrc[:, :, cb * 512:(cb + 1) * 512], bw)
                    for i in range(ntile):
                        r0 = tb0 + i * 128
                        ps = mmps[mc % 4]; bps = b_mm[mc % 4]; mc += 1
                        for kc in range(32):
                            self.mm(ps[:, :], mixT[:, kc, i * 128:(i + 1) * 128], w[:, kc, :], kc == 0, kc == 31,
                                    [b_mixT[i], bw], bps)
                        x = xs_[xc % 3]; bx = b_xs[xc % 3]
                        t = tm[xc % 2]; bt = b_tm[xc % 2]; xc += 1
                        self.load(x[:], X[r0:r0 + 128, cb * 512:(cb + 1) * 512], bx)
                        self.tt("dve", t[:], ps[:, :], P["gate"][:, ms, cb * 512:(cb + 1) * 512], ALU.mult, [bps], [bt])
                        self.tt("pool", x[:], x[:], t[:], ALU.add, [bx, bt], [bx])
                        self.store(Xn[r0:r0 + 128, cb * 512:(cb + 1) * 512], x[:], bx)
            p.barrier()


def rope_tables(cfg):
    t = np.arange(cfg.S)
    rows = (t // GW).astype(np.float32)
    cols = (t % GW).astype(np.float32)
    n_pairs = HD // 4
    freqs = (10000.0 ** (-np.arange(n_pairs, dtype=np.float32) / n_pairs)).astype(np.float32)
    ang = np.concatenate([rows[:, None] * freqs, cols[:, None] * freqs], axis=-1).astype(np.float32)
    cos, sin = np.cos(ang).astype(np.float32), np.sin(ang).astype(np.float32)
    tab = np.stack([cos, sin], axis=1)
    return np.ascontiguousarray(tab.reshape(cfg.NL, 128, 2, 64).transpose(1, 0, 2, 3))


def const_tables():
    k = np.arange(128)[:, None]
    t = np.arange(128)[None, :]
    c = np.zeros((128, 6, 128), np.float32)
    c[:, 0] = np.eye(128)
    c[:, 1] = (k <= t)
    c[:, 2] = (k > t)
    c[:, 3] = (k >= t)
    c[:, 4] = (k < t)
    negm = np.zeros((128, 2, 512), np.float32)
    negm[:, 0] = np.tile(NEG * (k > t), (1, 4))
    negm[:, 1] = np.tile(NEG * (k < t), (1, 4))
    return c, negm


def bias_classes(cfg):
    NL = cfg.NL
    out = []
    for i in range(NL):
        r0a = min(max(2 * i - 4, 0), cfg.rows - 8)
        r0b = min(max(2 * i + 1 - 4, 0), cfg.rows - 8)
        lo, hi = r0a, r0b + 7
        js = list(range(lo // 2, hi // 2 + 1))
        out.append(js)
    return out


def bias_slots(cfg):
    NL = cfg.NL
    keylists = bias_classes(cfg)
    slots = {}
    protos = []
    cls_of = []
    for i in range(NL):
        if 2 <= i <= NL - 3:
            cls = 0
        elif i < 2:
            cls = 1 + i
        else:
            cls = 3 + (i - (NL - 2))
        cls_of.append(cls)
        for j in keylists[i]:
            key = (cls, j - i)
            if key not in slots:
                slots[key] = len(protos)
                protos.append((i, j))
    assert len(protos) <= 21, len(protos)
    return slots, protos, keylists, cls_of


def build_bias_tables(cfg, rpb):
    depth = rpb.shape[0]
    slots, protos, keylists, cls_of = bias_slots(cfg)
    qa = np.arange(128)
    qr_off, qc = qa // 64, qa % 64
    tabs = np.full((depth, NH, 128, 21, 128), NEG, np.float32)
    rows = cfg.rows
    for sl, (i, j) in enumerate(protos):
        qr = 2 * i + qr_off
        kr = 2 * j + qr_off
        kc = qc
        r0 = np.clip(qr - 4, 0, rows - 8)
        row_ok = (kr[None, :] >= r0[:, None]) & (kr[None, :] < r0[:, None] + 8)
        c0 = np.clip(qc - 8, 0, GW - 16)
        col_ok = (kc[None, :] >= c0[:, None]) & (kc[None, :] < c0[:, None] + 16)
        ok = row_ok & col_ok
        ridx = np.clip(kr[None, :] - qr[:, None] + 7, 0, 14)
        cidx = np.clip(kc[None, :] - qc[:, None] + 15, 0, 30)
        g = rpb[:, :, ridx, cidx]
        tabs[:, :, :, sl, :] = np.where(ok[None, None], g, NEG)
    return tabs


def host_inputs(cfg, b, inp):
    depth = cfg.depth
    f = lambda a: np.ascontiguousarray(np.asarray(a, dtype=np.float32))
    m = {}
    m["xin"] = f(np.concatenate([inp["ctx"][b], inp["x"][b]], axis=0))
    cv = np.stack([inp["c"][b], inp["c_ctx"]], axis=-1)
    m["cvec"] = f(cv.reshape(16, 128, 2).transpose(1, 0, 2))
    m["ada_w"] = f(inp["ada_w"][:depth])
    ab = inp["ada_b"][:depth]
    m["ada_bf"] = f(ab.reshape(depth, 48, 128).transpose(0, 2, 1))
    m["ada_bg"] = f(ab[:, None, 2 * D:3 * D])
    m["norm_wf"] = f(inp["norm_w"][:depth].reshape(depth, 16, 128).transpose(0, 2, 1))
    m["w_in"] = f(inp["w_in"][:depth])
    m["q_norm"] = f(inp["q_norm"][:depth, None, :])
    m["k_norm"] = f(inp["k_norm"][:depth, None, :])
    m["rope"] = rope_tables(cfg)
    tabs = build_bias_tables(cfg, np.asarray(inp["rpb"][:depth], np.float32))
    m["bias"] = tabs
    cw = inp["conv_w"][:depth]
    m["conv_w"] = f(cw.reshape(depth, 5, 32, 128).transpose(0, 3, 2, 1))
    m["conv_b"] = f(inp["conv_b"][:depth].reshape(depth, 32, 128).transpose(0, 2, 1))
    m["dt_bias"] = f(inp["dt_bias"][:depth].reshape(depth, 1, 64))
    m["a_log"] = f(inp["a_log"][:depth].reshape(depth, 1, 64))
    m["d_skip"] = f(inp["d_skip"][:depth].reshape(depth, 1, 32))
    m["ssm_norm"] = f(inp["ssm_norm"][:depth].reshape(depth, 1, D))
    m["w_out"] = f(inp["w_out"][:depth])
    c, negm = const_tables()
    m["consts"] = c
    m["negm"] = negm
    return m


_NC_CACHE = {}


def kernel(**inputs):
    cfg = Cfg()
    inp = {k: np.asarray(v) for k, v in inputs.items()}
    if "nc" not in _NC_CACHE:
        _NC_CACHE["nc"] = Builder(cfg).build()
    nc = _NC_CACHE["nc"]
    B = inp["x"].shape[0]
    maps = [host_inputs(cfg, i % B, inp) for i in range(B)]
    in_maps = [maps[i % B] for i in range(8)]
    res = run_bass_kernel_spmd(nc, in_maps, core_ids=list(range(8)))
    out = np.stack([res.results[i]["xout"][cfg.L:] for i in range(B)], axis=0)
    return out.astype(np.float32)
```

```python
import math
from contextlib import ExitStack
import numpy as np
import ml_dtypes
import concourse.bass as bass
import concourse.mybir as mybir
from concourse.bass_utils import run_bass_kernel_spmd

F32 = mybir.dt.float32
BF16 = mybir.dt.bfloat16
AF = mybir.ActivationFunctionType
ALU = mybir.AluOpType
AX = mybir.AxisListType

D = 2048
GW = 64
NH = 16
HD = 128
SH = 32
SP = 64
SN = 128
SG = 8
CONV_CH = 4096
IN_COLS = 14400
EPS = 1e-6
NEG = -30000.0


class Cfg:
    def __init__(self, rows=128, L=256, depth=4, debug=False, phases=None):
        self.rows = rows
        self.S = rows * GW
        self.L = L
        self.T = self.S + L
        self.NT = self.T // 128
        self.NC = L // 128
        self.NL = self.S // 128
        self.depth = depth
        self.debug = debug
        self.phases = phases
        self.skew = True


class Buf:
    __slots__ = ("name", "w", "r", "dw", "dr", "sem")

    def __init__(self, name):
        self.name = name
        self.w = {}
        self.r = {}
        self.dw = 0
        self.dr = 0
        self.sem = None


class Op:
    __slots__ = ("eng", "fn", "deps", "marked", "count", "dma", "barrier", "totals")

    def __init__(self, eng, fn):
        self.eng = eng
        self.fn = fn
        self.deps = []
        self.marked = False
        self.count = 0
        self.dma = None
        self.barrier = False
        self.totals = None


ENGS = ("pe", "act", "dve", "pool", "sp")
NDMASEM = 40


class Prog:
    def __init__(self):
        self.ops = []
        self.dma_tot = [0] * NDMASEM
        self.phase_bufs = []
        self.next_dma_sem = 0
        self.last_op = {e: None for e in ENGS}

    def buf(self, name):
        b = Buf(name)
        self.phase_bufs.append(b)
        return b

    def bufs(self, name, n):
        return [self.buf(f"{name}{i}") for i in range(n)]

    def _sem_for(self, b):
        if b.sem is None:
            assert self.next_dma_sem < NDMASEM, "out of DMA semaphores in this phase"
            b.sem = self.next_dma_sem
            self.next_dma_sem += 1
        return b.sem

    def add(self, eng, fn, reads=(), writes=(), dma_buf=None):
        op = Op(eng, fn)
        is_dma = dma_buf is not None
        deps = op.deps
        for b in reads:
            for e, o in b.w.items():
                if e == eng and eng == "pe" and not is_dma:
                    continue
                deps.append(o)
            if b.dw:
                deps.append((b.sem, b.dw))
        for b in writes:
            for e, o in b.w.items():
                if e != eng or is_dma:
                    deps.append(o)
            for e, o in b.r.items():
                if e != eng or is_dma:
                    deps.append(o)
            if b.dw:
                deps.append((b.sem, b.dw))
            if b.dr:
                deps.append((b.sem, b.dr))
        for d in deps:
            if isinstance(d, Op):
                d.marked = True
        if is_dma:
            s = self._sem_for(dma_buf)
            self.dma_tot[s] += 16
            op.dma = (s, self.dma_tot[s])
            for b in reads:
                b.dr = self.dma_tot[s] if b is dma_buf else b.dr
                if b is not dma_buf:
                    raise AssertionError("dma touching tracked buf other than dma_buf")
            for b in writes:
                if b is not dma_buf:
                    raise AssertionError("dma touching tracked buf other than dma_buf")
                b.dw = self.dma_tot[s]
                b.w = {}
                b.r = {}
        else:
            for b in reads:
                b.r[eng] = op
            for b in writes:
                b.w = {eng: op}
                b.r = {}
                b.dw = 0
                b.dr = 0
        self.ops.append(op)
        if not is_dma:
            self.last_op[eng] = op
        return op

    def barrier(self):
        for e in ENGS:
            lo = self.last_op[e]
            if lo is not None:
                lo.marked = True
        op = Op(None, None)
        op.barrier = True
        self.ops.append(op)
        for b in self.phase_bufs:
            b.w = {}
            b.r = {}
            b.dw = 0
            b.dr = 0
            b.sem = None
        self.phase_bufs = []
        self.next_dma_sem = 0
        for e in ENGS:
            self.last_op[e] = None

    def emit(self, nc, csem, dsem):
        cnt = {e: 0 for e in ENGS}
        dtot = [0] * NDMASEM
        for op in self.ops:
            if op.barrier:
                op.totals = (dict(cnt), list(dtot))
                continue
            if op.dma is not None:
                dtot[op.dma[0]] = op.dma[1]
            elif op.marked:
                cnt[op.eng] += 1
                op.count = cnt[op.eng]
        ops = self.ops
        engobj = {}

        def run(eng, e):
            waited = {}

            def wait(key, sem, val):
                if waited.get(key, 0) < val:
                    e.wait_ge(sem, val)
                    waited[key] = val

            for op in ops:
                if op.barrier:
                    c, dt = op.totals
                    for en in ENGS:
                        if c[en]:
                            wait(en, csem[en], c[en])
                    for i, v in enumerate(dt):
                        if v:
                            wait(i, dsem[i], v)
                    continue
                if op.eng != eng:
                    continue
                for d in op.deps:
                    if isinstance(d, Op):
                        wait(d.eng, csem[d.eng], d.count)
                    else:
                        wait(d[0], dsem[d[0]], d[1])
                ins = op.fn(e)
                if op.dma is not None:
                    ins.then_inc(dsem[op.dma[0]], 16)
                elif op.marked:
                    ins.then_inc(csem[eng], 1)

        with nc.Block() as block:
            @block.tensor
            def _(e):
                run("pe", e)

            @block.scalar
            def _(e):
                run("act", e)

            @block.vector
            def _(e):
                run("dve", e)

            @block.gpsimd
            def _(e):
                run("pool", e)

            @block.sync
            def _(e):
                run("sp", e)


class Builder:
    def __init__(self, cfg):
        self.cfg = cfg
        self.nc = bass.Bass("TRN2", target_bir_lowering=False)
        self.p = Prog()
        self.dbg_names = []
        self._uid = 0

    def sbt(self, name, shape, dt):
        self._uid += 1
        return self.nc.sbuf_tensor(f"{name}_{self._uid}", shape, dt)

    def pst(self, name, shape, dt):
        self._uid += 1
        return self.nc.psum_tensor(f"{name}_{self._uid}", shape, dt)

    def dram_in(self, name, shape, dt=F32):
        return self.nc.dram_tensor(name, list(shape), dt, kind="ExternalInput").ap()

    def dram_scratch(self, name, shape, dt, dbg=True):
        if self.cfg.debug and dbg:
            self.dbg_names.append(name)
            return self.nc.dram_tensor(name, list(shape), dt, kind="ExternalOutput").ap()
        return self.nc.dram_tensor(name, list(shape), dt).ap()

    def load(self, out_ap, in_ap, buf, eng="sp", **kw):
        self.p.add(eng, lambda e: e.dma_start(out=out_ap, in_=in_ap, **kw), writes=[buf], dma_buf=buf)

    def store(self, out_ap, in_ap, buf, eng="pool", **kw):
        self.p.add(eng, lambda e: e.dma_start(out=out_ap, in_=in_ap, **kw), reads=[buf], dma_buf=buf)

    def d2d(self, out_ap, in_ap, b, eng="pool"):
        self.p.add(eng, lambda e: e.dma_start(out=out_ap, in_=in_ap), writes=[b], dma_buf=b)

    def mm(self, out_ap, lhsT, rhs, start, stop, reads, wbuf):
        self.p.add("pe", lambda e: e.matmul(out_ap, lhsT=lhsT, rhs=rhs, start=start, stop=stop),
                   reads=reads, writes=[wbuf])

    def tr(self, out_ap, in_ap, ident, reads, wbuf):
        self.p.add("pe", lambda e: e.transpose(out_ap, in_ap, ident), reads=reads, writes=[wbuf])

    def act(self, out, in_, func, reads, writes, eng="act", **kw):
        self.p.add(eng, lambda e: e.activation(out=out, in_=in_, func=func, **kw), reads=reads, writes=writes)

    def tt(self, eng, out, in0, in1, op, reads, writes):
        self.p.add(eng, lambda e: e.tensor_tensor(out=out, in0=in0, in1=in1, op=op), reads=reads, writes=writes)

    def ts(self, eng, out, in0, s1, s2, op0, op1, reads, writes):
        if s2 is None:
            self.p.add(eng, lambda e: e.tensor_scalar(out=out, in0=in0, scalar1=s1, scalar2=None, op0=op0),
                       reads=reads, writes=writes)
        else:
            self.p.add(eng, lambda e: e.tensor_scalar(out=out, in0=in0, scalar1=s1, scalar2=s2, op0=op0, op1=op1),
                       reads=reads, writes=writes)

    def stt(self, eng, out, in0, scalar, in1, op0, op1, reads, writes):
        self.p.add(eng, lambda e: e.scalar_tensor_tensor(out=out, in0=in0, scalar=scalar, in1=in1, op0=op0, op1=op1),
                   reads=reads, writes=writes)

    def copy(self, eng, out, in_, reads, writes):
        if eng == "act":
            self.p.add(eng, lambda e: e.copy(out=out, in_=in_), reads=reads, writes=writes)
        else:
            self.p.add(eng, lambda e: e.tensor_copy(out=out, in_=in_), reads=reads, writes=writes)

    def memset(self, eng, ap, val, writes):
        self.p.add(eng, lambda e: e.memset(ap, val), writes=writes)

    def build(self):
        cfg = self.cfg
        nc = self.nc
        T, NT, depth = cfg.T, cfg.NT, cfg.depth
        I = {}
        I["xin"] = self.dram_in("xin", [T, D])
        I["cvec"] = self.dram_in("cvec", [128, 16, 2])
        I["ada_w"] = self.dram_in("ada_w", [depth, D, 3 * D])
        I["ada_bf"] = self.dram_in("ada_bf", [depth, 128, 48])
        I["ada_bg"] = self.dram_in("ada_bg", [depth, 1, D])
        I["norm_wf"] = self.dram_in("norm_wf", [depth, 128, 16])
        I["w_in"] = self.dram_in("w_in", [depth, D, IN_COLS])
        I["q_norm"] = self.dram_in("q_norm", [depth, 1, HD])
        I["k_norm"] = self.dram_in("k_norm", [depth, 1, HD])
        I["rope"] = self.dram_in("rope", [128, cfg.NL, 2, 64])
        I["bias"] = self.dram_in("bias", [depth, NH, 128, 21, 128])
        I["conv_w"] = self.dram_in("conv_w", [depth, 128, 32, 5])
        I["conv_b"] = self.dram_in("conv_b", [depth, 128, 32])
        I["dt_bias"] = self.dram_in("dt_bias", [depth, 1, 64])
        I["a_log"] = self.dram_in("a_log", [depth, 1, 64])
        I["d_skip"] = self.dram_in("d_skip", [depth, 1, 32])
        I["ssm_norm"] = self.dram_in("ssm_norm", [depth, 1, D])
        I["w_out"] = self.dram_in("w_out", [depth, 2 * D, D])
        I["consts"] = self.dram_in("consts", [128, 6, 128])
        I["negm"] = self.dram_in("negm", [128, 2, 512])
        self.I = I
        xout = nc.dram_tensor("xout", [T, D], F32, kind="ExternalOutput").ap()
        S = {}
        S["X"] = [I["xin"]] + [self.dram_scratch(f"X{l}", [T, D], F32, dbg=False) for l in range(1, depth)] + [xout]
        S["wbf_in"] = [self.dram_scratch(f"wbf_in{i}", [D, IN_COLS], BF16, dbg=False) for i in range(2)]
        S["wbf_out"] = [self.dram_scratch(f"wbf_out{i}", [2 * D, D], BF16, dbg=False) for i in range(2)]
        S["gate"] = self.dram_scratch("gate_s", [2, D], F32)
        for n in ("q_s", "k_s", "v_s", "sg_s", "sz_s"):
            S[n] = self.dram_scratch(n, [T, D], BF16)
        S["xbc_pre"] = self.dram_scratch("xbc_pre", [CONV_CH, T], BF16)
        S["xbc_post"] = self.dram_scratch("xbc_post", [CONV_CH, T], BF16)
        S["dt_s"] = self.dram_scratch("dt_s", [T, 64], F32)
        S["yf_s"] = self.dram_scratch("yf_s", [T, D], F32)
        S["mix_s"] = self.dram_scratch("mix_s", [T, 2 * D], BF16)
        self.S = S

        with ExitStack() as es:
            csem = {e: es.enter_context(nc.semaphore(f"c_{e}")) for e in ENGS}
            dsem = [es.enter_context(nc.semaphore(f"d_{i}")) for i in range(NDMASEM)]
            P = {}
            P["consts"] = es.enter_context(self.sbt("p_consts", [128, 6, 128], F32))
            P["ident"] = es.enter_context(self.sbt("p_ident", [128, 128], BF16))
            P["modA"] = es.enter_context(self.sbt("p_modA", [128, 2, 16], F32))
            P["modB"] = es.enter_context(self.sbt("p_modB", [128, 2, 16], F32))
            P["gate"] = es.enter_context(self.sbt("p_gate", [128, 2, D], F32))
            self.P = P
            self.es_top = es
            self.phase_init()
            for l in range(depth):
                self.layer(l)
            self.p.barrier()
            self.p.emit(nc, csem, dsem)
        return nc

    def want(self, name):
        return self.cfg.phases is None or name in self.cfg.phases

    def phase_init(self):
        p, P, I = self.p, self.P, self.I
        b = p.buf("consts")
        self.load(P["consts"][:], I["consts"][:, :, :], b)
        self.copy("dve", P["ident"][:], P["consts"][:, 0, :], [b], [b])
        p.barrier()

    def layer(self, l):
        if self.want("W") and l == 0:
            self.issue_W(0)
            self.p.barrier()
        if self.want("M"):
            self.phase_M(l)
        if self.want("A"):
            self.phase_A(l)
        if self.want("B"):
            self.phase_B(l)
        if self.want("C0"):
            self.phase_C0(l)
        if self.want("C1"):
            self.phase_C(l, 0)
        if self.want("C2"):
            self.phase_C(l, 1)
        if self.want("D"):
            self.phase_D(l)

    def issue_W(self, l):
        I, S = self.I, self.S
        if l >= self.cfg.depth:
            return
        nchunk = 16
        rows = D // nchunk
        bd = [self.p.buf("d2da"), self.p.buf("d2db")]
        for i in range(nchunk):
            self.d2d(S["wbf_in"][l % 2][i * rows:(i + 1) * rows, :], I["w_in"][l, i * rows:(i + 1) * rows, :], bd[i % 2])
        rows = 2 * D // nchunk
        for i in range(nchunk):
            self.d2d(S["wbf_out"][l % 2][i * rows:(i + 1) * rows, :], I["w_out"][l, i * rows:(i + 1) * rows, :], bd[i % 2])

    def phase_M(self, l):
        nc, p, P, I, S = self.nc, self.p, self.P, self.I, self.S
        with ExitStack() as es:
            sb = lambda name, shape, dt=F32: es.enter_context(self.sbt(name, shape, dt))
            cv = sb("m_cv", [128, 16, 2]); sc = sb("m_sc", [128, 16, 2])
            wt = [sb(f"m_w{i}", [128, 16, 512]) for i in range(2)]
            bfm = sb("m_bf", [128, 48]); nw = sb("m_nw", [128, 16]); bg = sb("m_bg", [2, D])
            mf = sb("m_mf", [128, 32, 2]); grow = sb("m_grow", [2, D])
            psf = es.enter_context(self.pst("m_psf", [128, 32, 2], F32))
            psg = [es.enter_context(self.pst(f"m_psg{i}", [2, 512], F32)) for i in range(2)]
            b_cv, b_sc, b_bf, b_nw, b_bg, b_mf, b_grow, b_psf = (p.buf(n) for n in
                                                                ("cv", "sc", "bf", "nw", "bg", "mf", "grow", "psf"))
            b_wt = p.bufs("wt", 2); b_psg = p.bufs("psg", 2)
            b_modA, b_modB, b_gate = p.buf("modA"), p.buf("modB"), p.buf("gate")
            self.load(cv[:], I["cvec"][:, :, :], b_cv)
            self.load(bfm[:], I["ada_bf"][l, :, :], b_bf)
            self.load(nw[:], I["norm_wf"][l, :, :], b_nw)
            self.load(bg[:], I["ada_bg"][l, :, :].to_broadcast([2, D]), b_bg)
            self.act(sc[:], cv[:], AF.Silu, [b_cv], [b_sc])
            wv = I["ada_w"][l].rearrange("(kc p) n -> p kc n", p=128)
            for blk in range(12):
                w = wt[blk % 2]; bw = b_wt[blk % 2]
                self.load(w[:], wv[:, :, blk * 512:(blk + 1) * 512], bw)
                if blk < 8:
                    for ft in range(4):
                        j = blk * 4 + ft
                        for kc in range(16):
                            self.mm(psf[:, j, :], w[:, kc, ft * 128:(ft + 1) * 128], sc[:, kc, :],
                                    kc == 0, kc == 15, [bw, b_sc], b_psf)
                else:
                    g = blk - 8
                    ps = psg[g % 2]; bps = b_psg[g % 2]
                    for kc in range(16):
                        self.mm(ps[:, :], sc[:, kc, :], w[:, kc, :], kc == 0, kc == 15, [bw, b_sc], bps)
                    self.tt("dve", grow[:, g * 512:(g + 1) * 512], ps[:, :], bg[:, g * 512:(g + 1) * 512],
                            ALU.add, [bps, b_bg], [b_grow])
            self.tt("dve", mf[:], psf[:], bfm[:, 0:32].unsqueeze(2).to_broadcast([128, 32, 2]), ALU.add,
                    [b_psf, b_bf], [b_mf])
            for s, ci in ((0, 1), (1, 0)):
                self.stt("dve", P["modA"][:, s, :], mf[:, 16:32, ci], 1.0, nw[:], ALU.add, ALU.mult,
                         [b_mf, b_nw], [b_modA])
                self.copy("dve", P["modB"][:, s, :], mf[:, 0:16, ci], [b_mf], [b_modB])
            self.store(S["gate"][:, :], grow[:], b_grow, eng="sp")
            p.barrier()
            self.load(P["gate"][:, 0, :], S["gate"][1:2, :].to_broadcast([128, D]), b_gate)
            self.load(P["gate"][:, 1, :], S["gate"][0:1, :].to_broadcast([128, D]), b_gate)
            p.barrier()

    def phase_A(self, l):
        nc, p, P, I, S, cfg = self.nc, self.p, self.P, self.I, self.S, self.cfg
        X = S["X"][l]
        TB = 1024
        NTB = TB // 128
        sbs = [(0, cfg.L)]
        t0 = cfg.L
        while t0 < cfg.T:
            sbs.append((t0, min(TB, cfg.T - t0)))
            t0 += TB
        with ExitStack() as es:
            sb = lambda name, shape, dt=F32: es.enter_context(self.sbt(name, shape, dt))
            hT2 = [sb(f"a_hT{i}", [128, 16, TB], BF16) for i in range(2)]
            xt = [sb(f"a_xt{i}", [128, D]) for i in range(2)]
            xn = [sb(f"a_xn{i}", [128, D], BF16) for i in range(2)]
            junk = sb("a_junk", [128, D], BF16)
            ss = sb("a_ss", [128, 2]); rstd = sb("a_rstd", [128, 2])
            wt = [sb(f"a_wt{i}", [128, 16, 512], BF16) for i in range(3)]
            qkn = sb("a_qkn", [128, 2, 4, 128])
            rope = sb("a_rope", [128, NTB, 2, 64])
            ropew = sb("a_ropew", [128, NTB, 2, 4, 64])
            dtb = sb("a_dtb", [128, 64])
            sq = [sb(f"a_sq{i}", [128, 512]) for i in range(2)]
            hs = [sb(f"a_hs{i}", [128, 8]) for i in range(2)]
            tsb = [sb(f"a_tsb{i}", [128, 512]) for i in range(2)]
            m1 = [sb(f"a_m1{i}", [128, 4, 64]) for i in range(2)]
            m2 = [sb(f"a_m2{i}", [128, 4, 64]) for i in range(2)]
            m3 = [sb(f"a_m3{i}", [128, 4, 64]) for i in range(2)]
            m4 = [sb(f"a_m4{i}", [128, 4, 64]) for i in range(2)]
            ob = [sb(f"a_ob{i}", [128, 512], BF16) for i in range(3)]
            of = [sb(f"a_of{i}", [128, 512], BF16) for i in range(2)]
            dtt = [sb(f"a_dtt{i}", [128, 4, 64]) for i in range(2)]
            tp = [es.enter_context(self.pst(f"a_tp{i}", [128, 1024], BF16)) for i in range(2)]
            mmps = [es.enter_context(self.pst(f"a_mm{i}", [128, 512], F32)) for i in range(4)]
            b_hT2 = [p.bufs(f"hT{i}_", NTB) for i in range(2)]
            b_xt = p.bufs("xt", 2); b_xn = p.bufs("xn", 2); b_junk = p.buf("junk")
            b_ss = p.bufs("ss", 2); b_rstd = p.bufs("rstd", 2)
            b_wt = p.bufs("wt", 3); b_qkn = p.buf("qkn"); b_rope = p.buf("rope"); b_ropew = p.buf("ropew"); b_dtb = p.buf("dtb")
            b_sq = p.bufs("sq", 2); b_hs = p.bufs("hs", 2); b_tsb = p.bufs("tsb", 2)
            b_m1 = p.bufs("m1", 2); b_m2 = p.bufs("m2", 2); b_m3 = p.bufs("m3", 2); b_m4 = p.bufs("m4", 2)
            b_ob = p.bufs("ob", 3); b_of = p.bufs("of", 2)
            b_dtt = p.bufs("dtt", 2)
            b_tp = p.bufs("tp", 2); b_mm = p.bufs("mm", 4)
            ident = P["ident"]
            if l == 0:
                pass

            self.load(qkn[:, 0, :, :], I["q_norm"][l, :, :].unsqueeze(1).to_broadcast([128, 4, 128]), b_qkn)
            self.load(qkn[:, 1, :, :], I["k_norm"][l, :, :].unsqueeze(1).to_broadcast([128, 4, 128]), b_qkn)
            self.load(dtb[:], I["dt_bias"][l, :, :].to_broadcast([128, 64]), b_dtb)
            self.ts("dve", qkn[:, 0, :, :], qkn[:, 0, :, :], HD ** -0.5, None, ALU.mult, None, [b_qkn], [b_qkn])

            wsrc = S["wbf_in"][l % 2].rearrange("(kc p) n -> p kc n", p=128)
            xbcT = S["xbc_pre"]
            if self.want("W"):
                self.issue_W(l + 1)
            st_ = {"w": 0, "mm": 0, "epi": 0}

            def s1prep(sbi):
                tb0, tbn = sbs[sbi]
                ntile = tbn // 128
                if tb0 >= cfg.L:
                    lt0 = (tb0 - cfg.L) // 128
                    self.load(rope[:, 0:ntile, :, :], I["rope"][:, lt0:lt0 + ntile, :, :], b_rope)
                    for i in range(ntile):
                        for qi in range(2):
                            w2 = qkn[:, qi, 0, :].rearrange("p (i two) -> p i two", two=2)
                            we, wo = w2[:, :, 0], w2[:, :, 1]
                            cos_, sin_ = rope[:, i, 0, :], rope[:, i, 1, :]
                            for a_, (tb_, w_) in enumerate(((cos_, we), (sin_, wo), (sin_, we), (cos_, wo))):
                                self.tt("pool", ropew[:, i, qi, a_, :], tb_, w_, ALU.mult, [b_rope, b_qkn], [b_ropew])

            def s1a(sbi, i):
                tb0, tbn = sbs[sbi]
                s = i % 2
                r0 = tb0 + i * 128
                self.load(xt[s][:], X[r0:r0 + 128, :], b_xt[s])
                self.act(junk[:], xt[s][:], AF.Square, [b_xt[s]], [b_junk, b_ss[s]], accum_out=ss[:, s:s + 1])
                self.act(rstd[:, s:s + 1], ss[:, s:s + 1], AF.Sqrt, [b_ss[s]], [b_rstd[s]], scale=1.0 / D, bias=EPS)
                self.p.add("dve", (lambda e, s=s: e.reciprocal(out=rstd[:, s:s + 1], in_=rstd[:, s:s + 1])),
                           reads=[b_rstd[s]], writes=[b_rstd[s]])
                self.act(xn[s][:], xt[s][:], AF.Copy, [b_xt[s], b_rstd[s]], [b_xn[s]], scale=rstd[:, s:s + 1])

            def s1b(sbi, i):
                tb0, tbn = sbs[sbi]
                ms = 0 if tb0 < cfg.L else 1
                hT = hT2[sbi % 2]; b_hT = b_hT2[sbi % 2]
                s = i % 2
                for half in range(2):
                    for k8 in range(8):
                        kc = half * 8 + k8
                        self.tr(tp[half][:, k8 * 128:(k8 + 1) * 128], xn[s][:, kc * 128:(kc + 1) * 128], ident[:],
                                [b_xn[s]], b_tp[half])
                    for k8 in range(8):
                        kc = half * 8 + k8
                        src = tp[half][:, k8 * 128:(k8 + 1) * 128]
                        dst = hT[:, kc, i * 128:(i + 1) * 128]
                        if k8 % 2 == 0:
                            self.act(dst, src, AF.Identity, [b_tp[half]], [b_hT[i]],
                                     scale=P["modA"][:, ms, kc:kc + 1], bias=P["modB"][:, ms, kc:kc + 1])
                        else:
                            self.ts("dve", dst, src, P["modA"][:, ms, kc:kc + 1], P["modB"][:, ms, kc:kc + 1],
                                    ALU.mult, ALU.add, [b_tp[half]], [b_hT[i]])

            def step1(sbi):
                s1prep(sbi)
                for i in range(sbs[sbi][1] // 128):
                    s1a(sbi, i)
                    s1b(sbi, i)

            def do_block(sbi, fam, c0, ncol, j):
                tb0, tbn = sbs[sbi]
                ntile = tbn // 128
                is_ctx = tb0 < cfg.L
                hT = hT2[sbi % 2]; b_hT = b_hT2[sbi % 2]
                ws = st_["w"] % 3; st_["w"] += 1
                w = wt[ws]; bw = b_wt[ws]
                if not (getattr(cfg, "a_nowt", False) and sbi >= 2):
                    self.load(w[:, :, 0:ncol], wsrc[:, :, c0:c0 + ncol], bw)
                if fam == "xbc":
                    for cs_ in range(4):
                        ch0 = j * 512 + cs_ * 128
                        for tg in range(0, tbn, 512):
                            tn = min(512, tbn - tg)
                            m = st_["mm"] % 4; st_["mm"] += 1
                            tiles = [b_hT[(tg + q) // 128] for q in range(0, tn, 128)]
                            for kc in range(16):
                                self.mm(mmps[m][:, 0:tn], w[:, kc, cs_ * 128:(cs_ + 1) * 128], hT[:, kc, tg:tg + tn],
                                        kc == 0, kc == 15, [bw] + tiles, b_mm[m])
                            o = of[st_["epi"] % 2]; bo = b_of[st_["epi"] % 2]; st_["epi"] += 1
                            self.copy("act", o[:, 0:tn], mmps[m][:, 0:tn], [b_mm[m]], [bo])
                            self.store(xbcT[ch0:ch0 + 128, tb0 + tg:tb0 + tg + tn], o[:, 0:tn], bo)
                    return
                for i in range(ntile):
                    r0 = tb0 + i * 128
                    m = st_["mm"] % 4; st_["mm"] += 1
                    ps = mmps[m]; bps = b_mm[m]
                    for kc in range(16):
                        self.mm(ps[:, 0:ncol], hT[:, kc, i * 128:(i + 1) * 128], w[:, kc, 0:ncol],
                                kc == 0, kc == 15, [bw, b_hT[i]], bps)
                    e2 = st_["epi"] % 2; st_["epi"] += 1
                    epi = st_["epi"]
                    if fam in ("q", "k"):
                        qi = 0 if fam == "q" else 1
                        t = tsb[e2]
                        self.copy("act", t[:], ps[:, :], [bps], [b_tsb[e2]])
                        self.act(sq[e2][:], ps[:, :], AF.Square, [bps], [b_sq[e2]])
                        self.p.add("dve", (lambda e, e2=e2: e.reduce_sum(out=hs[e2][:, 0:4], in_=sq[e2][:].rearrange("p (h d) -> p h d", h=4), axis=AX.X)),
                                   reads=[b_sq[e2]], writes=[b_hs[e2]])
                        self.act(hs[e2][:, 4:8], hs[e2][:, 0:4], AF.Sqrt, [b_hs[e2]], [b_hs[e2]], scale=1.0 / HD, bias=EPS)
                        self.p.add("dve", (lambda e, e2=e2: e.reciprocal(out=hs[e2][:, 4:8], in_=hs[e2][:, 4:8])),
                                   reads=[b_hs[e2]], writes=[b_hs[e2]])
                        o = ob[epi % 3]; bo = b_ob[epi % 3]
                        if is_ctx:
                            t3 = t[:].rearrange("p (h d) -> p h d", h=4)
                            self.tt("dve", t3, t3, hs[e2][:, 4:8].unsqueeze(2).to_broadcast([128, 4, 128]), ALU.mult,
                                    [b_tsb[e2], b_hs[e2]], [b_tsb[e2]])
                            self.tt("dve", o[:].rearrange("p (h d) -> p h d", h=4), t3, qkn[:, qi, :, :], ALU.mult,
                                    [b_tsb[e2], b_qkn], [bo])
                        else:
                            t4 = t[:].rearrange("p (h i two) -> p h i two", h=4, two=2)
                            o4 = o[:].rearrange("p (h i two) -> p h i two", h=4, two=2)
                            te, to = t4[:, :, :, 0], t4[:, :, :, 1]
                            A = [ropew[:, i, qi, a_, :].unsqueeze(1).to_broadcast([128, 4, 64]) for a_ in range(4)]
                            rb = hs[e2][:, 4:8].unsqueeze(2).to_broadcast([128, 4, 64])
                            rd = [b_tsb[e2], b_ropew]
                            self.tt("dve", m1[e2][:], te, A[0], ALU.mult, rd, [b_m1[e2]])
                            self.tt("pool", m2[e2][:], to, A[1], ALU.mult, rd, [b_m2[e2]])
                            self.tt("dve", m3[e2][:], te, A[2], ALU.mult, rd, [b_m3[e2]])
                            self.tt("pool", m4[e2][:], to, A[3], ALU.mult, rd, [b_m4[e2]])
                            self.tt("dve", m1[e2][:], m1[e2][:], m2[e2][:], ALU.subtract, [b_m1[e2], b_m2[e2]], [b_m1[e2]])
                            self.tt("pool", m3[e2][:], m3[e2][:], m4[e2][:], ALU.add, [b_m3[e2], b_m4[e2]], [b_m3[e2]])
                            self.tt("dve", o4[:, :, :, 0], m1[e2][:], rb, ALU.mult, [b_m1[e2], b_hs[e2]], [bo])
                            self.tt("pool", o4[:, :, :, 1], m3[e2][:], rb, ALU.mult, [b_m3[e2], b_hs[e2]], [bo])
                        dst = S["q_s" if fam == "q" else "k_s"]
                        self.store(dst[r0:r0 + 128, j * 512:(j + 1) * 512], o[:], bo)
                    elif fam == "v":
                        o = ob[epi % 3]; bo = b_ob[epi % 3]
                        self.copy("act", o[:], ps[:, :], [bps], [bo])
                        self.store(S["v_s"][r0:r0 + 128, j * 512:(j + 1) * 512], o[:], bo)
                    elif fam in ("g", "z"):
                        o = ob[epi % 3]; bo = b_ob[epi % 3]
                        self.act(o[:], ps[:, :], AF.Silu, [bps], [bo])
                        dst = S["sg_s" if fam == "g" else "sz_s"]
                        self.store(dst[r0:r0 + 128, j * 512:(j + 1) * 512], o[:], bo)
                    else:
                        d4 = dtt[e2]; bd = b_dtt[e2]
                        self.tt("dve", d4[:, 0, :], ps[:, 0:64], dtb[:], ALU.add, [bps, b_dtb], [bd])
                        self.act(d4[:, 1, :], d4[:, 0, :], AF.Abs, [bd], [bd])
                        self.act(d4[:, 2, :], d4[:, 1, :], AF.Exp, [bd], [bd], scale=-1.0)
                        self.act(d4[:, 2, :], d4[:, 2, :], AF.Ln, [bd], [bd], bias=1.0)
                        self.ts("dve", d4[:, 1, :], d4[:, 0, :], 0.0, None, ALU.max, None, [bd], [bd])
                        self.tt("dve", d4[:, 3, :], d4[:, 1, :], d4[:, 2, :], ALU.add, [bd], [bd])
                        self.store(S["dt_s"][r0:r0 + 128, :], d4[:, 3, :], bd)

            blocks = []
            for f, fam in enumerate(("q", "k", "v", "g", "z")):
                for j in range(4):
                    blocks.append((fam, f * D + j * 512, 512, j))
            for j in range(8):
                blocks.append(("xbc", 5 * D + j * 512, 512, j))
            blocks.append(("dt", 5 * D + CONV_CH, 64, 0))
            if getattr(cfg, "a_fams", None):
                blocks = [b_ for b_ in blocks if b_[0] in cfg.a_fams]
            nb = len(blocks)
            hoist = max(0, nb - 11)
            step1(0)
            for sbi in range(len(sbs)):
                nxt = sbi + 1 if sbi + 1 < len(sbs) else None
                nt_n = (sbs[nxt][1] // 128) if nxt is not None else 0
                for bi, (fam, c0, ncol, j) in enumerate(blocks):
                    if nxt is not None:
                        if nb >= 12:
                            k = bi - hoist
                            if k == 0:
                                s1prep(nxt)
                            if 0 <= k < nt_n:
                                s1a(nxt, k)
                            if 1 <= k <= nt_n:
                                s1b(nxt, k - 1)
                        elif bi == 0:
                            step1(nxt)
                    do_block(sbi, fam, c0, ncol, j)
            p.barrier()

    def phase_B(self, l):
        nc, p, P, I, S, cfg = self.nc, self.p, self.P, self.I, self.S, self.cfg
        NT, NC = cfg.NT, cfg.NC
        slots, protos, keylists, cls_of = bias_slots(cfg)
        CH = 16
        nch = (NT + CH - 1) // CH
        ng8 = (NT + 7) // 8
        with ExitStack() as es:
            sb = lambda name, shape, dt=F32: es.enter_context(self.sbt(name, shape, dt))
            ktok = sb("b_ktok", [128, NT, 128], BF16); qtok = sb("b_qtok", [128, NT, 128], BF16)
            vtok = sb("b_vtok", [128, NT, 132], BF16); sg = sb("b_sg", [128, NT, 128], BF16)
            KT = sb("b_KT", [128, NT * 128], BF16); QT = sb("b_QT", [128, NT * 128], BF16)
            biasf = sb("b_biasf", [128, 21, 128], F32); biasb = sb("b_biasb", [128, 21, 128], BF16)
            PT = [sb(f"b_PT{i}", [128, 1024], BF16) for i in range(2)]
            rinv = sb("b_rinv", [128, 2])
            obuf = [sb(f"b_ob{i}", [128, 8, 128], BF16) for i in range(2)]
            tp = [es.enter_context(self.pst(f"b_tp{i}", [128, 1024], BF16)) for i in range(2)]
            st = [es.enter_context(self.pst(f"b_st{i}", [128, 1024], F32)) for i in range(2)]
            ops = [es.enter_context(self.pst(f"b_o{i}", [128, 512], F32)) for i in range(2)]
            b_ktok = p.bufs("ktok", nch); b_qtok = p.bufs("qtok", nch); b_vtok = p.bufs("vtok", nch)
            b_sg = p.bufs("sg", nch)
            b_KT = p.bufs("KT", ng8); b_QT = p.bufs("QT", ng8)
            b_biasf = p.buf("biasf"); b_biasb = p.buf("biasb")
            b_PT = p.bufs("PT", 2); b_rinv = p.bufs("rinv", 2); b_ob = p.bufs("ob", 2)
            b_tp = p.bufs("tp", 2); b_st = p.bufs("st", 2); b_o = p.bufs("o", 2)
            ident = P["ident"]
            for c in range(nch):
                t0, t1 = c * CH, min(NT, (c + 1) * CH)
                self.memset("pool", vtok[:, t0:t1, 128:129], 1.0, [b_vtok[c]])
            kv = S["k_s"].rearrange("(t p) c -> p t c", p=128)
            qv = S["q_s"].rearrange("(t p) c -> p t c", p=128)
            vv = S["v_s"].rearrange("(t p) c -> p t c", p=128)
            gv = S["sg_s"].rearrange("(t p) c -> p t c", p=128)
            mv = S["mix_s"].rearrange("(t p) c -> p t c", p=128)
            tpc = 0
            for h in range(NH):
                hs_ = slice(h * 128, (h + 1) * 128)
                self.load(biasf[:], I["bias"][l, h, :, :, :], b_biasf)
                for c in range(nch):
                    t0, t1 = c * CH, min(NT, (c + 1) * CH)
                    self.load(ktok[:, t0:t1, :], kv[:, t0:t1, hs_], b_ktok[c])
                    self.load(qtok[:, t0:t1, :], qv[:, t0:t1, hs_], b_qtok[c])
                for c in range(nch):
                    t0, t1 = c * CH, min(NT, (c + 1) * CH)
                    self.load(vtok[:, t0:t1, 0:128], vv[:, t0:t1, hs_], b_vtok[c])
                    self.load(sg[:, t0:t1, :], gv[:, t0:t1, hs_], b_sg[c])
                self.copy("pool", biasb[:], biasf[:], [b_biasf], [b_biasb])
                for g8 in range(ng8):
                    t0, t1 = g8 * 8, min(NT, g8 * 8 + 8)
                    for (src, bsrc, dstT, bdst) in ((ktok, b_ktok, KT, b_KT), (qtok, b_qtok, QT, b_QT)):
                        tps = tp[tpc % 2]; btp = b_tp[tpc % 2]; tpc += 1
                        for t in range(t0, t1):
                            self.tr(tps[:, (t - t0) * 128:(t - t0 + 1) * 128], src[:, t, :], ident[:], [bsrc[t // CH]], btp)
                        n = (t1 - t0) * 128
                        self.copy("act" if tpc % 2 else "dve", dstT[:, t0 * 128:t0 * 128 + n], tps[:, 0:n], [btp], [bdst[g8]])
                def keys_of(qi):
                    keys = []
                    if qi >= NC:
                        i = qi - NC
                        for j in keylists[i]:
                            keys.append((NC + j, slots[(cls_of[i], j - i)]))
                    for j in range(NC):
                        keys.append((j, None))
                    return keys

                def qk(qi):
                    x2 = qi % 2
                    keys = keys_of(qi)
                    n = len(keys)
                    for s_, (kt, bs) in enumerate(keys):
                        osl = st[x2][:, s_ * 128:(s_ + 1) * 128]
                        self.mm(osl, KT[:, kt * 128:(kt + 1) * 128], QT[:, qi * 128:(qi + 1) * 128], True, bs is None,
                                [b_KT[kt // 8], b_QT[qi // 8]], b_st[x2])
                        if bs is not None:
                            self.mm(osl, biasb[:, bs, :], ident[:], False, True, [b_biasb], b_st[x2])
                    self.act(PT[x2][:, 0:n * 128], st[x2][:, 0:n * 128], AF.Exp, [b_st[x2]], [b_PT[x2]])

                def pv(qi):
                    x2 = qi % 2
                    keys = keys_of(qi)
                    n = len(keys)
                    for s_, (kt, bs) in enumerate(keys):
                        self.mm(ops[x2][:, 0:129], PT[x2][:, s_ * 128:(s_ + 1) * 128], vtok[:, kt, 0:129], s_ == 0, s_ == n - 1,
                                [b_PT[x2], b_vtok[kt // CH]], b_o[x2])
                    self.p.add("dve", (lambda e, x2=x2: e.reciprocal(out=rinv[:, x2:x2 + 1], in_=ops[x2][:, 128:129])),
                               reads=[b_o[x2]], writes=[b_rinv[x2]])
                    og = (qi // 8) % 2
                    self.stt("dve", obuf[og][:, qi % 8, :], ops[x2][:, 0:128], rinv[:, x2:x2 + 1], sg[:, qi, :], ALU.mult, ALU.mult,
                             [b_o[x2], b_rinv[x2], b_sg[qi // CH]], [b_ob[og]])
                    if qi % 8 == 7 or qi == NT - 1:
                        t0 = (qi // 8) * 8
                        self.store(mv[:, t0:qi + 1, hs_], obuf[og][:, 0:qi + 1 - t0, :], b_ob[og])

                qk(0)
                for qi in range(NT):
                    if qi + 1 < NT:
                        qk(qi + 1)
                    pv(qi)
            p.barrier()

    def phase_C0(self, l):
        nc, p, P, I, S, cfg = self.nc, self.p, self.P, self.I, self.S, self.cfg
        T, L, S_ = cfg.T, cfg.L, cfg.S
        TP = T + 8
        with ExitStack() as es:
            sb = lambda name, shape, dt=F32: es.enter_context(self.sbt(name, shape, dt))
            xr = [sb(f"c_xr{i}", [128, TP], BF16) for i in range(2)]
            ob = [sb(f"c_ob{i}", [128, T], BF16) for i in range(2)]
            dg = [sb(f"c_dg{i}", [128, 5, 128], BF16) for i in range(2)]
            cw = sb("c_cw", [128, 32, 5]); cb = sb("c_cb", [128, 32])
            ps = [es.enter_context(self.pst(f"c_ps{i}", [128, 512], F32)) for i in range(4)]
            b_xr = p.bufs("xr", 2); b_ob = p.bufs("ob", 2); b_dg = p.bufs("dg", 2); b_cw = p.buf("cw"); b_ps = p.bufs("ps", 4)
            self.load(cw[:], I["conv_w"][l, :, :, :], b_cw)
            self.load(cb[:], I["conv_b"][l, :, :], b_cw)
            for i in range(2):
                self.memset("pool", xr[i][:], 0.0, [b_xr[i]])
            blocks = [(0, L, 2)]
            t0 = L
            while t0 < T:
                n = min(512, T - t0)
                blocks.append((t0, n, t0 + 6))
                t0 += n
            pc = 0
            for ct in range(32):
                s_ = ct % 2
                x = xr[s_]
                self.load(x[:, 2:2 + L], S["xbc_pre"][ct * 128:(ct + 1) * 128, 0:L], b_xr[s_])
                self.load(x[:, L + 6:L + 6 + S_], S["xbc_pre"][ct * 128:(ct + 1) * 128, L:T], b_xr[s_])
                for j in range(5):
                    self.ts("dve", dg[s_][:, j, :], P["consts"][:, 0, :], cw[:, ct, j:j + 1], None, ALU.mult, None,
                            [b_cw], [b_dg[s_]])
                for (tk0, n, pc0) in blocks:
                    q = pc % 4; pc += 1
                    for j in range(5):
                        self.mm(ps[q][:, 0:n], dg[s_][:, j, :], x[:, pc0 + j - 2:pc0 + j - 2 + n], j == 0, j == 4,
                                [b_dg[s_], b_xr[s_]], b_ps[q])
                    self.act(ob[s_][:, tk0:tk0 + n], ps[q][:, 0:n], AF.Silu, [b_ps[q], b_cw], [b_ob[s_]], bias=cb[:, ct:ct + 1])
                self.store(S["xbc_post"][ct * 128:(ct + 1) * 128, :], ob[s_][:], b_ob[s_], eng="sp")
            p.barrier()

    def phase_C(self, l, dr):
        nc, p, P, I, S, cfg = self.nc, self.p, self.P, self.I, self.S, self.cfg
        NT, NC = cfg.NT, cfg.NC
        order = list(range(NT)) if dr == 0 else (list(range(NC - 1, -1, -1)) + list(range(NT - 1, NC - 1, -1)))
        tcol = 127 if dr == 0 else 0
        with ExitStack() as es:
            sb = lambda name, shape, dt=F32: es.enter_context(self.sbt(name, shape, dt))
            alog = sb("s_alog", [128, 64]); abc = sb("s_abc", [128, 64])
            dsk = sb("s_dsk", [128, 32]); snw = sb("s_snw", [128, D])
            negf = sb("s_negf", [128, 512]); negb = sb("s_negb", [128, 512], BF16)
            BCt = [sb(f"s_BCt{i}", [128, 16, 128], BF16) for i in range(2)]
            xsT = [sb(f"s_xsT{i}", [128, 16, 128], BF16) for i in range(2)]
            dtt = [sb(f"s_dt{i}", [128, 32]) for i in range(2)]
            xst = [sb(f"s_xst{i}", [128, 32, 64], BF16) for i in range(2)]
            Btok = [sb(f"s_Btok{i}", [128, 8, 128], BF16) for i in range(2)]
            da = [sb(f"s_da{i}", [128, 32]) for i in range(2)]
            ecr = [sb(f"s_ecr{i}", [128, 64]) for i in range(2)]
            ncs = [sb(f"s_ncs{i}", [128, 32]) for i in range(2)]
            xd = [sb(f"s_xd{i}", [128, 32, 64], BF16) for i in range(2)]
            xdd = [sb(f"s_xdd{i}", [128, 32, 64], BF16) for i in range(2)]
            CBs = [sb(f"s_CBs{i}", [128, 128]) for i in range(2)]
            dec4 = [sb(f"s_dec{i}", [128, 4, 128]) for i in range(2)]
            etot = [sb(f"s_etot{i}", [128, 4]) for i in range(2)]
            MT4 = [sb(f"s_MT{i}", [128, 4, 128], BF16) for i in range(2)]
            yo = [sb(f"s_yo{i}", [128, 4, 64]) for i in range(2)]
            yacc = [sb(f"s_yacc{i}", [128, 32, 64]) for i in range(2)]
            stT = sb("s_stT", [128, 32, 64]); stb = sb("s_stb", [128, 32, 64], BF16)
            tmp = sb("s_tmp", [128, 32, 64])
            if dr == 1:
                yft = [sb(f"s_yf{i}", [128, D]) for i in range(2)]
                szt = [sb(f"s_sz{i}", [128, D], BF16) for i in range(2)]
                junk = sb("s_junk", [128, D], BF16)
                ssq = sb("s_ssq", [128, 2])
                outb = [sb(f"s_out{i}", [128, D], BF16) for i in range(2)]
                b_yf = p.bufs("yf", 2); b_sz = p.bufs("sz", 2); b_junk = p.buf("junk"); b_ssq = p.bufs("ssq", 2)
                b_out = p.bufs("out", 2)
            tp = [es.enter_context(self.pst("s_tp", [128, 1024], BF16))]
            csb_l = [es.enter_context(self.pst(f"s_csb{i}", [128, 512], F32)) for i in range(2)]
            cy_l = [es.enter_context(self.pst(f"s_cy{i}", [128, 512], F32)) for i in range(2)]
            ydc_l = [es.enter_context(self.pst(f"s_ydc{i}", [128, 512], F32)) for i in range(2)]
            cr_t = es.enter_context(self.pst("s_cr", [128, 512], F32))
            cr_ps = cr_t[:, 0:64]
            b_c = p.buf("consts_s")
            b_BCt = p.bufs("BCt", 2); b_xsT = p.bufs("xsT", 2); b_dt = p.bufs("dt", 2); b_xst = p.bufs("xst", 2)
            b_Btok = p.bufs("Btok", 2); b_da = p.bufs("da", 2); b_ecr = p.bufs("ecr", 2); b_ncs = p.bufs("ncs", 2)
            b_xd = p.bufs("xd", 2); b_xdd = p.bufs("xdd", 2); b_CBs = p.bufs("CBs", 2); b_dec = p.bufs("dec", 2)
            b_etot = p.bufs("etot", 2); b_MT = p.bufs("MT", 2); b_yo = p.bufs("yo", 2); b_yacc = p.bufs("yacc", 2)
            b_stT = p.bufs("stT", 8); b_stb = p.bufs("stb", 8); b_tmp = p.buf("tmp")
            b_tp = p.bufs("tp", 1); b_cr = p.buf("cr"); b_csb = p.bufs("csb", 2); b_cb = p.bufs("cy", 2)
            b_yoff = b_cb; b_yd = p.bufs("ydc", 2); b_cst = b_yd
            ident = P["ident"]
            C = P["consts"]
            Uc = C[:, 1, :] if dr == 0 else C[:, 3, :]
            SLc = C[:, 2, :] if dr == 0 else C[:, 4, :]
            self.load(alog[:], I["a_log"][l, :, :].to_broadcast([128, 64]), b_c)
            self.load(dsk[:], I["d_skip"][l, :, :].to_broadcast([128, 32]), b_c)
            self.load(snw[:], I["ssm_norm"][l, :, :].to_broadcast([128, D]), b_c)
            self.load(negf[:], I["negm"][:, dr, :], b_c)
            self.act(abc[:], alog[:], AF.Exp, [b_c], [b_c])
            self.ts("dve", abc[:], abc[:], -1.0, None, ALU.mult, None, [b_c], [b_c])
            self.copy("dve", negb[:], negf[:], [b_c], [b_c])
            self.memset("dve", stT[:], 0.0, b_stT)
            self.memset("pool", stb[:], 0.0, b_stb)
            def prologue(it):
                c = order[it]
                x2 = it % 2
                cs_ = slice(c * 128, (c + 1) * 128)
                self.load(BCt[x2][:], S["xbc_post"][2048:4096, cs_].rearrange("(g n) t -> n g t", n=128), b_BCt[x2])
                self.load(xsT[x2][:], S["xbc_post"][0:2048, cs_].rearrange("(g n) t -> n g t", n=128), b_xsT[x2])
                self.load(dtt[x2][:], S["dt_s"][cs_, dr * 32:(dr + 1) * 32], b_dt[x2])
                if dr == 1:
                    self.load(yft[x2][:], S["yf_s"][cs_, :], b_yf[x2])
                    self.load(szt[x2][:], S["sz_s"][cs_, :], b_sz[x2])
                tps = tp[0]; btp = b_tp[0]
                for half in range(2):
                    for k8 in range(8):
                        self.tr(tps[:, k8 * 128:(k8 + 1) * 128], xsT[x2][:, half * 8 + k8, :], ident[:], [b_xsT[x2]], btp)
                    self.copy("act", xst[x2][:, half * 16:(half + 1) * 16, :].rearrange("p h d -> p (h d)"), tps[:, :],
                              [btp], [b_xst[x2]])
                for g in range(8):
                    self.tr(tps[:, g * 128:(g + 1) * 128], BCt[x2][:, g, :], ident[:], [b_BCt[x2]], btp)
                self.copy("act", Btok[x2][:].rearrange("p g n -> p (g n)"), tps[:, :], [btp], [b_Btok[x2]])
                self.tt("dve", da[x2][:], dtt[x2][:], abc[:, dr * 32:(dr + 1) * 32], ALU.mult, [b_dt[x2], b_c], [b_da[x2]])
                self.mm(cr_ps[:, 0:32], Uc, da[x2][:], True, True, [b_da[x2]], b_cr)
                self.mm(cr_ps[:, 32:64], SLc, da[x2][:], True, True, [b_da[x2]], b_cr)
                self.act(ecr[x2][:], cr_ps, AF.Exp, [b_cr], [b_ecr[x2]])
                self.ts("dve", ncs[x2][:], cr_ps[:, 0:32], -1.0, None, ALU.mult, None, [b_cr], [b_ncs[x2]])
                self.tt("dve", xd[x2][:], xst[x2][:], dtt[x2][:].unsqueeze(2).to_broadcast([128, 32, 64]), ALU.mult,
                        [b_xst[x2], b_dt[x2]], [b_xd[x2]])
                self.tt("pool", xdd[x2][:], xd[x2][:], ecr[x2][:, 32:64].unsqueeze(2).to_broadcast([128, 32, 64]), ALU.mult,
                        [b_xd[x2], b_ecr[x2]], [b_xdd[x2]])

            def stage1(it, g):
                x2 = it % 2
                g2 = g % 2
                csb_ps = csb_l[g2]; bcsb = b_csb[g2]
                cb_ps = cy_l[g2][:, 0:128]; yoff_ps = cy_l[g2][:, 128:384]
                self.mm(csb_ps[:, :], ident[:], negb[:], True, False, [b_c], bcsb)
                for h4 in range(4):
                    h = g * 4 + h4
                    self.mm(csb_ps[:, h4 * 128:(h4 + 1) * 128], da[x2][:, h:h + 1].to_broadcast([128, 128]), Uc,
                            False, h4 == 3, [b_da[x2]], bcsb)
                self.mm(cb_ps, BCt[x2][:, g, :], BCt[x2][:, 8 + g, :], True, True, [b_BCt[x2]], b_cb[g2])
                self.mm(yoff_ps, BCt[x2][:, 8 + g, :], stb[:, g * 4:(g + 1) * 4, :].rearrange("p h d -> p (h d)"),
                        True, True, [b_BCt[x2], b_stb[g]], b_yoff[g2])
                self.copy("act", CBs[g2][:], cb_ps, [b_cb[g2]], [b_CBs[g2]])
                for h4 in range(4):
                    h = g * 4 + h4
                    self.act(dec4[g2][:, h4, :], csb_ps[:, h4 * 128:(h4 + 1) * 128], AF.Exp, [bcsb, b_ncs[x2]], [b_dec[g2]],
                             bias=ncs[x2][:, h:h + 1])
                self.act(etot[g2][:], csb_ps[:, :].rearrange("p (h t) -> p h t", h=4)[:, :, tcol], AF.Exp, [bcsb], [b_etot[g2]])
                self.tt("dve", MT4[g2][:], dec4[g2][:], CBs[g2][:].unsqueeze(1).to_broadcast([128, 4, 128]), ALU.mult,
                        [b_dec[g2], b_CBs[g2]], [b_MT[g2]])
                self.tt("dve", yo[g2][:], yoff_ps.rearrange("p (h d) -> p h d", h=4),
                        ecr[x2][:, g * 4:(g + 1) * 4].unsqueeze(2).to_broadcast([128, 4, 64]), ALU.mult,
                        [b_yoff[g2], b_ecr[x2]], [b_yo[g2]])

            def stage2(it, g):
                x2 = it % 2
                g2 = g % 2
                yd_ps = ydc_l[g2][:, 0:256]; cst_ps = ydc_l[g2][:, 256:512]
                for h4 in range(4):
                    h = g * 4 + h4
                    self.mm(yd_ps[:, h4 * 64:(h4 + 1) * 64], MT4[g2][:, h4, :], xd[x2][:, h, :], True, True,
                            [b_MT[g2], b_xd[x2]], b_yd[g2])
                self.mm(cst_ps, Btok[x2][:, g, :], xdd[x2][:, g * 4:(g + 1) * 4, :].rearrange("p h d -> p (h d)"),
                        True, True, [b_Btok[x2], b_xdd[x2]], b_cst[g2])
                self.tt("dve", yacc[x2][:, g * 4:(g + 1) * 4, :], yd_ps.rearrange("p (h d) -> p h d", h=4), yo[g2][:],
                        ALU.add, [b_yd[g2], b_yo[g2]], [b_yacc[x2]])
                for h4 in range(4):
                    h = g * 4 + h4
                    self.stt("dve", stT[:, h, :], stT[:, h, :], etot[g2][:, h4:h4 + 1], cst_ps[:, h4 * 64:(h4 + 1) * 64],
                             ALU.mult, ALU.add, [b_stT[g], b_etot[g2], b_cst[g2]], [b_stT[g]])
                self.copy("pool", stb[:, g * 4:(g + 1) * 4, :], stT[:, g * 4:(g + 1) * 4, :], [b_stT[g]], [b_stb[g]])

            def epilogue(it):
                c = order[it]
                x2 = it % 2
                cs_ = slice(c * 128, (c + 1) * 128)
                if dr == 0:
                    self.tt("pool", tmp[:], xst[x2][:], dsk[:].unsqueeze(2).to_broadcast([128, 32, 64]), ALU.mult,
                            [b_xst[x2], b_c], [b_tmp])
                    self.tt("pool", yacc[x2][:], yacc[x2][:], tmp[:], ALU.add, [b_yacc[x2], b_tmp], [b_yacc[x2]])
                    self.store(S["yf_s"][cs_, :], yacc[x2][:].rearrange("p h d -> p (h d)"), b_yacc[x2], eng="sp")
                else:
                    ya = yacc[x2][:].rearrange("p h d -> p (h d)")
                    self.tt("pool", ya, ya, yft[x2][:], ALU.add, [b_yacc[x2], b_yf[x2]], [b_yacc[x2]])
                    self.tt("pool", ya, ya, szt[x2][:], ALU.mult, [b_yacc[x2], b_sz[x2]], [b_yacc[x2]])
                    self.act(junk[:], ya, AF.Square, [b_yacc[x2]], [b_junk, b_ssq[x2]], accum_out=ssq[:, x2:x2 + 1])
                    self.act(ssq[:, x2:x2 + 1], ssq[:, x2:x2 + 1], AF.Sqrt, [b_ssq[x2]], [b_ssq[x2]], scale=1.0 / D, bias=EPS)
                    self.p.add("dve", (lambda e, x2=x2: e.reciprocal(out=ssq[:, x2:x2 + 1], in_=ssq[:, x2:x2 + 1])),
                               reads=[b_ssq[x2]], writes=[b_ssq[x2]])
                    self.stt("dve", outb[x2][:], ya, ssq[:, x2:x2 + 1], snw[:], ALU.mult, ALU.mult,
                             [b_yacc[x2], b_ssq[x2], b_c], [b_out[x2]])
                    self.store(S["mix_s"][cs_, D:2 * D], outb[x2][:], b_out[x2], eng="sp")

            nit = len(order)
            if not self.cfg.skew:
                for it in range(nit):
                    prologue(it)
                    for g in range(8):
                        stage1(it, g)
                        stage2(it, g)
                    epilogue(it)
            else:
                prologue(0)
                stage1(0, 0)
                for it in range(nit):
                    for g in range(8):
                        if g == 3 and it + 1 < nit:
                            prologue(it + 1)
                        if g + 1 < 8:
                            stage1(it, g + 1)
                        elif it + 1 < nit:
                            stage1(it + 1, 0)
                        stage2(it, g)
                    epilogue(it)
            p.barrier()

    def phase_D(self, l):
        nc, p, P, I, S, cfg = self.nc, self.p, self.P, self.I, self.S, self.cfg
        X, Xn = S["X"][l], S["X"][l + 1]
        TB = 1024
        NTB = TB // 128
        sbs = [(0, cfg.L)]
        t0 = cfg.L
        while t0 < cfg.T:
            sbs.append((t0, min(TB, cfg.T - t0)))
            t0 += TB
        with ExitStack() as es:
            sb = lambda name, shape, dt=F32: es.enter_context(self.sbt(name, shape, dt))
            mixT = sb("d_mixT", [128, 32, TB], BF16)
            mt = [sb(f"d_mt{i}", [128, 2 * D], BF16) for i in range(3)]
            wo = [sb(f"d_wo{i}", [128, 32, 512], BF16) for i in range(2)]
            xs_ = [sb(f"d_xs{i}", [128, 512]) for i in range(3)]
            tm = [sb(f"d_tm{i}", [128, 512]) for i in range(2)]
            tp = [es.enter_context(self.pst(f"d_tp{i}", [128, 1024], BF16)) for i in range(2)]
            mmps = [es.enter_context(self.pst(f"d_mm{i}", [128, 512], F32)) for i in range(4)]
            b_mixT = p.bufs("mixT", NTB)
            b_mt = p.bufs("mt", 3); b_wo = p.bufs("wo", 2)
            b_xs = p.bufs("xs", 3); b_tm = p.bufs("tm", 2); b_tp = p.bufs("tp", 2); b_mm = p.bufs("mm", 4)
            ident = P["ident"]
            wsrc = S["wbf_out"][l % 2].rearrange("(kc p) n -> p kc n", p=128)
            st_ = {"tp": 0, "w": 0, "m": 0, "x": 0, "mt": 0}
            for (tb0, tbn) in sbs:
                ntile = tbn // 128
                ms = 0 if tb0 < cfg.L else 1
                wq = []
                for cb in range(2):
                    w = wo[st_["w"] % 2]; bw = b_wo[st_["w"] % 2]; st_["w"] += 1
                    self.load(w[:], wsrc[:, :, cb * 512:(cb + 1) * 512], bw)
                    wq.append((w, bw))
                for i in range(ntile):
                    r0 = tb0 + i * 128
                    m = mt[st_["mt"] % 3]; bm = b_mt[st_["mt"] % 3]; st_["mt"] += 1
                    self.load(m[:], S["mix_s"][r0:r0 + 128, :], bm)
                    for b4 in range(4):
                        tps = tp[st_["tp"] % 2]; btp = b_tp[st_["tp"] % 2]; st_["tp"] += 1
                        for k8 in range(8):
                            kc = b4 * 8 + k8
                            self.tr(tps[:, k8 * 128:(k8 + 1) * 128], m[:, kc * 128:(kc + 1) * 128], ident[:], [bm], btp)
                        self.copy("act" if b4 % 2 else "dve", mixT[:, b4 * 8:(b4 + 1) * 8, i * 128:(i + 1) * 128],
                                  tps[:, :].rearrange("p (k t) -> p k t", k=8), [btp], [b_mixT[i]])
                for cb in range(4):
                    if cb < 2:
                        w, bw = wq[cb]
                    else:
                        w = wo[st_["w"] % 2]; bw = b_wo[st_["w"] % 2]; st_["w"] += 1
                        self.load(w[:], wsrc[:, :, cb * 512:(cb + 1) * 512], bw)
                    for i in range(ntile):
                        r0 = tb0 + i * 128
                        ps = mmps[st_["m"] % 4]; bps = b_mm[st_["m"] % 4]; st_["m"] += 1
                        for kc in range(32):
                            self.mm(ps[:, :], mixT[:, kc, i * 128:(i + 1) * 128], w[:, kc, :], kc == 0, kc == 31,
                                    [b_mixT[i], bw], bps)
                        xc = st_["x"]; st_["x"] += 1
                        x = xs_[xc % 3]; bx = b_xs[xc % 3]
                        t = tm[xc % 2]; bt = b_tm[xc % 2]
                        self.load(x[:], X[r0:r0 + 128, cb * 512:(cb + 1) * 512], bx)
                        self.tt("dve", t[:], ps[:, :], P["gate"][:, ms, cb * 512:(cb + 1) * 512], ALU.mult, [bps], [bt])
                        self.tt("pool", x[:], x[:], t[:], ALU.add, [bx, bt], [bx])
                        self.store(Xn[r0:r0 + 128, cb * 512:(cb + 1) * 512], x[:], bx)
            p.barrier()


def rope_tables(cfg):
    t = np.arange(cfg.S)
    rows = (t // GW).astype(np.float32)
    cols = (t % GW).astype(np.float32)
    n_pairs = HD // 4
    freqs = (10000.0 ** (-np.arange(n_pairs, dtype=np.float32) / n_pairs)).astype(np.float32)
    ang = np.concatenate([rows[:, None] * freqs, cols[:, None] * freqs], axis=-1).astype(np.float32)
    cos, sin = np.cos(ang).astype(np.float32), np.sin(ang).astype(np.float32)
    tab = np.stack([cos, sin], axis=1)
    return np.ascontiguousarray(tab.reshape(cfg.NL, 128, 2, 64).transpose(1, 0, 2, 3))


def const_tables():
    k = np.arange(128)[:, None]
    t = np.arange(128)[None, :]
    c = np.zeros((128, 6, 128), np.float32)
    c[:, 0] = np.eye(128)
    c[:, 1] = (k <= t)
    c[:, 2] = (k > t)
    c[:, 3] = (k >= t)
    c[:, 4] = (k < t)
    negm = np.zeros((128, 2, 512), np.float32)
    negm[:, 0] = np.tile(NEG * (k > t), (1, 4))
    negm[:, 1] = np.tile(NEG * (k < t), (1, 4))
    return c, negm


def bias_classes(cfg):
    NL = cfg.NL
    out = []
    for i in range(NL):
        r0a = min(max(2 * i - 4, 0), cfg.rows - 8)
        r0b = min(max(2 * i + 1 - 4, 0), cfg.rows - 8)
        lo, hi = r0a, r0b + 7
        js = list(range(lo // 2, hi // 2 + 1))
        out.append(js)
    return out


def bias_slots(cfg):
    NL = cfg.NL
    keylists = bias_classes(cfg)
    slots = {}
    protos = []
    cls_of = []
    for i in range(NL):
        if 2 <= i <= NL - 3:
            cls = 0
        elif i < 2:
            cls = 1 + i
        else:
            cls = 3 + (i - (NL - 2))
        cls_of.append(cls)
        for j in keylists[i]:
            key = (cls, j - i)
            if key not in slots:
                slots[key] = len(protos)
                protos.append((i, j))
    assert len(protos) <= 21, len(protos)
    return slots, protos, keylists, cls_of


def build_bias_tables(cfg, rpb):
    depth = rpb.shape[0]
    slots, protos, keylists, cls_of = bias_slots(cfg)
    qa = np.arange(128)
    qr_off, qc = qa // 64, qa % 64
    tabs = np.full((depth, NH, 128, 21, 128), NEG, np.float32)
    rows = cfg.rows
    for sl, (i, j) in enumerate(protos):
        qr = 2 * i + qr_off
        kr = 2 * j + qr_off
        kc = qc
        r0 = np.clip(qr - 4, 0, rows - 8)
        row_ok = (kr[None, :] >= r0[:, None]) & (kr[None, :] < r0[:, None] + 8)
        c0 = np.clip(qc - 8, 0, GW - 16)
        col_ok = (kc[None, :] >= c0[:, None]) & (kc[None, :] < c0[:, None] + 16)
        ok = row_ok & col_ok
        ridx = np.clip(kr[None, :] - qr[:, None] + 7, 0, 14)
        cidx = np.clip(kc[None, :] - qc[:, None] + 15, 0, 30)
        g = rpb[:, :, ridx, cidx]
        tabs[:, :, :, sl, :] = np.where(ok[None, None], g, NEG)
    return tabs


def host_inputs(cfg, b, inp):
    depth = cfg.depth
    f = lambda a: np.ascontiguousarray(np.asarray(a, dtype=np.float32))
    m = {}
    m["xin"] = f(np.concatenate([inp["ctx"][b], inp["x"][b]], axis=0))
    cv = np.stack([inp["c"][b], inp["c_ctx"]], axis=-1)
    m["cvec"] = f(cv.reshape(16, 128, 2).transpose(1, 0, 2))
    m["ada_w"] = f(inp["ada_w"][:depth])
    ab = inp["ada_b"][:depth]
    m["ada_bf"] = f(ab.reshape(depth, 48, 128).transpose(0, 2, 1))
    m["ada_bg"] = f(ab[:, None, 2 * D:3 * D])
    m["norm_wf"] = f(inp["norm_w"][:depth].reshape(depth, 16, 128).transpose(0, 2, 1))
    m["w_in"] = f(inp["w_in"][:depth])
    m["q_norm"] = f(inp["q_norm"][:depth, None, :])
    m["k_norm"] = f(inp["k_norm"][:depth, None, :])
    m["rope"] = rope_tables(cfg)
    tabs = build_bias_tables(cfg, np.asarray(inp["rpb"][:depth], np.float32))
    m["bias"] = tabs
    cw = inp["conv_w"][:depth]
    m["conv_w"] = f(cw.reshape(depth, 5, 32, 128).transpose(0, 3, 2, 1))
    m["conv_b"] = f(inp["conv_b"][:depth].reshape(depth, 32, 128).transpose(0, 2, 1))
    m["dt_bias"] = f(inp["dt_bias"][:depth].reshape(depth, 1, 64))
    m["a_log"] = f(inp["a_log"][:depth].reshape(depth, 1, 64))
    m["d_skip"] = f(inp["d_skip"][:depth].reshape(depth, 1, 32))
    m["ssm_norm"] = f(inp["ssm_norm"][:depth].reshape(depth, 1, D))
    m["w_out"] = f(inp["w_out"][:depth])
    c, negm = const_tables()
    m["consts"] = c
    m["negm"] = negm
    return m


_NC_CACHE = {}


def kernel(**inputs):
    cfg = Cfg()
    inp = {k: np.asarray(v) for k, v in inputs.items()}
    if "nc" not in _NC_CACHE:
        _NC_CACHE["nc"] = Builder(cfg).build()
    nc = _NC_CACHE["nc"]
    B = inp["x"].shape[0]
    maps = [host_inputs(cfg, i % B, inp) for i in range(B)]
    in_maps = [maps[i % B] for i in range(8)]
    res = run_bass_kernel_spmd(nc, in_maps, core_ids=list(range(8)))
    out = np.stack([res.results[i]["xout"][cfg.L:] for i in range(B)], axis=0)
    return out.astype(np.float32)
```
